# Optimizing a Trainium2 kernel written in Bass

```python
import jax, jax.numpy as jnp
from jax import lax
import numpy as np

D_MODEL = 1024
BATCH = 16
SEQ = 2048
DEPTH = 4

HEAD_DIM = 64
ATTN_W = D_MODEL // 2
N_Q_HEADS = ATTN_W // HEAD_DIM
N_KV_HEADS = N_Q_HEADS // 4
GQA_GROUP = N_Q_HEADS // N_KV_HEADS
KV_W = N_KV_HEADS * HEAD_DIM
CONV_W = D_MODEL // 4
CONV_HEADS = CONV_W // HEAD_DIM
CONV_K = 3
FFT_W = D_MODEL // 4
FFT_GROUP_DIM = 64
FFT_GROUPS = FFT_W // FFT_GROUP_DIM
MIX_W = ATTN_W + CONV_W + FFT_W
IN_SIZES = (ATTN_W, KV_W, KV_W, ATTN_W, CONV_W, CONV_W, CONV_W, CONV_W, FFT_W, FFT_W)
IN_W = sum(IN_SIZES)
GRID_W = 64
ROPE_THETA = 10000.0
ROPE_HALF = HEAD_DIM // 2
Q_BLOCK = 128
RMS_EPS = 1e-6
LN_EPS = 1e-5
DEEPNORM_ALPHA = (2 * DEPTH) ** 0.25
DEEPNORM_BETA = (8 * DEPTH) ** -0.25
ADA_SCALE = 0.1

kernel_name = "hybrid_parallel_heads_deepnorm_encoder"


def _layer_norm(x, eps):
    xf = x.astype(jnp.float32)
    mu = jnp.mean(xf, axis=-1, keepdims=True)
    xc = xf - mu
    var = jnp.mean(xc * xc, axis=-1, keepdims=True)
    return (xc * lax.rsqrt(var + eps)).astype(x.dtype)


def _rms_norm(x, gain):
    xf = x.astype(jnp.float32)
    y = xf * lax.rsqrt(jnp.mean(xf * xf, axis=-1, keepdims=True) + RMS_EPS)
    return (y * gain.astype(jnp.float32)).astype(x.dtype)


def _axial_angles(seq_len):
    rows = seq_len // GRID_W
    row_pos = jnp.repeat(jnp.arange(rows, dtype=jnp.float32), GRID_W)
    col_pos = jnp.tile(jnp.arange(GRID_W, dtype=jnp.float32), rows)
    inv_freq = 1.0 / (ROPE_THETA ** (jnp.arange(0, ROPE_HALF, 2, dtype=jnp.float32) / ROPE_HALF))
    return row_pos[:, None] * inv_freq, col_pos[:, None] * inv_freq


def _rope_half(x, ang):
    cos = jnp.cos(ang)[:, None, :].astype(x.dtype)
    sin = jnp.sin(ang)[:, None, :].astype(x.dtype)
    x1, x2 = jnp.split(x, 2, axis=-1)
    return jnp.concatenate([x1 * cos - x2 * sin, x2 * cos + x1 * sin], axis=-1)


def _axial_rope(x, row_ang, col_ang):
    xr, xc = jnp.split(x, 2, axis=-1)
    return jnp.concatenate([_rope_half(xr, row_ang), _rope_half(xc, col_ang)], axis=-1)


def _attention(q, k, v):
    b, s = q.shape[0], q.shape[1]
    nb = s // Q_BLOCK
    scale = HEAD_DIM ** -0.5
    qg = q.reshape(b, nb, Q_BLOCK, N_KV_HEADS, GQA_GROUP, HEAD_DIM).transpose(1, 0, 2, 3, 4, 5)

    def block(qb):
        scores = jnp.einsum('bqkgd,bskd->bkgqs', qb, k,
                            preferred_element_type=jnp.float32) * scale
        p = jax.nn.softmax(scores, axis=-1).astype(v.dtype)
        return jnp.einsum('bkgqs,bskd->bqkgd', p, v)

    o = lax.map(block, qg)
    return o.transpose(1, 0, 2, 3, 4, 5).reshape(b, s, ATTN_W)


def _short_conv(u, w):
    s = u.shape[1]
    pad = CONV_K // 2
    up = jnp.pad(u, ((0, 0), (pad, pad), (0, 0)))
    return sum(up[:, j:j + s, :] * w[j] for j in range(CONV_K))


def _fourier(u, f_mix):
    b, s = u.shape[0], u.shape[1]
    ug = u.reshape(b, s, FFT_GROUPS, FFT_GROUP_DIM).astype(jnp.float32)
    f = jnp.fft.fft2(ug, axes=(1, 3), norm='ortho').real.astype(u.dtype)
    f = jnp.einsum('bsgc,gcd->bsgd', f, f_mix)
    return f.reshape(b, s, FFT_W)


def _split_indices():
    return np.cumsum(np.array(IN_SIZES))[:-1].tolist()


def setup_inputs(seed: int = 0) -> dict:
    key = jax.random.key(seed)
    ks = jax.random.split(key, 12)
    f32 = jnp.float32
    x = jax.random.normal(ks[0], (BATCH, SEQ, D_MODEL), f32)
    c = jax.random.normal(ks[1], (BATCH, D_MODEL), f32)
    w_ada = jax.random.normal(ks[2], (DEPTH, D_MODEL, 3 * D_MODEL), f32) * (D_MODEL ** -0.5) * ADA_SCALE
    b_ada = jax.random.normal(ks[3], (DEPTH, 3 * D_MODEL), f32) * 0.01
    w_in = jax.random.normal(ks[4], (DEPTH, D_MODEL, IN_W), f32) * (D_MODEL ** -0.5)
    q_gain = 1.0 + 0.05 * jax.random.normal(ks[5], (DEPTH, HEAD_DIM), f32)
    k_gain = 1.0 + 0.05 * jax.random.normal(ks[6], (DEPTH, HEAD_DIM), f32)
    conv_w = jax.random.normal(ks[7], (DEPTH, CONV_K, CONV_W), f32) * (CONV_K ** -0.5)
    f_mix = jax.random.normal(ks[8], (DEPTH, FFT_GROUPS, FFT_GROUP_DIM, FFT_GROUP_DIM), f32) * (FFT_GROUP_DIM ** -0.5)
    w_out = jax.random.normal(ks[9], (DEPTH, MIX_W, D_MODEL), f32) * (MIX_W ** -0.5) * DEEPNORM_BETA
    ln_g = 1.0 + 0.05 * jax.random.normal(ks[10], (DEPTH, D_MODEL), f32)
    ln_b = 0.01 * jax.random.normal(ks[11], (DEPTH, D_MODEL), f32)
    return {"x": x, "c": c, "w_ada": w_ada, "b_ada": b_ada, "w_in": w_in,
            "q_gain": q_gain, "k_gain": k_gain, "conv_w": conv_w, "f_mix": f_mix,
            "w_out": w_out, "ln_g": ln_g, "ln_b": ln_b}


def reference(x, c, w_ada, b_ada, w_in, q_gain, k_gain, conv_w, f_mix, w_out, ln_g, ln_b):
    b, s, _ = x.shape
    row_ang, col_ang = _axial_angles(s)
    split_idx = _split_indices()
    for l in range(DEPTH):
        mod = c @ w_ada[l] + b_ada[l]
        shift, scale, gate = jnp.split(mod, 3, axis=-1)
        h = _layer_norm(x, LN_EPS) * (1.0 + scale[:, None, :]) + shift[:, None, :]

        proj = h @ w_in[l]
        q, k, v, g_a, cu, cb, cc, g_c, fu, g_f = jnp.split(proj, split_idx, axis=-1)

        q = _axial_rope(_rms_norm(q.reshape(b, s, N_Q_HEADS, HEAD_DIM), q_gain[l]), row_ang, col_ang)
        k = _axial_rope(_rms_norm(k.reshape(b, s, N_KV_HEADS, HEAD_DIM), k_gain[l]), row_ang, col_ang)
        v = v.reshape(b, s, N_KV_HEADS, HEAD_DIM)
        y_attn = _attention(q, k, v) * jax.nn.silu(g_a)

        y_conv = cb * _short_conv(cc * cu, conv_w[l]) * jax.nn.silu(g_c)

        y_four = _fourier(fu, f_mix[l]) * jax.nn.silu(g_f)

        y = jnp.concatenate([y_attn, y_conv, y_four], axis=-1) @ w_out[l]

        x = _layer_norm(DEEPNORM_ALPHA * x + (1.0 + gate[:, None, :]) * y, LN_EPS) * ln_g[l] + ln_b[l]
    return x
```

```python
import contextlib
import numpy as np
import ml_dtypes
import concourse.bass as bass
import concourse.mybir as mybir
from concourse.bass_utils import run_bass_kernel_spmd

F32 = mybir.dt.float32
BF16 = mybir.dt.bfloat16
ALU = mybir.AluOpType
ACTF = mybir.ActivationFunctionType
AX = mybir.AxisListType

S = 2048
D = 1024
NT = 16
KC = 8
DEPTH = 4
IN_W = 2816
ALPHA = (2 * DEPTH) ** 0.25
RMS_EPS = 1e-6
LN_EPS = 1e-5
NCORES = 8
BPC = 2


class Buf:
    __slots__ = ("name", "w", "r")

    def __init__(self, name):
        self.name = name
        self.w = None
        self.r = []


class Sem:
    def __init__(self, h, name):
        self.h = h
        self.name = name
        self.count = 0


class Eng:
    def __init__(self, name, h, sem):
        self.name = name
        self.h = h
        self.sem = sem
        self.seen = {}


class FW:
    def __init__(self, nc, stack):
        self.nc = nc
        self.stack = stack
        self.pe = self._eng("pe", nc.tensor)
        self.act = self._eng("act", nc.scalar)
        self.dve = self._eng("dve", nc.vector)
        self.pool = self._eng("pool", nc.gpsimd)
        self.sp = self._eng("sp", nc.sync)
        self.ninst = 0
        self.nwait = 0

    def sem(self, name):
        h = self.stack.enter_context(self.nc.semaphore(name))
        return Sem(h, name)

    def _eng(self, name, h):
        return Eng(name, h, self.sem("s_" + name))

    def sbuf(self, name, shape, dt):
        return self.stack.enter_context(self.nc.sbuf_tensor(name, shape, dt))

    def psum(self, name, shape, dt):
        return self.stack.enter_context(self.nc.psum_tensor(name, shape, dt))

    def _waits(self, eng, reads, writes, is_dma=False, my_sem=None):
        need = {}

        def add(tok):
            s, v, who = tok
            if (not is_dma) and who == eng.name and eng.name == "pe":
                return
            if v > need.get(s, (None, 0))[1]:
                need[s] = (s, v)

        for b in reads:
            if b.w is not None:
                add(b.w)
        for b in writes:
            if b.w is not None and (is_dma or b.w[2] != eng.name) and not (is_dma and b.w[0] is my_sem):
                add(b.w)
            for r in b.r:
                if is_dma or r[2] != eng.name:
                    add(r)
        for s, v in need.values():
            if eng.seen.get(s, 0) >= v:
                continue
            eng.h.wait_ge(s.h, v)
            eng.seen[s] = v
            self.nwait += 1

    def op(self, eng, fn, reads=(), writes=(), sig=True):
        self._waits(eng, reads, writes)
        ins = fn()
        self.ninst += 1
        if sig:
            eng.sem.count += 1
            ins.then_inc(eng.sem.h, 1)
            tok = (eng.sem, eng.sem.count, eng.name)
        else:
            tok = (eng.sem, eng.sem.count + 1, eng.name)
        for b in reads:
            b.r.append(tok)
        for b in writes:
            b.w = tok
            b.r = []
        return ins

    def dma(self, sem, out, in_, reads=(), writes=(), q=None, **kw):
        q = q or self.sp
        self._waits(q, reads, writes, is_dma=True, my_sem=sem)
        ins = q.h.dma_start(out=out, in_=in_, **kw)
        ins.then_inc(sem.h, 16)
        sem.count += 16
        tok = (sem, sem.count, "dma:" + sem.name)
        for b in reads:
            b.r.append(tok)
        for b in writes:
            b.w = tok
            b.r = []
        self.ninst += 1
        return ins

    def wait_all(self, eng, sems):
        for s in sems:
            if s.count > 0 and eng.seen.get(s, 0) < s.count:
                eng.h.wait_ge(s.h, s.count)
                eng.seen[s] = s.count


def build_program(n_layers=DEPTH, n_seq=BPC, stop=99):
    nc = bass.Bass("TRN2", target_bir_lowering=False)
    dt_in = lambda name, shape, dt=F32: nc.dram_tensor(name, shape, dt, kind="ExternalInput").ap()
    x_d = dt_in("x", [BPC, S, D])
    cT_d = dt_in("cT", [128, KC, BPC])
    wada_d = dt_in("w_ada", [DEPTH, D, 3 * D])
    badaT_d = dt_in("b_adaT", [128, DEPTH, 24])
    win_d = dt_in("w_in", [DEPTH, D, IN_W])
    qg_d = dt_in("q_gain", [DEPTH, 64])
    kg_d = dt_in("k_gain", [DEPTH, 64])
    cwT_d = dt_in("conv_wT", [128, DEPTH, 2, 3])
    fmix_d = dt_in("f_mix", [DEPTH, 4, 64, 64])
    wout_d = dt_in("w_out", [DEPTH, D, D])
    lng_d = dt_in("ln_g", [DEPTH, D])
    lnb_d = dt_in("ln_b", [DEPTH, D])
    identb_d = dt_in("ident_bf", [128, 128], BF16)
    identf_d = dt_in("ident_f", [128, 128])
    ropec_d = dt_in("rope_cos", [128, NT, 32])
    ropes_d = dt_in("rope_sin", [128, NT, 32])
    dft_d = dt_in("dft_cs", [S, 2, S], BF16)
    csbd_d = dt_in("cs64bd", [128, 2, 2, 128], BF16)
    y_d = nc.dram_tensor("y", [BPC, S, D], F32, kind="ExternalOutput").ap()

    with contextlib.ExitStack() as st:
        fw = FW(nc, st)
        pe, act, dve, pool = fw.pe, fw.act, fw.dve, fw.pool
        V, A, P, T = nc.vector, nc.scalar, nc.gpsimd, nc.tensor

        hT = fw.sbuf("hT", [128, KC, S], BF16)
        yT = fw.sbuf("yT", [128, 8, S], BF16)
        QT = fw.sbuf("QT", [128, 4, S], BF16)
        KTp = fw.sbuf("KTp", [128, 2, S], BF16)
        Vaug = fw.sbuf("Vaug", [128, NT, 2, 128], BF16)
        NPT = 4
        PT = fw.sbuf("PT", [128, NPT, 2, 512], BF16)
        fuT = fw.sbuf("fuT", [128, 2, S], BF16)
        wbf = fw.sbuf("wbf", [128, 2, KC, 512], BF16)
        NSTG = 3
        stage = fw.sbuf("stage", [128, NSTG, 1024], F32)
        NXIN = 2
        xin = fw.sbuf("xin", [128, NXIN, D], F32)
        NXR = 2
        xres = fw.sbuf("xres", [128, NXR, D], F32)
        lngb = fw.sbuf("lngb", [128, D], F32)
        lnbb = fw.sbuf("lnbb", [128, D], F32)
        gateb = fw.sbuf("gateb", [128, D], F32)
        ropec = fw.sbuf("ropec", [128, NT, 32], F32)
        ropes = fw.sbuf("ropes", [128, NT, 32], F32)
        zt = fw.sbuf("zt", [128, S + 2], F32)
        xh = fw.sbuf("xh", [128, 2, D], BF16)
        wk = fw.sbuf("wk", [128, 4, 512], F32)
        qrb = fw.sbuf("qrb", [128, 2, 512], BF16)
        identb = fw.sbuf("identb", [128, 128], BF16)
        identf = fw.sbuf("identf", [128, 128], F32)
        onesb = fw.sbuf("onesb", [128, 128], BF16)
        dg = fw.sbuf("dg", [128, 2, 128], F32)
        dgb = fw.sbuf("dgb", [128, 2, 2, 128], BF16)
        csbd = fw.sbuf("csbd", [128, 2, 2, 128], BF16)
        fm = fw.sbuf("fm", [128, 2, 64], F32)
        fmb = fw.sbuf("fmb", [128, 2, 2, 64], BF16)
        cTb = fw.sbuf("cTb", [128, 2, KC, BPC], BF16)
        whi = fw.sbuf("whi", [128, 2, 1024], BF16)
        ABbd = fw.sbuf("ABbd", [128, 2, 2, 128], BF16)
        cTs = fw.sbuf("cTs", [128, KC, BPC], F32)
        badaT = fw.sbuf("badaT", [128, DEPTH, 24], F32)
        modT = fw.sbuf("modT", [128, 24, BPC], F32)
        s1T = fw.sbuf("s1T", [128, 8, BPC], F32)
        g1T = fw.sbuf("g1T", [128, 8, BPC], F32)
        cwT = fw.sbuf("cwT", [128, DEPTH, 2, 3], F32)
        qgb = fw.sbuf("qgb", [128, 64], F32)
        kgb = fw.sbuf("kgb", [128, 64], F32)
        mhalf = fw.sbuf("mhalf", [128, 8], F32)
        stt = fw.sbuf("stt", [128, 4, 2, 6], F32)
        mv = fw.sbuf("mv", [128, 4, 2], F32)
        sml = fw.sbuf("sml", [128, 4, 32], F32)

        pacc = fw.psum("pacc", [128, 6, 512], F32)
        ptr = fw.psum("ptr", [128, 2, 1024], BF16)

        B = lambda n: Buf(n)
        hT_b = [B("hT%d" % i) for i in range(NT)]
        tab_b = [B("tab%d" % i) for i in range(3)]
        yT_b = [[B("yT%d_%d" % (c, t)) for t in range(4)] for c in range(8)]
        QT_b = [B("QT%d" % j) for j in range(4)]
        UAB_b = B("UAB")
        KT_b = B("KTp")
        Va_b = B("Vaug")
        PT_b = [B("PT%d" % i) for i in range(NPT)]
        fuT_b = B("fuT")
        wbf_b = [B("wbf0"), B("wbf1")]
        stg_b = [B("stg%d" % i) for i in range(NSTG)]
        xin_b = [B("xin%d" % i) for i in range(NXIN)]
        xres_b = [B("xres%d" % i) for i in range(NXR)]
        lng_b, lnb_b, gate_b = B("lng"), B("lnb"), B("gate")
        rope_b, z_b = B("rope"), B("z")
        xh_b = [B("xh0"), B("xh1")]
        wk_b = [B("wk%d" % i) for i in range(4)]
        qrb_b = [B("qrb0"), B("qrb1")]
        cd_b, cm_b = B("constd"), B("constm")
        dg_b = [B("dg0"), B("dg1")]
        fm_b, AB_b = B("fm"), B("ABbd")
        whi_b = [B("whi0"), B("whi1")]
        cTb_b = B("cTb")
        mod_b, s1_b, g1_b = B("modT"), B("s1T"), B("g1T")
        gain_b = B("gains")
        stt_b = [B("stt%d" % i) for i in range(4)]
        mv_b = [B("mv%d" % i) for i in range(4)]
        sml_b = [B("sml%d" % i) for i in range(4)]
        pacc_b = [B("pacc%d" % i) for i in range(6)]
        ptr_b = [B("ptr0"), B("ptr1")]
        ydram_b = [[B("yd%d_%d" % (b, i)) for i in range(NT)] for b in range(BPC)]

        s_const = fw.sem("d_const")
        s_stg = [fw.sem("d_stg%d" % i) for i in range(NSTG)]
        s_xin = [fw.sem("d_xin%d" % i) for i in range(NXIN)]
        s_xres = [fw.sem("d_xres%d" % i) for i in range(NXR)]
        s_tab = [fw.sem("d_tab%d" % i) for i in range(3)]
        s_lng, s_lnb, s_gain, s_fm = fw.sem("d_lng"), fw.sem("d_lnb"), fw.sem("d_gain"), fw.sem("d_fm")

        tabv = hT[:, 0:3, :].rearrange("p a (c k) -> p a c k", c=2)
        UAB = QT[:, :, :].rearrange("p j (a m c n) -> p (j a) m c n", a=4, m=2, c=2)

        for (dst, src) in [(identb, identb_d), (identf, identf_d), (ropec, ropec_d), (ropes, ropes_d),
                           (csbd, csbd_d), (cTs, cT_d), (badaT, badaT_d), (cwT, cwT_d)]:
            fw.dma(s_const, dst[:], src, writes=[cd_b, rope_b])
        fw.op(pool, lambda: P.memset(Vaug[:], 1.0), writes=[Va_b])
        fw.op(pool, lambda: P.memset(KTp[:], 0.0), writes=[KT_b])
        fw.op(pool, lambda: P.memset(zt[:], 0.0), writes=[z_b])
        fw.op(pool, lambda: P.memset(mhalf[:], -0.5), writes=[cm_b])
        fw.op(pool, lambda: P.memset(onesb[:], 1.0), writes=[cm_b])
        fw.op(dve, lambda: V.tensor_copy(out=cTb[:, 0, :, :], in_=cTs[:]), reads=[cd_b], writes=[cTb_b])
        fw.op(dve, lambda: V.tensor_tensor(out=cTb[:, 1, :, :], in0=cTs[:], in1=cTb[:, 0, :, :], op=ALU.subtract), reads=[cd_b, cTb_b], writes=[cTb_b])
        fw.op(pool, lambda: P.memset(ABbd[:], 0.0), writes=[AB_b])

        rr = {"stg": 0, "xin": 0, "xres": 0, "tab": 0, "wk": 0, "whi": 0, "st": 0, "pa": 0, "ptr": 0, "pt": 0, "xh": 0, "qrb": 0, "dg": 0}

        def nxt(key, n):
            v = rr[key]
            rr[key] = (v + 1) % n
            return v

        def rstd_from(var_ap, out_ap, slot_b, scale, eps):
            n = var_ap.shape[-1]
            fw.op(dve, lambda: V.tensor_scalar(out=out_ap, in0=var_ap, scalar1=scale, scalar2=eps,
                                               op0=ALU.mult, op1=ALU.add), reads=[slot_b], writes=[slot_b])
            fw.op(pool, lambda: P.tensor_tensor(out=out_ap, in0=out_ap, in1=mhalf[:, 0:n], op=ALU.pow),
                  reads=[slot_b, cd_b, cm_b], writes=[slot_b])

        def emit_mod(l):
            pa = 5
            for oc in range(24):
                sl = nxt("stg", NSTG)
                fw.dma(s_stg[sl], stage[:, sl, :].rearrange("p (k c) -> p k c", k=KC),
                       wada_d[l, :, oc * 128:(oc + 1) * 128].rearrange("(k p) c -> p k c", p=128),
                       writes=[stg_b[sl]])
                ws = nxt("whi", 2)
                fw.op(pool, lambda sl=sl, ws=ws: P.tensor_copy(out=whi[:, ws, :], in_=stage[:, sl, :]), reads=[stg_b[sl]], writes=[whi_b[ws]])
                for kc in range(KC):
                    for hl in range(2):
                        fw.op(pe, lambda kc=kc, ws=ws, oc=oc, hl=hl: T.matmul(
                            pacc[:, pa, oc * 2:oc * 2 + 2], lhsT=whi[:, ws, kc * 128:(kc + 1) * 128],
                            rhs=cTb[:, hl, kc, :], start=(kc == 0 and hl == 0), stop=(kc == KC - 1 and hl == 1)),
                            reads=[whi_b[ws], cTb_b], writes=[pacc_b[pa]], sig=(kc == KC - 1 and hl == 1))
            fw.op(dve, lambda: V.tensor_tensor(
                out=modT[:], in0=pacc[:, pa, 0:48].rearrange("p (o b) -> p o b", b=BPC),
                in1=badaT[:, l, :].unsqueeze(2).to_broadcast([128, 24, BPC]), op=ALU.add),
                reads=[pacc_b[pa], cd_b, cm_b], writes=[mod_b])
            fw.op(dve, lambda: V.tensor_scalar(out=s1T[:], in0=modT[:, 8:16, :], scalar1=1.0, scalar2=None, op0=ALU.add),
                  reads=[mod_b], writes=[s1_b])
            fw.op(dve, lambda: V.tensor_scalar(out=g1T[:], in0=modT[:, 16:24, :], scalar1=1.0, scalar2=None, op0=ALU.add),
                  reads=[mod_b], writes=[g1_b])
            fw.dma(s_gain, qgb[:], qg_d[l].partition_broadcast(128), writes=[gain_b])
            fw.dma(s_gain, kgb[:], kg_d[l].partition_broadcast(128), writes=[gain_b])
            fw.dma(s_fm, fm[:], fmix_d[l].rearrange("(a g) m d -> (g m) a d", a=2), writes=[fm_b])
            fw.op(dve, lambda: V.tensor_copy(out=fmb[:, :, 0, :], in_=fm[:]), reads=[fm_b], writes=[fm_b])
            fw.op(dve, lambda: V.tensor_tensor(out=fmb[:, :, 1, :], in0=fm[:], in1=fmb[:, :, 0, :], op=ALU.subtract), reads=[fm_b], writes=[fm_b])
            for m2 in range(2):
                for ab in range(2):
                    for t3, (th, fh) in enumerate(((0, 0), (0, 1), (1, 0))):
                        fw.op(pe, lambda m2=m2, ab=ab, th=th, fh=fh, t3=t3: T.matmul(
                            pacc[:, pa, 64 + (m2 * 2 + ab) * 64: 64 + (m2 * 2 + ab + 1) * 64], lhsT=csbd[:, ab, th, :], rhs=fmb[:, m2, fh, :],
                            start=(t3 == 0), stop=(t3 == 2)), reads=[fm_b, cd_b, cm_b], writes=[pacc_b[pa]], sig=(t3 == 2))
            for m2 in range(2):
                for ab in range(2):
                    c0 = 64 + (m2 * 2 + ab) * 64
                    fw.op(dve, lambda m2=m2, ab=ab, c0=c0: V.tensor_copy(out=ABbd[0:64, m2, ab, 0:64], in_=pacc[0:64, pa, c0:c0 + 64]),
                          reads=[pacc_b[pa]], writes=[AB_b])
                    fw.op(dve, lambda m2=m2, ab=ab, c0=c0: V.tensor_copy(out=ABbd[64:128, m2, ab, 64:128], in_=pacc[64:128, pa, c0:c0 + 64]),
                          reads=[pacc_b[pa]], writes=[AB_b])

        def load_wgroup(l, slot, src0, ncol_src, casts):
            for kc in range(KC):
                sl = nxt("stg", NSTG)
                fw.dma(s_stg[sl], stage[:, sl, 0:ncol_src], win_d[l, kc * 128:(kc + 1) * 128, src0:src0 + ncol_src],
                       writes=[stg_b[sl]])
                for cf in casts:
                    o_ap, i_ap = cf(wbf[:, slot, kc, :], stage[:, sl, :])
                    fw.op(pool, lambda o_ap=o_ap, i_ap=i_ap: P.tensor_copy(out=o_ap, in_=i_ap),
                          reads=[stg_b[sl]], writes=[wbf_b[slot]])

        perm_cast = lambda o, i: (o[:, 0:512].rearrange("p (j h d) -> p j h d", j=4, h=2),
                                  i[:, 0:512].rearrange("p (h j d) -> p j h d", h=2, j=4))

        def rope(src, dst, i, H, sb, wkA, wkB, src_bufs, dst_bufs):
            sv = src.rearrange("p (h a b d) -> p h a b d", h=H, a=2, b=2)
            dv = dst.rearrange("p (h a b d) -> p h a b d", h=H, a=2, b=2)
            cs = ropec[:, i, :].rearrange("p (a d) -> p a d", a=2).unsqueeze(1).to_broadcast([128, H, 2, 16])
            sn = ropes[:, i, :].rearrange("p (a d) -> p a d", a=2).unsqueeze(1).to_broadcast([128, H, 2, 16])
            n = H * 32
            t1 = wkA[:, 0:n].rearrange("p (h a d) -> p h a d", h=H, a=2)
            t2 = wkA[:, 256:256 + n].rearrange("p (h a d) -> p h a d", h=H, a=2)
            t3 = wkB[:, 0:n].rearrange("p (h a d) -> p h a d", h=H, a=2)
            t4 = wkB[:, 256:256 + n].rearrange("p (h a d) -> p h a d", h=H, a=2)
            x0 = sv[:, :, :, 0, :]
            x1 = sv[:, :, :, 1, :]
            wa, wb = sb
            fw.op(dve, lambda: V.tensor_tensor(out=t1, in0=x0, in1=cs, op=ALU.mult), reads=src_bufs + [rope_b], writes=[wa])
            fw.op(pool, lambda: P.tensor_tensor(out=t2, in0=x1, in1=sn, op=ALU.mult), reads=src_bufs + [rope_b], writes=[wa])
            fw.op(dve, lambda: V.tensor_tensor(out=t3, in0=x1, in1=cs, op=ALU.mult), reads=src_bufs + [rope_b], writes=[wb])
            fw.op(pool, lambda: P.tensor_tensor(out=t4, in0=x0, in1=sn, op=ALU.mult), reads=src_bufs + [rope_b], writes=[wb])
            fw.op(dve, lambda: V.tensor_tensor(out=dv[:, :, :, 0, :], in0=t1, in1=t2, op=ALU.subtract), reads=[wa], writes=dst_bufs)
            fw.op(dve, lambda: V.tensor_tensor(out=dv[:, :, :, 1, :], in0=t3, in1=t4, op=ALU.add), reads=[wb], writes=dst_bufs)

        def emit_layer(b, l, x_src, last):
            for kc in range(KC):
                ds = nxt("dg", 2)
                pa = 4 + (kc // 4)
                fw.op(dve, lambda kc=kc, ds=ds: V.tensor_scalar(out=dg[:, ds, :], in0=identf[:], scalar1=g1T[:, kc, b:b + 1],
                                                              scalar2=None, op0=ALU.mult),
                      reads=[cd_b, cm_b, g1_b], writes=[dg_b[ds]])
                fw.op(dve, lambda ds=ds: V.tensor_copy(out=dgb[:, ds, 0, :], in_=dg[:, ds, :]), reads=[dg_b[ds]], writes=[dg_b[ds]])
                fw.op(dve, lambda ds=ds: V.tensor_tensor(out=dgb[:, ds, 1, :], in0=dg[:, ds, :], in1=dgb[:, ds, 0, :], op=ALU.subtract), reads=[dg_b[ds]], writes=[dg_b[ds]])
                for hl in range(2):
                    fw.op(pe, lambda kc=kc, ds=ds, pa=pa, hl=hl: T.matmul(pacc[:, pa, (kc % 4) * 128:(kc % 4 + 1) * 128], lhsT=onesb[:], rhs=dgb[:, ds, hl, :],
                                                                        start=(hl == 0), stop=(hl == 1)),
                          reads=[dg_b[ds], cd_b, cm_b], writes=[pacc_b[pa]], sig=(hl == 1))
            for h2 in range(2):
                fw.op(act, lambda h2=h2: A.copy(out=gateb[:, h2 * 512:(h2 + 1) * 512], in_=pacc[:, 4 + h2, :]),
                      reads=[pacc_b[4 + h2]], writes=[gate_b])

            if stop <= 1:
                return
            load_wgroup(l, 0, 512, 256, [lambda o, i: (o[:, 0:256], i[:, 0:256])])
            load_wgroup(l, 1, 0, 512, [perm_cast])

            if stop <= 1.5:
                return
            for i in range(NT):
                xs_ = nxt("xin", NXIN)
                fw.dma(s_xin[xs_], xin[:, xs_, :], x_src[b, i * 128:(i + 1) * 128, :],
                       reads=[ydram_b[b][i]], writes=[xin_b[xs_]])
                ss = nxt("st", 4)
                for h2 in range(2):
                    fw.op(dve, lambda h2=h2, ss=ss, xs_=xs_: V.bn_stats(out=stt[:, ss, h2, :], in_=xin[:, xs_, h2 * 512:(h2 + 1) * 512]),
                          reads=[xin_b[xs_]], writes=[stt_b[ss]])
                fw.op(dve, lambda ss=ss: V.bn_aggr(out=mv[:, ss, :], in_=stt[:, ss, :, :].rearrange("p a b -> p (a b)")), reads=[stt_b[ss]], writes=[mv_b[ss]])
                fw.op(dve, lambda ss=ss: V.tensor_scalar(out=sml[:, ss, 0:1], in0=mv[:, ss, 1:2], scalar1=1.0, scalar2=LN_EPS,
                                                       op0=ALU.mult, op1=ALU.add), reads=[mv_b[ss]], writes=[sml_b[ss]])
                fw.op(pool, lambda ss=ss: P.tensor_tensor(out=sml[:, ss, 0:1], in0=sml[:, ss, 0:1], in1=mhalf[:, 0:1], op=ALU.pow),
                      reads=[sml_b[ss], cd_b, cm_b], writes=[sml_b[ss]])
                if stop <= 1.7:
                    continue
                hs = nxt("xh", 2)
                fw.op(dve, lambda ss=ss, hs=hs, xs_=xs_: V.tensor_scalar(
                    out=xh[:, hs, :], in0=xin[:, xs_, :], scalar1=mv[:, ss, 0:1], scalar2=sml[:, ss, 0:1],
                    op0=ALU.subtract, op1=ALU.mult), reads=[xin_b[xs_], mv_b[ss], sml_b[ss]], writes=[xh_b[hs]])
                if stop <= 1.8:
                    continue
                pb = nxt("ptr", 2)
                for kc in range(KC):
                    fw.op(pe, lambda kc=kc, hs=hs, pb=pb: T.transpose(ptr[:, pb, kc * 128:(kc + 1) * 128], xh[:, hs, kc * 128:(kc + 1) * 128], identb[:]),
                          reads=[xh_b[hs], cd_b, cm_b], writes=[ptr_b[pb]], sig=(kc == KC - 1))
                for kc in range(KC):
                    fw.op(act, lambda kc=kc, pb=pb, i=i: A.activation(
                        out=hT[:, kc, i * 128:(i + 1) * 128], in_=ptr[:, pb, kc * 128:(kc + 1) * 128], func=ACTF.Identity,
                        scale=s1T[:, kc, b:b + 1], bias=modT[:, kc, b:b + 1]),
                        reads=[ptr_b[pb], s1_b, mod_b], writes=[hT_b[i]] + (tab_b if kc == 0 else []))

            if stop <= 2:
                return
            for i in range(NT):
                pa = nxt("pa", 4)
                for kc in range(KC):
                    fw.op(pe, lambda kc=kc, pa=pa, i=i: T.matmul(pacc[:, pa, 0:256], lhsT=hT[:, kc, i * 128:(i + 1) * 128], rhs=wbf[:, 0, kc, 0:256],
                                                               start=(kc == 0), stop=(kc == KC - 1)),
                          reads=[hT_b[i], wbf_b[0]], writes=[pacc_b[pa]], sig=(kc == KC - 1))
                fw.op(act, lambda pa=pa, i=i: A.copy(out=Vaug[:, i, 0, 0:64], in_=pacc[:, pa, 128:192]), reads=[pacc_b[pa]], writes=[Va_b])
                fw.op(act, lambda pa=pa, i=i: A.copy(out=Vaug[:, i, 1, 64:128], in_=pacc[:, pa, 192:256]), reads=[pacc_b[pa]], writes=[Va_b])
                w0 = nxt("wk", 4); w1 = nxt("wk", 4)
                ss = nxt("st", 4)
                fw.op(act, lambda pa=pa, w0=w0: A.activation(out=wk[:, w0, 0:128], in_=pacc[:, pa, 0:128], func=ACTF.Square),
                      reads=[pacc_b[pa]], writes=[wk_b[w0]])
                fw.op(dve, lambda w0=w0, ss=ss: V.tensor_reduce(out=sml[:, ss, 0:2], in_=wk[:, w0, 0:128].rearrange("p (h d) -> p h d", h=2),
                                                              axis=AX.X, op=ALU.add), reads=[wk_b[w0]], writes=[sml_b[ss]])
                rstd_from(sml[:, ss, 0:2], sml[:, ss, 0:2], sml_b[ss], 1.0 / 64, RMS_EPS)
                fw.op(dve, lambda pa=pa, w0=w0, ss=ss: V.tensor_tensor(
                    out=wk[:, w0, 128:256].rearrange("p (h d) -> p h d", h=2), in0=pacc[:, pa, 0:128].rearrange("p (h d) -> p h d", h=2),
                    in1=sml[:, ss, 0:2].unsqueeze(2).to_broadcast([128, 2, 64]), op=ALU.mult),
                    reads=[pacc_b[pa], sml_b[ss]], writes=[wk_b[w0]])
                fw.op(pool, lambda w0=w0: P.tensor_tensor(
                    out=wk[:, w0, 256:384].rearrange("p (h d) -> p h d", h=2), in0=wk[:, w0, 128:256].rearrange("p (h d) -> p h d", h=2),
                    in1=kgb[:].unsqueeze(1).to_broadcast([128, 2, 64]), op=ALU.mult), reads=[wk_b[w0], gain_b], writes=[wk_b[w0]])
                qs = nxt("qrb", 2)
                w2 = nxt("wk", 4)
                rope(wk[:, w0, 256:384], qrb[:, qs, 0:128], i, 2, (wk_b[w1], wk_b[w2]), wk[:, w1, :], wk[:, w2, :], [wk_b[w0]], [qrb_b[qs]])
                pb = nxt("ptr", 2)
                fw.op(pe, lambda qs=qs, pb=pb: T.transpose(ptr[:, pb, 0:128], qrb[:, qs, 0:128], identb[:]),
                      reads=[qrb_b[qs], cd_b, cm_b], writes=[ptr_b[pb]])
                fw.op(act, lambda pb=pb, i=i: A.copy(out=KTp[0:64, 0, i * 128:(i + 1) * 128], in_=ptr[0:64, pb, 0:128]), reads=[ptr_b[pb]], writes=[KT_b])
                fw.op(act, lambda pb=pb, i=i: A.copy(out=KTp[64:128, 1, i * 128:(i + 1) * 128], in_=ptr[64:128, pb, 0:128]), reads=[ptr_b[pb]], writes=[KT_b])

            if stop <= 3:
                return
            load_wgroup(l, 0, 768, 512, [perm_cast])

            for i in range(NT):
                pa = nxt("pa", 4)
                for kc in range(KC):
                    fw.op(pe, lambda kc=kc, pa=pa, i=i: T.matmul(pacc[:, pa, :], lhsT=hT[:, kc, i * 128:(i + 1) * 128], rhs=wbf[:, 1, kc, :],
                                                               start=(kc == 0), stop=(kc == KC - 1)),
                          reads=[hT_b[i], wbf_b[1]], writes=[pacc_b[pa]], sig=(kc == KC - 1))
                w0 = nxt("wk", 4); w1 = nxt("wk", 4); w2 = nxt("wk", 4); w3 = nxt("wk", 4)
                ss = nxt("st", 4)
                fw.op(act, lambda pa=pa, w0=w0: A.activation(out=wk[:, w0, :], in_=pacc[:, pa, :], func=ACTF.Square),
                      reads=[pacc_b[pa]], writes=[wk_b[w0]])
                fw.op(dve, lambda w0=w0, ss=ss: V.tensor_reduce(out=sml[:, ss, 0:8], in_=wk[:, w0, :].rearrange("p (h d) -> p h d", h=8),
                                                              axis=AX.X, op=ALU.add), reads=[wk_b[w0]], writes=[sml_b[ss]])
                rstd_from(sml[:, ss, 0:8], sml[:, ss, 0:8], sml_b[ss], 1.0 / 64, RMS_EPS)
                fw.op(dve, lambda pa=pa, w0=w0, ss=ss: V.tensor_tensor(
                    out=wk[:, w0, :].rearrange("p (h d) -> p h d", h=8), in0=pacc[:, pa, :].rearrange("p (h d) -> p h d", h=8),
                    in1=sml[:, ss, 0:8].unsqueeze(2).to_broadcast([128, 8, 64]), op=ALU.mult),
                    reads=[pacc_b[pa], sml_b[ss]], writes=[wk_b[w0]])
                fw.op(pool, lambda w0=w0, w1=w1: P.tensor_tensor(
                    out=wk[:, w1, :].rearrange("p (h d) -> p h d", h=8), in0=wk[:, w0, :].rearrange("p (h d) -> p h d", h=8),
                    in1=qgb[:].unsqueeze(1).to_broadcast([128, 8, 64]), op=ALU.mult), reads=[wk_b[w0], gain_b], writes=[wk_b[w1]])
                qs = nxt("qrb", 2)
                rope(wk[:, w1, :], qrb[:, qs, :], i, 8, (wk_b[w2], wk_b[w3]), wk[:, w2, :], wk[:, w3, :], [wk_b[w1]], [qrb_b[qs]])
                pb = nxt("ptr", 2)
                for j in range(4):
                    fw.op(pe, lambda j=j, qs=qs, pb=pb: T.transpose(ptr[:, pb, j * 128:(j + 1) * 128], qrb[:, qs, j * 128:(j + 1) * 128], identb[:]),
                          reads=[qrb_b[qs], cd_b, cm_b], writes=[ptr_b[pb]], sig=(j == 3))
                fw.op(act, lambda pb=pb, i=i: A.copy(out=QT[:, :, i * 128:(i + 1) * 128], in_=ptr[:, pb, 0:512].rearrange("p (j t) -> p j t", j=4)),
                      reads=[ptr_b[pb]], writes=QT_b + [UAB_b])

            if stop <= 4:
                return
            for j in range(4):
                for tc in range(4):
                    pa = nxt("pa", 4)
                    for kc in range(KC):
                        fw.op(pe, lambda kc=kc, pa=pa, j=j, tc=tc: T.matmul(
                            pacc[:, pa, :], lhsT=wbf[:, 0, kc, j * 128:(j + 1) * 128], rhs=hT[:, kc, tc * 512:(tc + 1) * 512],
                            start=(kc == 0), stop=(kc == KC - 1)),
                            reads=hT_b[tc * 4:(tc + 1) * 4] + [wbf_b[0]], writes=[pacc_b[pa]], sig=(kc == KC - 1))
                    fw.op(act, lambda pa=pa, j=j, tc=tc: A.activation(out=yT[:, j, tc * 512:(tc + 1) * 512], in_=pacc[:, pa, :], func=ACTF.Silu),
                          reads=[pacc_b[pa]], writes=[yT_b[j][tc]])

            if stop <= 5:
                return
            conv_cast = lambda m: (lambda o, i: (o[:, 0:512].rearrange("p (g c) -> p g c", g=4),
                                                 i[:, 0:1024].rearrange("p (g m c) -> p g m c", g=4, m=2)[:, :, m, :]))
            load_wgroup(l, 1, 1280, 1024, [conv_cast(0)])
            load_wgroup(l, 0, 1280, 1024, [conv_cast(1)])

            if stop <= 6:
                return
            for j in range(4):
                for qc in range(4):
                    steps = []

                    def qk(kt, sb_):
                        for kv in range(2):
                            fw.op(pe, lambda kv=kv, kt=kt, sb_=sb_: T.matmul(
                                pacc[:, sb_ * 2 + kv, :], lhsT=KTp[:, kv, kt * 128:(kt + 1) * 128], rhs=QT[:, j, qc * 512:(qc + 1) * 512],
                                start=True, stop=True), reads=[KT_b, QT_b[j]], writes=[pacc_b[sb_ * 2 + kv]])

                    def ex(kt, sb_, ps):
                        fw.op(act, lambda sb_=sb_, ps=ps: A.activation(out=PT[:, ps, :, :], in_=pacc[:, sb_ * 2:sb_ * 2 + 2, :], func=ACTF.Exp, scale=0.125),
                              reads=[pacc_b[sb_ * 2], pacc_b[sb_ * 2 + 1]], writes=[PT_b[ps]])

                    def pv(kt, ps):
                        for kv in range(2):
                            fw.op(pe, lambda kv=kv, kt=kt, ps=ps: T.matmul(
                                pacc[:, 4 + kv, :], lhsT=Vaug[:, kt, kv, :], rhs=PT[:, ps, kv, :], start=(kt == 0), stop=(kt == NT - 1)),
                                reads=[Va_b, PT_b[ps]], writes=[pacc_b[4 + kv]], sig=(kt == NT - 1))

                    pss = []
                    qk(0, 0)
                    for kt in range(NT):
                        ps = nxt("pt", NPT)
                        pss.append(ps)
                        ex(kt, kt % 2, ps)
                        if kt + 1 < NT:
                            qk(kt + 1, (kt + 1) % 2)
                        pv(kt, ps)
                    w0 = nxt("wk", 4); w1 = nxt("wk", 4)
                    cols = slice(qc * 512, (qc + 1) * 512)
                    fw.op(dve, lambda w0=w0: V.reciprocal(out=wk[0:64, w0, :], in_=pacc[64:128, 4, :]), reads=[pacc_b[4]], writes=[wk_b[w0]])
                    fw.op(dve, lambda w0=w0: V.tensor_tensor(out=wk[0:64, w0, :], in0=pacc[0:64, 4, :], in1=wk[0:64, w0, :], op=ALU.mult),
                          reads=[pacc_b[4], wk_b[w0]], writes=[wk_b[w0]])
                    fw.op(pool, lambda w0=w0: P.tensor_tensor(out=yT[0:64, j, cols], in0=yT[0:64, j, cols], in1=wk[0:64, w0, :], op=ALU.mult),
                          reads=[wk_b[w0], yT_b[j][qc]], writes=[yT_b[j][qc]])
                    fw.op(dve, lambda w1=w1: V.reciprocal(out=wk[64:128, w1, :], in_=pacc[0:64, 5, :]), reads=[pacc_b[5]], writes=[wk_b[w1]])
                    fw.op(dve, lambda w1=w1: V.tensor_tensor(out=wk[64:128, w1, :], in0=pacc[64:128, 5, :], in1=wk[64:128, w1, :], op=ALU.mult),
                          reads=[pacc_b[5], wk_b[w1]], writes=[wk_b[w1]])
                    fw.op(pool, lambda w1=w1: P.tensor_tensor(out=yT[64:128, j, cols], in0=yT[64:128, j, cols], in1=wk[64:128, w1, :], op=ALU.mult),
                          reads=[wk_b[w1], yT_b[j][qc]], writes=[yT_b[j][qc]])

            if stop <= 7:
                return
            for m in range(2):
                slot = 1 - m
                for tc in range(4):
                    pas = [nxt("pa", 4) for _ in range(4)]
                    for g in range(4):
                        for kc in range(KC):
                            fw.op(pe, lambda kc=kc, g=g, tc=tc, slot=slot, pas=pas: T.matmul(
                                pacc[:, pas[g], :], lhsT=wbf[:, slot, kc, g * 128:(g + 1) * 128], rhs=hT[:, kc, tc * 512:(tc + 1) * 512],
                                start=(kc == 0), stop=(kc == KC - 1)),
                                reads=hT_b[tc * 4:(tc + 1) * 4] + [wbf_b[slot]], writes=[pacc_b[pas[g]]], sig=(kc == KC - 1))
                    w0 = nxt("wk", 4); w1 = nxt("wk", 4)
                    fw.op(act, lambda w0=w0, pas=pas: A.copy(out=wk[:, w0, :], in_=pacc[:, pas[2], :]), reads=[pacc_b[pas[2]]], writes=[wk_b[w0]])
                    fw.op(dve, lambda w0=w0, pas=pas, tc=tc: V.tensor_tensor(out=zt[:, 1 + tc * 512:1 + (tc + 1) * 512], in0=pacc[:, pas[0], :], in1=wk[:, w0, :], op=ALU.mult),
                          reads=[pacc_b[pas[0]], wk_b[w0]], writes=[z_b])
                    fw.op(act, lambda w1=w1, pas=pas: A.activation(out=wk[:, w1, :], in_=pacc[:, pas[3], :], func=ACTF.Silu), reads=[pacc_b[pas[3]]], writes=[wk_b[w1]])
                    fw.op(dve, lambda w1=w1, pas=pas, tc=tc, m=m: V.tensor_tensor(out=yT[:, 4 + m, tc * 512:(tc + 1) * 512], in0=pacc[:, pas[1], :], in1=wk[:, w1, :], op=ALU.mult),
                          reads=[pacc_b[pas[1]], wk_b[w1]], writes=[yT_b[4 + m][tc]])
                for tc in range(4):
                    w0 = nxt("wk", 4)
                    c0 = 1 + tc * 512
                    fw.op(dve, lambda w0=w0, c0=c0, m=m: V.tensor_scalar(out=wk[:, w0, :], in0=zt[:, c0:c0 + 512], scalar1=cwT[:, l, m, 1:2], scalar2=None, op0=ALU.mult),
                          reads=[z_b, cd_b, cm_b], writes=[wk_b[w0]])
                    fw.op(dve, lambda w0=w0, c0=c0, m=m: V.scalar_tensor_tensor(out=wk[:, w0, :], in0=zt[:, c0 - 1:c0 + 511], scalar=cwT[:, l, m, 0:1], in1=wk[:, w0, :],
                                                                           op0=ALU.mult, op1=ALU.add), reads=[z_b, cd_b, cm_b, wk_b[w0]], writes=[wk_b[w0]])
                    fw.op(dve, lambda w0=w0, c0=c0, m=m: V.scalar_tensor_tensor(out=wk[:, w0, :], in0=zt[:, c0 + 1:c0 + 513], scalar=cwT[:, l, m, 2:3], in1=wk[:, w0, :],
                                                                           op0=ALU.mult, op1=ALU.add), reads=[z_b, cd_b, cm_b, wk_b[w0]], writes=[wk_b[w0]])
                    fw.op(pool, lambda w0=w0, tc=tc, m=m: P.tensor_tensor(out=yT[:, 4 + m, tc * 512:(tc + 1) * 512], in0=yT[:, 4 + m, tc * 512:(tc + 1) * 512], in1=wk[:, w0, :], op=ALU.mult),
                          reads=[wk_b[w0], yT_b[4 + m][tc]], writes=[yT_b[4 + m][tc]])
                if m == 0:
                    load_wgroup(l, 1, 2304, 512, [lambda o, i: (o[:, 0:512], i[:, 0:512])])

            if stop <= 8:
                return
            for m2 in range(2):
                for tc in range(4):
                    pa = nxt("pa", 4)
                    for kc in range(KC):
                        fw.op(pe, lambda kc=kc, pa=pa, m2=m2, tc=tc: T.matmul(
                            pacc[:, pa, :], lhsT=wbf[:, 1, kc, m2 * 128:(m2 + 1) * 128], rhs=hT[:, kc, tc * 512:(tc + 1) * 512],
                            start=(kc == 0), stop=(kc == KC - 1)),
                            reads=hT_b[tc * 4:(tc + 1) * 4] + [wbf_b[1]], writes=[pacc_b[pa]], sig=(kc == KC - 1))
                    fw.op(act, lambda pa=pa, m2=m2, tc=tc: A.copy(out=fuT[:, m2, tc * 512:(tc + 1) * 512], in_=pacc[:, pa, :]),
                          reads=[pacc_b[pa]], writes=[fuT_b])
                    pa = nxt("pa", 4)
                    for kc in range(KC):
                        fw.op(pe, lambda kc=kc, pa=pa, m2=m2, tc=tc: T.matmul(
                            pacc[:, pa, :], lhsT=wbf[:, 1, kc, 256 + m2 * 128:256 + (m2 + 1) * 128], rhs=hT[:, kc, tc * 512:(tc + 1) * 512],
                            start=(kc == 0), stop=(kc == KC - 1)),
                            reads=hT_b[tc * 4:(tc + 1) * 4] + [wbf_b[1]], writes=[pacc_b[pa]], sig=(kc == KC - 1))
                    fw.op(act, lambda pa=pa, m2=m2, tc=tc: A.activation(out=yT[:, 6 + m2, tc * 512:(tc + 1) * 512], in_=pacc[:, pa, :], func=ACTF.Silu),
                          reads=[pacc_b[pa]], writes=[yT_b[6 + m2][tc]])
            wall = wbf[:, :, :, :].rearrange("p s k c -> p (s k c)").rearrange("p (k c) -> p k c", k=KC)
            for kc in range(KC):
                sl = nxt("stg", NSTG)
                if kc < 4:
                    fw.dma(s_stg[sl], stage[0:64, sl, :], wout_d[l, kc * 64:(kc + 1) * 64, :], writes=[stg_b[sl]])
                    fw.dma(s_stg[sl], stage[64:128, sl, :], wout_d[l, (4 + kc) * 64:(5 + kc) * 64, :], writes=[stg_b[sl]])
                else:
                    fw.dma(s_stg[sl], stage[:, sl, :], wout_d[l, kc * 128:(kc + 1) * 128, :], writes=[stg_b[sl]])
                fw.op(pool, lambda kc=kc, sl=sl: P.tensor_tensor(out=wall[:, kc, :], in0=stage[:, sl, :], in1=gateb[:], op=ALU.mult),
                      reads=[stg_b[sl], gate_b], writes=wbf_b)
            for i in range(NT):
                pa = nxt("pa", 4)
                for m2 in range(2):
                    fw.op(pe, lambda m2=m2, pa=pa, i=i: T.matmul(
                        pacc[:, pa, m2 * 256:(m2 + 1) * 256], lhsT=fuT[:, m2, i * 128:(i + 1) * 128],
                        rhs=ABbd[:, m2, :, :].rearrange("p a n -> p (a n)"), start=True, stop=True),
                        reads=[fuT_b, AB_b], writes=[pacc_b[pa]], sig=(m2 == 1))
                fw.op(act, lambda pa=pa, i=i: A.copy(out=UAB[:, i, :, :, :].rearrange("p m c n -> p (m c n)"), in_=pacc[:, pa, :]),
                      reads=[pacc_b[pa]], writes=[UAB_b] + QT_b)
            for half in range(2):
                pas = [[nxt("pa", 4) for _ in range(2)] for _ in range(2)]
                for a in range(NT):
                    ts = nxt("tab", 3)
                    fw.dma(s_tab[ts], tabv[:, ts, :, :], dft_d[a * 128:(a + 1) * 128, :, half * 1024:(half + 1) * 1024], writes=[tab_b[ts]] + hT_b)
                    for m2 in range(2):
                        for kq in range(2):
                            for ab in range(2):
                                fw.op(pe, lambda m2=m2, kq=kq, ab=ab, a=a, ts=ts, pas=pas: T.matmul(
                                    pacc[:, pas[m2][kq], :], lhsT=UAB[:, a, m2, ab, :], rhs=tabv[:, ts, ab, kq * 512:(kq + 1) * 512],
                                    start=(a == 0 and ab == 0), stop=(a == NT - 1 and ab == 1)),
                                    reads=[UAB_b, tab_b[ts]], writes=[pacc_b[pas[m2][kq]]], sig=((a == NT - 1 and ab == 1) or (m2 == 1 and kq == 1 and ab == 1)))
                for m2 in range(2):
                    for kq in range(2):
                        tcc = half * 2 + kq
                        fw.op(dve, lambda m2=m2, kq=kq, tcc=tcc, pas=pas: V.tensor_tensor(
                            out=yT[:, 6 + m2, tcc * 512:(tcc + 1) * 512], in0=pacc[:, pas[m2][kq], :], in1=yT[:, 6 + m2, tcc * 512:(tcc + 1) * 512], op=ALU.mult),
                            reads=[pacc_b[pas[m2][kq]], yT_b[6 + m2][tcc]], writes=[yT_b[6 + m2][tcc]])

            if stop <= 9:
                return
            fw.dma(s_lng, lngb[:], lng_d[l].partition_broadcast(128), writes=[lng_b])
            fw.dma(s_lnb, lnbb[:], lnb_d[l].partition_broadcast(128), writes=[lnb_b])
            for i in range(NT):
                xr = nxt("xres", NXR)
                fw.dma(s_xres[xr], xres[:, xr, :], x_src[b, i * 128:(i + 1) * 128, :], reads=[ydram_b[b][i]], writes=[xres_b[xr]])
                pas = [4, 5]
                for n in range(2):
                    for kc in range(KC):
                        fw.op(pe, lambda kc=kc, n=n, i=i: T.matmul(
                            pacc[:, 4 + n, :], lhsT=yT[:, kc, i * 128:(i + 1) * 128], rhs=wall[:, kc, n * 512:(n + 1) * 512],
                            start=(kc == 0), stop=(kc == KC - 1)),
                            reads=[yT_b[kc][i // 4]] + wbf_b, writes=[pacc_b[4 + n]], sig=(kc == KC - 1))
                ss = nxt("st", 4)
                for n in range(2):
                    fw.op(dve, lambda n=n, xr=xr: V.scalar_tensor_tensor(
                        out=xres[:, xr, n * 512:(n + 1) * 512], in0=xres[:, xr, n * 512:(n + 1) * 512], scalar=float(ALPHA),
                        in1=pacc[:, 4 + n, :], op0=ALU.mult, op1=ALU.add),
                        reads=[pacc_b[4 + n], xres_b[xr]], writes=[xres_b[xr]])
                for n in range(2):
                    fw.op(dve, lambda n=n, xr=xr, ss=ss: V.bn_stats(out=stt[:, ss, n, :], in_=xres[:, xr, n * 512:(n + 1) * 512]),
                          reads=[xres_b[xr]], writes=[stt_b[ss]])
                fw.op(dve, lambda ss=ss: V.bn_aggr(out=mv[:, ss, :], in_=stt[:, ss, :, :].rearrange("p a b -> p (a b)")), reads=[stt_b[ss]], writes=[mv_b[ss]])
                fw.op(dve, lambda ss=ss: V.tensor_scalar(out=sml[:, ss, 0:1], in0=mv[:, ss, 1:2], scalar1=1.0, scalar2=LN_EPS,
                                                       op0=ALU.mult, op1=ALU.add), reads=[mv_b[ss]], writes=[sml_b[ss]])
                fw.op(pool, lambda ss=ss: P.tensor_tensor(out=sml[:, ss, 0:1], in0=sml[:, ss, 0:1], in1=mhalf[:, 0:1], op=ALU.pow),
                      reads=[sml_b[ss], cd_b, cm_b], writes=[sml_b[ss]])
                fw.op(dve, lambda ss=ss: V.scalar_tensor_tensor(out=sml[:, ss, 1:2], in0=mv[:, ss, 0:1], scalar=-1.0, in1=sml[:, ss, 0:1],
                                                              op0=ALU.mult, op1=ALU.mult), reads=[mv_b[ss], sml_b[ss]], writes=[sml_b[ss]])
                fw.op(act, lambda xr=xr, ss=ss: A.activation(out=xres[:, xr, :], in_=xres[:, xr, :], func=ACTF.Identity,
                                                           scale=sml[:, ss, 0:1], bias=sml[:, ss, 1:2]),
                      reads=[xres_b[xr], sml_b[ss]], writes=[xres_b[xr]])
                fw.op(pool, lambda xr=xr: P.tensor_tensor(out=xres[:, xr, :], in0=xres[:, xr, :], in1=lngb[:], op=ALU.mult),
                      reads=[xres_b[xr], lng_b], writes=[xres_b[xr]])
                fw.op(dve, lambda xr=xr: V.tensor_tensor(out=xres[:, xr, :], in0=xres[:, xr, :], in1=lnbb[:], op=ALU.add),
                      reads=[xres_b[xr], lnb_b], writes=[xres_b[xr]])
                fw.dma(s_xres[xr], y_d[b, i * 128:(i + 1) * 128, :], xres[:, xr, :], reads=[xres_b[xr]], writes=[ydram_b[b][i]])

        for l in range(n_layers):
            for b in range(n_seq):
                if b == 0:
                    emit_mod(l)
                emit_layer(b, l, x_d if l == 0 else y_d, l == n_layers - 1)
        fw.wait_all(fw.sp, s_xres)
        build_program.stats = (fw.ninst, fw.nwait, {e.name: e.sem.count for e in (pe, act, dve, pool)})
    return nc


def _constants():
    bf = ml_dtypes.bfloat16
    t = np.arange(S)
    rows = (t // 64).astype(np.float32)
    cols = (t % 64).astype(np.float32)
    inv = (1.0 / (10000.0 ** (np.arange(0, 32, 2, dtype=np.float32) / np.float32(32)))).astype(np.float32)
    ra = (rows[:, None] * inv[None, :]).astype(np.float32)
    ca = (cols[:, None] * inv[None, :]).astype(np.float32)
    ang = np.concatenate([ra, ca], axis=1)
    cos = np.cos(ang.astype(np.float64)).astype(np.float32)
    sin = np.sin(ang.astype(np.float64)).astype(np.float32)
    rope_cos = np.ascontiguousarray(cos.reshape(NT, 128, 32).transpose(1, 0, 2))
    rope_sin = np.ascontiguousarray(sin.reshape(NT, 128, 32).transpose(1, 0, 2))
    ks = (np.outer(t, t) % S).astype(np.float64) * (2.0 * np.pi / S)
    dft_cs = np.ascontiguousarray(np.stack([np.cos(ks), -np.sin(ks)], axis=1).astype(np.float32).astype(bf))
    m = np.arange(64)
    a64 = (np.outer(m, m) % 64).astype(np.float64) * (2.0 * np.pi / 64)
    sc = 1.0 / np.sqrt(float(S) * 64.0)
    c64 = np.cos(a64) * sc
    s64 = np.sin(a64) * sc
    z = np.zeros((64, 64))
    c64bd = np.block([[c64, z], [z, c64]]).astype(np.float32)
    s64bd = np.block([[s64, z], [z, s64]]).astype(np.float32)
    def hilo(a):
        hi = a.astype(bf)
        lo = (a - hi.astype(np.float32)).astype(bf)
        return np.stack([hi, lo], axis=1)
    cs64bd = np.ascontiguousarray(np.stack([hilo(c64bd), hilo(s64bd)], axis=1))
    return {
        "ident_bf": np.eye(128, dtype=np.float32).astype(bf),
        "ident_f": np.eye(128, dtype=np.float32),
        "rope_cos": rope_cos, "rope_sin": rope_sin,
        "dft_cs": dft_cs,
        "cs64bd": cs64bd,
    }


def make_in_maps(x, c, w_ada, b_ada, w_in, q_gain, k_gain, conv_w, f_mix, w_out, ln_g, ln_b, ncores=NCORES):
    f = lambda a: np.ascontiguousarray(np.asarray(a, dtype=np.float32))
    consts = _constants()
    shared = {
        "w_ada": f(w_ada), "w_in": f(w_in), "q_gain": f(q_gain), "k_gain": f(k_gain),
        "f_mix": f(f_mix), "w_out": f(w_out), "ln_g": f(ln_g), "ln_b": f(ln_b),
        "b_adaT": f(np.asarray(b_ada).reshape(DEPTH, 24, 128).transpose(2, 0, 1)),
        "conv_wT": f(np.asarray(conv_w).reshape(DEPTH, 3, 2, 128).transpose(3, 0, 2, 1)),
    }
    shared.update(consts)
    x = np.asarray(x, dtype=np.float32)
    c = np.asarray(c, dtype=np.float32)
    maps = []
    for ci in range(ncores):
        m = dict(shared)
        m["x"] = np.ascontiguousarray(x[ci * BPC:(ci + 1) * BPC])
        m["cT"] = f(c[ci * BPC:(ci + 1) * BPC].reshape(BPC, KC, 128).transpose(2, 1, 0))
        maps.append(m)
    return maps


def kernel(x, c, w_ada, b_ada, w_in, q_gain, k_gain, conv_w, f_mix, w_out, ln_g, ln_b):
    maps = make_in_maps(x, c, w_ada, b_ada, w_in, q_gain, k_gain, conv_w, f_mix, w_out, ln_g, ln_b)
    nc = build_program()
    res = run_bass_kernel_spmd(nc, maps, core_ids=list(range(NCORES)))
    return np.concatenate([np.asarray(r["y"], dtype=np.float32) for r in res.results], axis=0)
```

```python
import contextlib
import numpy as np
import ml_dtypes
import concourse.bass as bass
import concourse.mybir as mybir
from concourse.bass_utils import run_bass_kernel_spmd

F32 = mybir.dt.float32
BF16 = mybir.dt.bfloat16
ALU = mybir.AluOpType
ACTF = mybir.ActivationFunctionType
AX = mybir.AxisListType

S = 2048
D = 1024
NT = 16
KC = 8
DEPTH = 4
IN_W = 2816
ALPHA = (2 * DEPTH) ** 0.25
RMS_EPS = 1e-6
LN_EPS = 1e-5
NCORES = 8
BPC = 2


class Buf:
    __slots__ = ("name", "w", "r")

    def __init__(self, name):
        self.name = name
        self.w = None
        self.r = []


class Sem:
    def __init__(self, h, name):
        self.h = h
        self.name = name
        self.count = 0


class Eng:
    def __init__(self, name, h, sem):
        self.name = name
        self.h = h
        self.sem = sem
        self.seen = {}


class FW:
    def __init__(self, nc, stack):
        self.nc = nc
        self.stack = stack
        self.pe = self._eng("pe", nc.tensor)
        self.act = self._eng("act", nc.scalar)
        self.dve = self._eng("dve", nc.vector)
        self.pool = self._eng("pool", nc.gpsimd)
        self.sp = self._eng("sp", nc.sync)
        self.ninst = 0
        self.nwait = 0

    def sem(self, name):
        h = self.stack.enter_context(self.nc.semaphore(name))
        return Sem(h, name)

    def _eng(self, name, h):
        return Eng(name, h, self.sem("s_" + name))

    def sbuf(self, name, shape, dt):
        return self.stack.enter_context(self.nc.sbuf_tensor(name, shape, dt))

    def psum(self, name, shape, dt):
        return self.stack.enter_context(self.nc.psum_tensor(name, shape, dt))

    def _waits(self, eng, reads, writes, is_dma=False, my_sem=None):
        need = {}

        def add(tok):
            s, v, who = tok
            if (not is_dma) and who == eng.name and eng.name == "pe":
                return
            if v > need.get(s, (None, 0))[1]:
                need[s] = (s, v)

        for b in reads:
            if b.w is not None:
                add(b.w)
        for b in writes:
            if b.w is not None and not (is_dma and b.w[0] is my_sem):
                add(b.w)
            for r in b.r:
                add(r)
        for s, v in need.values():
            if eng.seen.get(s, 0) >= v:
                continue
            eng.h.wait_ge(s.h, v)
            eng.seen[s] = v
            self.nwait += 1

    def op(self, eng, fn, reads=(), writes=(), sig=True):
        self._waits(eng, reads, writes)
        ins = fn()
        self.ninst += 1
        if sig:
            eng.sem.count += 1
            ins.then_inc(eng.sem.h, 1)
            tok = (eng.sem, eng.sem.count, eng.name)
        else:
            tok = (eng.sem, eng.sem.count + 1, eng.name)
        for b in reads:
            b.r.append(tok)
        for b in writes:
            b.w = tok
            b.r = []
        return ins

    def dma(self, sem, out, in_, reads=(), writes=(), q=None, **kw):
        q = q or self.sp
        self._waits(q, reads, writes, is_dma=True, my_sem=sem)
        ins = q.h.dma_start(out=out, in_=in_, **kw)
        ins.then_inc(sem.h, 16)
        sem.count += 16
        tok = (sem, sem.count, "dma:" + sem.name)
        for b in reads:
            b.r.append(tok)
        for b in writes:
            b.w = tok
            b.r = []
        self.ninst += 1
        return ins

    def wait_all(self, eng, sems):
        for s in sems:
            if s.count > 0 and eng.seen.get(s, 0) < s.count:
                eng.h.wait_ge(s.h, s.count)
                eng.seen[s] = s.count


def build_program(n_layers=DEPTH, n_seq=BPC, stop=99):
    nc = bass.Bass("TRN2", target_bir_lowering=False)
    dt_in = lambda name, shape, dt=F32: nc.dram_tensor(name, shape, dt, kind="ExternalInput").ap()
    x_d = dt_in("x", [BPC, S, D])
    cT_d = dt_in("cT", [128, KC, BPC])
    wada_d = dt_in("w_ada", [DEPTH, D, 3 * D])
    badaT_d = dt_in("b_adaT", [128, DEPTH, 24])
    win_d = dt_in("w_in", [DEPTH, D, IN_W])
    qg_d = dt_in("q_gain", [DEPTH, 64])
    kg_d = dt_in("k_gain", [DEPTH, 64])
    cwT_d = dt_in("conv_wT", [128, DEPTH, 2, 3])
    fmix_d = dt_in("f_mix", [DEPTH, 4, 64, 64])
    wout_d = dt_in("w_out", [DEPTH, D, D])
    lng_d = dt_in("ln_g", [DEPTH, D])
    lnb_d = dt_in("ln_b", [DEPTH, D])
    identb_d = dt_in("ident_bf", [128, 128], BF16)
    identf_d = dt_in("ident_f", [128, 128])
    ropec_d = dt_in("rope_cos", [128, NT, 32])
    ropes_d = dt_in("rope_sin", [128, NT, 32])
    dft_d = dt_in("dft_cs", [S, 2, S], BF16)
    csbd_d = dt_in("cs64bd", [128, 2, 2, 128], BF16)
    y_d = nc.dram_tensor("y", [BPC, S, D], F32, kind="ExternalOutput").ap()

    with contextlib.ExitStack() as st:
        fw = FW(nc, st)
        pe, act, dve, pool = fw.pe, fw.act, fw.dve, fw.pool
        V, A, P, T = nc.vector, nc.scalar, nc.gpsimd, nc.tensor

        hT = fw.sbuf("hT", [128, KC, S], BF16)
        yT = fw.sbuf("yT", [128, 8, S], BF16)
        QT = fw.sbuf("QT", [128, 4, S], BF16)
        KTp = fw.sbuf("KTp", [128, 2, S], BF16)
        Vaug = fw.sbuf("Vaug", [128, NT, 2, 128], BF16)
        NPT = 4
        PT = fw.sbuf("PT", [128, NPT, 2, 512], BF16)
        fuT = PT[:, :, :, :].rearrange("p a b c -> p (a b c)").rearrange("p (m s) -> p m s", m=2)
        wbf = fw.sbuf("wbf", [128, 2, KC, 512], BF16)
        NSTG = 3
        stage = fw.sbuf("stage", [128, NSTG, 1024], F32)
        NXIN = 3
        xin = fw.sbuf("xin", [128, NXIN, D], F32)
        NXR = 3
        xres = fw.sbuf("xres", [128, NXR, D], F32)
        lngb = fw.sbuf("lngb", [128, D], F32)
        lnbb = fw.sbuf("lnbb", [128, D], F32)
        gateb = fw.sbuf("gateb", [128, D], F32)
        ropec = fw.sbuf("ropec", [128, NT, 32], F32)
        ropes = fw.sbuf("ropes", [128, NT, 32], F32)
        zt = xin[:, :, :].rearrange("p a d -> p (a d)")[:, 0:S + 2]
        xh = fw.sbuf("xh", [128, 2, D], BF16)
        NWK = 8
        wk = fw.sbuf("wk", [128, NWK, 512], F32)
        qrb = fw.sbuf("qrb", [128, 2, 512], BF16)
        identb = fw.sbuf("identb", [128, 128], BF16)
        identf = fw.sbuf("identf", [128, 128], F32)
        onesb = fw.sbuf("onesb", [128, 128], BF16)
        dg = fw.sbuf("dg", [128, 2, 128], F32)
        dgb = fw.sbuf("dgb", [128, 2, 2, 128], BF16)
        csbd = fw.sbuf("csbd", [128, 2, 2, 128], BF16)
        fm = fw.sbuf("fm", [128, 2, 64], F32)
        fmb = fw.sbuf("fmb", [128, 2, 2, 64], BF16)
        cTb = fw.sbuf("cTb", [128, 2, KC, BPC], BF16)
        whi = fw.sbuf("whi", [128, 2, 1024], BF16)
        ABbd = fw.sbuf("ABbd", [128, 2, 2, 128], BF16)
        cTs = fw.sbuf("cTs", [128, KC, BPC], F32)
        badaT = fw.sbuf("badaT", [128, DEPTH, 24], F32)
        modT = fw.sbuf("modT", [128, 24, BPC], F32)
        s1T = fw.sbuf("s1T", [128, 8, BPC], F32)
        g1T = fw.sbuf("g1T", [128, 8, BPC], F32)
        cwT = fw.sbuf("cwT", [128, DEPTH, 2, 3], F32)
        qgb = fw.sbuf("qgb", [128, 64], F32)
        kgb = fw.sbuf("kgb", [128, 64], F32)
        mhalf = fw.sbuf("mhalf", [128, 8], F32)
        stt = fw.sbuf("stt", [128, 4, 2, 6], F32)
        mv = fw.sbuf("mv", [128, 4, 2], F32)
        sml = fw.sbuf("sml", [128, 4, 32], F32)

        pacc = fw.psum("pacc", [128, 8, 512], F32)
        ptrv = [pacc[:, 6, :].bitcast(BF16), pacc[:, 7, :].bitcast(BF16)]

        B = lambda n: Buf(n)
        hT_b = [B("hT%d" % i) for i in range(NT)]
        NTAB = 6
        tab_b = [B("tab%d" % i) for i in range(NTAB)]
        yT_b = [[B("yT%d_%d" % (c, t)) for t in range(4)] for c in range(8)]
        QT_b = [B("QT%d" % j) for j in range(4)]
        UAB_b = B("UAB")
        KT_b = B("KTp")
        Va_b = B("Vaug")
        PT_b = [B("PT%d" % i) for i in range(NPT)]
        fuT_b = B("fuT")
        wbf_b = [B("wbf0"), B("wbf1")]
        stg_b = [B("stg%d" % i) for i in range(NSTG)]
        xin_b = [B("xin%d" % i) for i in range(NXIN)]
        xres_b = [B("xres%d" % i) for i in range(NXR)]
        lng_b, lnb_b, gate_b = B("lng"), B("lnb"), B("gate")
        rope_b, z_b = B("rope"), B("z")
        xh_b = [B("xh0"), B("xh1")]
        wk_b = [B("wk%d" % i) for i in range(NWK)]
        qrb_b = [B("qrb0"), B("qrb1")]
        cd_b, cm_b = B("constd"), B("constm")
        dg_b = [B("dg0"), B("dg1")]
        fm_b, AB_b = B("fm"), B("ABbd")
        whi_b = [B("whi0"), B("whi1")]
        cTb_b = B("cTb")
        mod_b, s1_b, g1_b = B("modT"), B("s1T"), B("g1T")
        gain_b = B("gains")
        stt_b = [B("stt%d" % i) for i in range(4)]
        mv_b = [B("mv%d" % i) for i in range(4)]
        sml_b = [B("sml%d" % i) for i in range(4)]
        pacc_b = [B("pacc%d" % i) for i in range(8)]
        ptr_b = [pacc_b[6], pacc_b[7]]
        ydram_b = [[B("yd%d_%d" % (b, i)) for i in range(NT)] for b in range(BPC)]

        s_const = fw.sem("d_const")
        s_stg = [fw.sem("d_stg%d" % i) for i in range(NSTG)]
        s_xin = [fw.sem("d_xin%d" % i) for i in range(NXIN)]
        s_xres = [fw.sem("d_xres%d" % i) for i in range(NXR)]
        s_tab = [fw.sem("d_tab%d" % i) for i in range(NTAB)]
        s_lng, s_lnb, s_gain, s_fm = fw.sem("d_lng"), fw.sem("d_lnb"), fw.sem("d_gain"), fw.sem("d_fm")

        tabv = hT[:, 0:NTAB, :].rearrange("p a (c k) -> p a c k", c=2)
        UAB = QT[:, :, :].rearrange("p j (a m c n) -> p (j a) m c n", a=4, m=2, c=2)

        for (dst, src) in [(identb, identb_d), (identf, identf_d), (ropec, ropec_d), (ropes, ropes_d),
                           (csbd, csbd_d), (cTs, cT_d), (badaT, badaT_d), (cwT, cwT_d)]:
            fw.dma(s_const, dst[:], src, writes=[cd_b, rope_b])
        fw.op(pool, lambda: P.memset(Vaug[:], 1.0), writes=[Va_b])
        fw.op(pool, lambda: P.memset(KTp[:], 0.0), writes=[KT_b])
        fw.op(pool, lambda: P.memset(mhalf[:], -0.5), writes=[cm_b])
        fw.op(pool, lambda: P.memset(onesb[:], 1.0), writes=[cm_b])
        fw.op(dve, lambda: V.tensor_copy(out=cTb[:, 0, :, :], in_=cTs[:]), reads=[cd_b], writes=[cTb_b])
        fw.op(dve, lambda: V.tensor_tensor(out=cTb[:, 1, :, :], in0=cTs[:], in1=cTb[:, 0, :, :], op=ALU.subtract), reads=[cd_b, cTb_b], writes=[cTb_b])
        fw.op(pool, lambda: P.memset(ABbd[:], 0.0), writes=[AB_b])

        rr = {"stg": 0, "xin": 0, "xres": 0, "tab": 0, "wk": 0, "whi": 0, "st": 0, "pa": 0, "ptr": 0, "pt": 0, "xh": 0, "qrb": 0, "dg": 0}

        def nxt(key, n):
            v = rr[key]
            rr[key] = (v + 1) % n
            return v

        def rstd_from(var_ap, out_ap, slot_b, scale, eps):
            n = var_ap.shape[-1]
            fw.op(dve, lambda: V.tensor_scalar(out=out_ap, in0=var_ap, scalar1=scale, scalar2=eps,
                                               op0=ALU.mult, op1=ALU.add), reads=[slot_b], writes=[slot_b])
            fw.op(pool, lambda: P.tensor_tensor(out=out_ap, in0=out_ap, in1=mhalf[:, 0:n], op=ALU.pow),
                  reads=[slot_b, cd_b, cm_b], writes=[slot_b])

        def emit_mod(l):
            pa = 5
            for oc in range(24):
                sl = nxt("stg", NSTG)
                fw.dma(s_stg[sl], stage[:, sl, :].rearrange("p (k c) -> p k c", k=KC),
                       wada_d[l, :, oc * 128:(oc + 1) * 128].rearrange("(k p) c -> p k c", p=128),
                       writes=[stg_b[sl]])
                ws = nxt("whi", 2)
                if oc % 3 == 0:
                    fw.op(dve, lambda sl=sl, ws=ws: V.tensor_copy(out=whi[:, ws, :], in_=stage[:, sl, :]), reads=[stg_b[sl]], writes=[whi_b[ws]])
                elif oc % 3 == 1:
                    fw.op(act, lambda sl=sl, ws=ws: A.copy(out=whi[:, ws, :], in_=stage[:, sl, :]), reads=[stg_b[sl]], writes=[whi_b[ws]])
                else:
                    fw.op(pool, lambda sl=sl, ws=ws: P.tensor_copy(out=whi[:, ws, :], in_=stage[:, sl, :]), reads=[stg_b[sl]], writes=[whi_b[ws]])
                for kc in range(KC):
                    for hl in range(2):
                        fw.op(pe, lambda kc=kc, ws=ws, oc=oc, hl=hl: T.matmul(
                            pacc[:, pa, oc * 2:oc * 2 + 2], lhsT=whi[:, ws, kc * 128:(kc + 1) * 128],
                            rhs=cTb[:, hl, kc, :], start=(kc == 0 and hl == 0), stop=(kc == KC - 1 and hl == 1)),
                            reads=[whi_b[ws], cTb_b], writes=[pacc_b[pa]], sig=(kc == KC - 1 and hl == 1))
            fw.op(dve, lambda: V.tensor_tensor(
                out=modT[:], in0=pacc[:, pa, 0:48].rearrange("p (o b) -> p o b", b=BPC),
                in1=badaT[:, l, :].unsqueeze(2).to_broadcast([128, 24, BPC]), op=ALU.add),
                reads=[pacc_b[pa], cd_b, cm_b], writes=[mod_b])
            fw.op(dve, lambda: V.tensor_scalar(out=s1T[:], in0=modT[:, 8:16, :], scalar1=1.0, scalar2=None, op0=ALU.add),
                  reads=[mod_b], writes=[s1_b])
            fw.op(dve, lambda: V.tensor_scalar(out=g1T[:], in0=modT[:, 16:24, :], scalar1=1.0, scalar2=None, op0=ALU.add),
                  reads=[mod_b], writes=[g1_b])
            fw.dma(s_gain, qgb[:], qg_d[l].partition_broadcast(128), writes=[gain_b])
            fw.dma(s_gain, kgb[:], kg_d[l].partition_broadcast(128), writes=[gain_b])
            fw.dma(s_fm, fm[:], fmix_d[l].rearrange("(a g) m d -> (g m) a d", a=2), writes=[fm_b])
            fw.op(dve, lambda: V.tensor_copy(out=fmb[:, :, 0, :], in_=fm[:]), reads=[fm_b], writes=[fm_b])
            fw.op(dve, lambda: V.tensor_tensor(out=fmb[:, :, 1, :], in0=fm[:], in1=fmb[:, :, 0, :], op=ALU.subtract), reads=[fm_b], writes=[fm_b])
            for m2 in range(2):
                for ab in range(2):
                    for t3, (th, fh) in enumerate(((0, 0), (0, 1), (1, 0))):
                        fw.op(pe, lambda m2=m2, ab=ab, th=th, fh=fh, t3=t3: T.matmul(
                            pacc[:, pa, 64 + (m2 * 2 + ab) * 64: 64 + (m2 * 2 + ab + 1) * 64], lhsT=csbd[:, ab, th, :], rhs=fmb[:, m2, fh, :],
                            start=(t3 == 0), stop=(t3 == 2)), reads=[fm_b, cd_b, cm_b], writes=[pacc_b[pa]], sig=(t3 == 2))
            for m2 in range(2):
                for ab in range(2):
                    c0 = 64 + (m2 * 2 + ab) * 64
                    fw.op(dve, lambda m2=m2, ab=ab, c0=c0: V.tensor_copy(out=ABbd[0:64, m2, ab, 0:64], in_=pacc[0:64, pa, c0:c0 + 64]),
                          reads=[pacc_b[pa]], writes=[AB_b])
                    fw.op(dve, lambda m2=m2, ab=ab, c0=c0: V.tensor_copy(out=ABbd[64:128, m2, ab, 64:128], in_=pacc[64:128, pa, c0:c0 + 64]),
                          reads=[pacc_b[pa]], writes=[AB_b])

        def load_wgroup(l, slot, src0, ncol_src, casts):
            for kc in range(KC):
                sl = nxt("stg", NSTG)
                fw.dma(s_stg[sl], stage[:, sl, 0:ncol_src], win_d[l, kc * 128:(kc + 1) * 128, src0:src0 + ncol_src],
                       writes=[stg_b[sl]])
                for cf in casts:
                    o_ap, i_ap = cf(wbf[:, slot, kc, :], stage[:, sl, :])
                    fw.op(pool, lambda o_ap=o_ap, i_ap=i_ap: P.tensor_copy(out=o_ap, in_=i_ap),
                          reads=[stg_b[sl]], writes=[wbf_b[slot]])

        perm_cast = lambda o, i: (o[:, 0:512].rearrange("p (j h d) -> p j h d", j=4, h=2),
                                  i[:, 0:512].rearrange("p (h j d) -> p j h d", h=2, j=4))

        def rope(src, dst, i, H, sb, wkA, wkB, src_bufs, dst_bufs):
            sv = src.rearrange("p (h a b d) -> p h a b d", h=H, a=2, b=2)
            dv = dst.rearrange("p (h a b d) -> p h a b d", h=H, a=2, b=2)
            cs = ropec[:, i, :].rearrange("p (a d) -> p a d", a=2).unsqueeze(1).to_broadcast([128, H, 2, 16])
            sn = ropes[:, i, :].rearrange("p (a d) -> p a d", a=2).unsqueeze(1).to_broadcast([128, H, 2, 16])
            n = H * 32
            t1 = wkA[:, 0:n].rearrange("p (h a d) -> p h a d", h=H, a=2)
            t2 = wkA[:, 256:256 + n].rearrange("p (h a d) -> p h a d", h=H, a=2)
            t3 = wkB[:, 0:n].rearrange("p (h a d) -> p h a d", h=H, a=2)
            t4 = wkB[:, 256:256 + n].rearrange("p (h a d) -> p h a d", h=H, a=2)
            x0 = sv[:, :, :, 0, :]
            x1 = sv[:, :, :, 1, :]
            wa, wb = sb
            fw.op(dve, lambda: V.tensor_tensor(out=t1, in0=x0, in1=cs, op=ALU.mult), reads=src_bufs + [rope_b], writes=[wa])
            fw.op(dve, lambda: V.tensor_tensor(out=t2, in0=x1, in1=sn, op=ALU.mult), reads=src_bufs + [rope_b], writes=[wa])
            fw.op(dve, lambda: V.tensor_tensor(out=t3, in0=x1, in1=cs, op=ALU.mult), reads=src_bufs + [rope_b], writes=[wb])
            fw.op(dve, lambda: V.tensor_tensor(out=t4, in0=x0, in1=sn, op=ALU.mult), reads=src_bufs + [rope_b], writes=[wb])
            fw.op(dve, lambda: V.tensor_tensor(out=dv[:, :, :, 0, :], in0=t1, in1=t2, op=ALU.subtract), reads=[wa], writes=dst_bufs)
            fw.op(dve, lambda: V.tensor_tensor(out=dv[:, :, :, 1, :], in0=t3, in1=t4, op=ALU.add), reads=[wb], writes=dst_bufs)

        def emit_layer(b, l, x_src, last):
            for kc in range(KC):
                ds = nxt("dg", 2)
                pa = 4 + (kc // 4)
                fw.op(dve, lambda kc=kc, ds=ds: V.tensor_scalar(out=dg[:, ds, :], in0=identf[:], scalar1=g1T[:, kc, b:b + 1],
                                                              scalar2=None, op0=ALU.mult),
                      reads=[cd_b, cm_b, g1_b], writes=[dg_b[ds]])
                fw.op(dve, lambda ds=ds: V.tensor_copy(out=dgb[:, ds, 0, :], in_=dg[:, ds, :]), reads=[dg_b[ds]], writes=[dg_b[ds]])
                fw.op(dve, lambda ds=ds: V.tensor_tensor(out=dgb[:, ds, 1, :], in0=dg[:, ds, :], in1=dgb[:, ds, 0, :], op=ALU.subtract), reads=[dg_b[ds]], writes=[dg_b[ds]])
                for hl in range(2):
                    fw.op(pe, lambda kc=kc, ds=ds, pa=pa, hl=hl: T.matmul(pacc[:, pa, (kc % 4) * 128:(kc % 4 + 1) * 128], lhsT=onesb[:], rhs=dgb[:, ds, hl, :],
                                                                        start=(hl == 0), stop=(hl == 1)),
                          reads=[dg_b[ds], cd_b, cm_b], writes=[pacc_b[pa]], sig=(hl == 1))
            for h2 in range(2):
                fw.op(act, lambda h2=h2: A.copy(out=gateb[:, h2 * 512:(h2 + 1) * 512], in_=pacc[:, 4 + h2, :]),
                      reads=[pacc_b[4 + h2]], writes=[gate_b])

            if stop <= 1:
                return
            load_wgroup(l, 0, 512, 256, [lambda o, i: (o[:, 0:256], i[:, 0:256])])
            load_wgroup(l, 1, 0, 512, [perm_cast])

            if stop <= 1.5:
                return
            def ln1_a(i):
                xs_ = nxt("xin", NXIN)
                fw.dma(s_xin[xs_], xin[:, xs_, :], x_src[b, i * 128:(i + 1) * 128, :],
                       reads=[ydram_b[b][i]], writes=[xin_b[xs_]] + ([z_b] if i < NXIN else []))
                ss = nxt("st", 4)
                for h2 in range(2):
                    fw.op(dve, lambda h2=h2: V.bn_stats(out=stt[:, ss, h2, :], in_=xin[:, xs_, h2 * 512:(h2 + 1) * 512]),
                          reads=[xin_b[xs_]], writes=[stt_b[ss]])
                fw.op(dve, lambda: V.bn_aggr(out=mv[:, ss, :], in_=stt[:, ss, :, :].rearrange("p a b -> p (a b)")), reads=[stt_b[ss]], writes=[mv_b[ss]])
                fw.op(dve, lambda: V.tensor_scalar(out=sml[:, ss, 0:1], in0=mv[:, ss, 1:2], scalar1=1.0, scalar2=LN_EPS,
                                                   op0=ALU.mult, op1=ALU.add), reads=[mv_b[ss]], writes=[sml_b[ss]])
                fw.op(pool, lambda: P.tensor_tensor(out=sml[:, ss, 0:1], in0=sml[:, ss, 0:1], in1=mhalf[:, 0:1], op=ALU.pow),
                      reads=[sml_b[ss], cd_b, cm_b], writes=[sml_b[ss]])
                return xs_, ss

            def ln1_b(i, xs_, ss):
                hs = nxt("xh", 2)
                fw.op(dve, lambda: V.tensor_scalar(
                    out=xh[:, hs, :], in0=xin[:, xs_, :], scalar1=mv[:, ss, 0:1], scalar2=sml[:, ss, 0:1],
                    op0=ALU.subtract, op1=ALU.mult), reads=[xin_b[xs_], mv_b[ss], sml_b[ss]], writes=[xh_b[hs]])
                pb = nxt("ptr", 2)
                for kc in range(KC):
                    fw.op(pe, lambda kc=kc: T.transpose(ptrv[pb][:, kc * 128:(kc + 1) * 128], xh[:, hs, kc * 128:(kc + 1) * 128], identb[:]),
                          reads=[xh_b[hs], cd_b, cm_b], writes=[ptr_b[pb]], sig=(kc == KC - 1))
                for kc in range(KC):
                    if True:
                        fw.op(act, lambda kc=kc: A.activation(
                            out=hT[:, kc, i * 128:(i + 1) * 128], in_=ptrv[pb][:, kc * 128:(kc + 1) * 128], func=ACTF.Identity,
                            scale=s1T[:, kc, b:b + 1], bias=modT[:, kc, b:b + 1]),
                            reads=[ptr_b[pb], s1_b, mod_b], writes=[hT_b[i]] + (tab_b if kc == 0 else []))
                    else:
                        fw.op(dve, lambda kc=kc: V.tensor_scalar(
                            out=hT[:, kc, i * 128:(i + 1) * 128], in0=ptrv[pb][:, kc * 128:(kc + 1) * 128],
                            scalar1=s1T[:, kc, b:b + 1], scalar2=modT[:, kc, b:b + 1], op0=ALU.mult, op1=ALU.add),
                            reads=[ptr_b[pb], s1_b, mod_b], writes=[hT_b[i]])

            pend = ln1_a(0)
            for i in range(NT):
                nxt_pend = ln1_a(i + 1) if i + 1 < NT else None
                ln1_b(i, *pend)
                pend = nxt_pend

            if stop <= 2:
                return
            def qk_post_a(pa, H, i, gain_t):
                n = H * 64
                wA = nxt("wk", NWK); wB = nxt("wk", NWK); wC = nxt("wk", NWK)
                ss = nxt("st", 4)
                src = pacc[:, pa, 0:n]
                fw.op(act, lambda: A.activation(out=wk[:, wA, 0:n], in_=src, func=ACTF.Square), reads=[pacc_b[pa]], writes=[wk_b[wA]])
                fw.op(dve, lambda: V.tensor_reduce(out=sml[:, ss, 0:H], in_=wk[:, wA, 0:n].rearrange("p (h d) -> p h d", h=H),
                                                   axis=AX.X, op=ALU.add), reads=[wk_b[wA]], writes=[sml_b[ss]])
                rstd_from(sml[:, ss, 0:H], sml[:, ss, 0:H], sml_b[ss], 1.0 / 64, RMS_EPS)
                fw.op(dve, lambda: V.tensor_tensor(out=wk[:, wB, 0:n].rearrange("p (h d) -> p h d", h=H), in0=src.rearrange("p (h d) -> p h d", h=H),
                                                   in1=gain_t[:].unsqueeze(1).to_broadcast([128, H, 64]), op=ALU.mult),
                      reads=[pacc_b[pa], gain_b], writes=[wk_b[wB]])
                rope(wk[:, wB, 0:n], wk[:, wB, 0:n], i, H, (wk_b[wC], wk_b[wA]), wk[:, wC, :], wk[:, wA, :], [wk_b[wB]], [wk_b[wB]])
                return wB, ss

            def qk_post_b(wB, ss, H, qs):
                n = H * 64
                fw.op(pool, lambda: P.tensor_tensor(out=qrb[:, qs, 0:n].rearrange("p (h d) -> p h d", h=H), in0=wk[:, wB, 0:n].rearrange("p (h d) -> p h d", h=H),
                                                    in1=sml[:, ss, 0:H].unsqueeze(2).to_broadcast([128, H, 64]), op=ALU.mult),
                      reads=[wk_b[wB], sml_b[ss]], writes=[qrb_b[qs]])

            def kv_a(i):
                pa = nxt("pa", 4)
                for kc in range(KC):
                    fw.op(pe, lambda kc=kc: T.matmul(pacc[:, pa, 0:256], lhsT=hT[:, kc, i * 128:(i + 1) * 128], rhs=wbf[:, 0, kc, 0:256],
                                                     start=(kc == 0), stop=(kc == KC - 1)),
                          reads=[hT_b[i], wbf_b[0]], writes=[pacc_b[pa]], sig=(kc == KC - 1))
                fw.op(act, lambda: A.copy(out=Vaug[:, i, 0, 0:64], in_=pacc[:, pa, 128:192]), reads=[pacc_b[pa]], writes=[Va_b])
                fw.op(act, lambda: A.copy(out=Vaug[:, i, 1, 64:128], in_=pacc[:, pa, 192:256]), reads=[pacc_b[pa]], writes=[Va_b])
                return qk_post_a(pa, 2, i, kgb)

            def kv_b(i, wB, ss):
                qs = nxt("qrb", 2)
                qk_post_b(wB, ss, 2, qs)
                pb = nxt("ptr", 2)
                fw.op(pe, lambda: T.transpose(ptrv[pb][:, 0:128], qrb[:, qs, 0:128], identb[:]),
                      reads=[qrb_b[qs], cd_b, cm_b], writes=[ptr_b[pb]])
                fw.op(act, lambda: A.copy(out=KTp[0:64, 0, i * 128:(i + 1) * 128], in_=ptrv[pb][0:64, 0:128]), reads=[ptr_b[pb]], writes=[KT_b])
                fw.op(act, lambda: A.copy(out=KTp[64:128, 1, i * 128:(i + 1) * 128], in_=ptrv[pb][64:128, 0:128]), reads=[ptr_b[pb]], writes=[KT_b])

            pend = kv_a(0)
            for i in range(NT):
                nxt_pend = kv_a(i + 1) if i + 1 < NT else None
                kv_b(i, *pend)
                pend = nxt_pend
            if stop <= 3:
                return
            load_wgroup(l, 0, 768, 512, [perm_cast])

            def q_a(i):
                pa = nxt("pa", 4)
                for kc in range(KC):
                    fw.op(pe, lambda kc=kc: T.matmul(pacc[:, pa, :], lhsT=hT[:, kc, i * 128:(i + 1) * 128], rhs=wbf[:, 1, kc, :],
                                                     start=(kc == 0), stop=(kc == KC - 1)),
                          reads=[hT_b[i], wbf_b[1]], writes=[pacc_b[pa]], sig=(kc == KC - 1))
                return qk_post_a(pa, 8, i, qgb)

            def q_b(i, wB, ss):
                qs = nxt("qrb", 2)
                qk_post_b(wB, ss, 8, qs)
                pb = nxt("ptr", 2)
                for j in range(4):
                    fw.op(pe, lambda j=j: T.transpose(ptrv[pb][:, j * 128:(j + 1) * 128], qrb[:, qs, j * 128:(j + 1) * 128], identb[:]),
                          reads=[qrb_b[qs], cd_b, cm_b], writes=[ptr_b[pb]], sig=(j == 3))
                fw.op(act, lambda: A.copy(out=QT[:, :, i * 128:(i + 1) * 128], in_=ptrv[pb][:, 0:512].rearrange("p (j t) -> p j t", j=4)),
                      reads=[ptr_b[pb]], writes=QT_b + [UAB_b])

            pend = q_a(0)
            for i in range(NT):
                nxt_pend = q_a(i + 1) if i + 1 < NT else None
                q_b(i, *pend)
                pend = nxt_pend
            if stop <= 4:
                return
            for j in range(4):
                for tc in range(4):
                    pa = nxt("pa", 4)
                    for kc in range(KC):
                        fw.op(pe, lambda kc=kc, pa=pa, j=j, tc=tc: T.matmul(
                            pacc[:, pa, :], lhsT=wbf[:, 0, kc, j * 128:(j + 1) * 128], rhs=hT[:, kc, tc * 512:(tc + 1) * 512],
                            start=(kc == 0), stop=(kc == KC - 1)),
                            reads=hT_b[tc * 4:(tc + 1) * 4] + [wbf_b[0]], writes=[pacc_b[pa]], sig=(kc == KC - 1))
                    fw.op(act, lambda pa=pa, j=j, tc=tc: A.activation(out=yT[:, j, tc * 512:(tc + 1) * 512], in_=pacc[:, pa, :], func=ACTF.Silu),
                          reads=[pacc_b[pa]], writes=[yT_b[j][tc]])

            if stop <= 5:
                return
            conv_cast = lambda m: (lambda o, i: (o[:, 0:512].rearrange("p (g c) -> p g c", g=4),
                                                 i[:, 0:1024].rearrange("p (g m c) -> p g m c", g=4, m=2)[:, :, m, :]))
            load_wgroup(l, 1, 1280, 1024, [conv_cast(0)])
            load_wgroup(l, 0, 1280, 1024, [conv_cast(1)])

            if stop <= 6:
                return
            for j in range(4):
                for qc in range(4):
                    steps = []

                    def qk(kt, sb_):
                        for kv in range(2):
                            fw.op(pe, lambda kv=kv, kt=kt, sb_=sb_: T.matmul(
                                pacc[:, SBASE[sb_] + kv, :], lhsT=KTp[:, kv, kt * 128:(kt + 1) * 128], rhs=QT[:, j, qc * 512:(qc + 1) * 512],
                                start=True, stop=True), reads=[KT_b, QT_b[j]], writes=[pacc_b[SBASE[sb_] + kv]])

                    def ex(kt, sb_, ps):
                        fw.op(act, lambda sb_=sb_, ps=ps: A.activation(out=PT[:, ps, :, :], in_=pacc[:, SBASE[sb_]:SBASE[sb_] + 2, :], func=ACTF.Exp, scale=0.125),
                              reads=[pacc_b[SBASE[sb_]], pacc_b[SBASE[sb_] + 1]], writes=[PT_b[ps], fuT_b])

                    def pv(kt, ps):
                        for kv in range(2):
                            fw.op(pe, lambda kv=kv, kt=kt, ps=ps: T.matmul(
                                pacc[:, 4 + kv, :], lhsT=Vaug[:, kt, kv, :], rhs=PT[:, ps, kv, :], start=(kt == 0), stop=(kt == NT - 1)),
                                reads=[Va_b, PT_b[ps]], writes=[pacc_b[4 + kv]], sig=(kt == NT - 1))

                    SBASE = [0, 2, 6]
                    qk(0, 0)
                    qk(1, 1)
                    for kt in range(NT):
                        ps = nxt("pt", NPT)
                        ex(kt, kt % 3, ps)
                        if kt + 2 < NT:
                            qk(kt + 2, (kt + 2) % 3)
                        pv(kt, ps)
                    w0 = nxt("wk", NWK); w1 = nxt("wk", NWK); w2 = nxt("wk", NWK)
                    cols = slice(qc * 512, (qc + 1) * 512)
                    fw.op(dve, lambda w0=w0: V.tensor_copy(out=wk[:, w0, :], in_=pacc[:, 4, :]), reads=[pacc_b[4]], writes=[wk_b[w0]])
                    fw.op(dve, lambda w1=w1: V.tensor_copy(out=wk[:, w1, :], in_=pacc[:, 5, :]), reads=[pacc_b[5]], writes=[wk_b[w1]])
                    fw.op(dve, lambda w0=w0, w2=w2: V.reciprocal(out=wk[0:64, w2, :], in_=wk[64:128, w0, :]), reads=[wk_b[w0]], writes=[wk_b[w2]])
                    fw.op(dve, lambda w0=w0, w2=w2: V.tensor_tensor(out=wk[0:64, w0, :], in0=wk[0:64, w0, :], in1=wk[0:64, w2, :], op=ALU.mult),
                          reads=[wk_b[w0], wk_b[w2]], writes=[wk_b[w0]])
                    fw.op(pool, lambda w0=w0: P.tensor_tensor(out=yT[0:64, j, cols], in0=yT[0:64, j, cols], in1=wk[0:64, w0, :], op=ALU.mult),
                          reads=[wk_b[w0], yT_b[j][qc]], writes=[yT_b[j][qc]])
                    fw.op(dve, lambda w1=w1, w2=w2: V.reciprocal(out=wk[64:128, w2, :], in_=wk[0:64, w1, :]), reads=[wk_b[w1]], writes=[wk_b[w2]])
                    fw.op(dve, lambda w1=w1, w2=w2: V.tensor_tensor(out=wk[64:128, w1, :], in0=wk[64:128, w1, :], in1=wk[64:128, w2, :], op=ALU.mult),
                          reads=[wk_b[w1], wk_b[w2]], writes=[wk_b[w1]])
                    fw.op(pool, lambda w1=w1: P.tensor_tensor(out=yT[64:128, j, cols], in0=yT[64:128, j, cols], in1=wk[64:128, w1, :], op=ALU.mult),
                          reads=[wk_b[w1], yT_b[j][qc]], writes=[yT_b[j][qc]])

            if stop <= 7:
                return
            fw.op(pool, lambda: P.memset(zt[:, 0:1], 0.0), writes=[z_b] + xin_b)
            fw.op(pool, lambda: P.memset(zt[:, S + 1:S + 2], 0.0), writes=[z_b] + xin_b)
            for m in range(2):
                slot = 1 - m
                for tc in range(4):
                    pas = [nxt("pa", 4) for _ in range(4)]
                    for g in range(4):
                        for kc in range(KC):
                            fw.op(pe, lambda kc=kc, g=g, tc=tc, slot=slot, pas=pas: T.matmul(
                                pacc[:, pas[g], :], lhsT=wbf[:, slot, kc, g * 128:(g + 1) * 128], rhs=hT[:, kc, tc * 512:(tc + 1) * 512],
                                start=(kc == 0), stop=(kc == KC - 1)),
                                reads=hT_b[tc * 4:(tc + 1) * 4] + [wbf_b[slot]], writes=[pacc_b[pas[g]]], sig=(kc == KC - 1))
                    w0 = nxt("wk", NWK); w1 = nxt("wk", NWK)
                    fw.op(act, lambda w0=w0, pas=pas: A.copy(out=wk[:, w0, :], in_=pacc[:, pas[2], :]), reads=[pacc_b[pas[2]]], writes=[wk_b[w0]])
                    fw.op(dve, lambda w0=w0, pas=pas, tc=tc: V.tensor_tensor(out=zt[:, 1 + tc * 512:1 + (tc + 1) * 512], in0=pacc[:, pas[0], :], in1=wk[:, w0, :], op=ALU.mult),
                          reads=[pacc_b[pas[0]], wk_b[w0]], writes=[z_b])
                    fw.op(act, lambda w1=w1, pas=pas: A.activation(out=wk[:, w1, :], in_=pacc[:, pas[3], :], func=ACTF.Silu), reads=[pacc_b[pas[3]]], writes=[wk_b[w1]])
                    fw.op(dve, lambda w1=w1, pas=pas, tc=tc, m=m: V.tensor_tensor(out=yT[:, 4 + m, tc * 512:(tc + 1) * 512], in0=pacc[:, pas[1], :], in1=wk[:, w1, :], op=ALU.mult),
                          reads=[pacc_b[pas[1]], wk_b[w1]], writes=[yT_b[4 + m][tc]])
                for tc in range(4):
                    w0 = nxt("wk", NWK)
                    c0 = 1 + tc * 512
                    fw.op(dve, lambda w0=w0, c0=c0, m=m: V.tensor_scalar(out=wk[:, w0, :], in0=zt[:, c0:c0 + 512], scalar1=cwT[:, l, m, 1:2], scalar2=None, op0=ALU.mult),
                          reads=[z_b, cd_b, cm_b], writes=[wk_b[w0]])
                    fw.op(dve, lambda w0=w0, c0=c0, m=m: V.scalar_tensor_tensor(out=wk[:, w0, :], in0=zt[:, c0 - 1:c0 + 511], scalar=cwT[:, l, m, 0:1], in1=wk[:, w0, :],
                                                                           op0=ALU.mult, op1=ALU.add), reads=[z_b, cd_b, cm_b, wk_b[w0]], writes=[wk_b[w0]])
                    fw.op(dve, lambda w0=w0, c0=c0, m=m: V.scalar_tensor_tensor(out=wk[:, w0, :], in0=zt[:, c0 + 1:c0 + 513], scalar=cwT[:, l, m, 2:3], in1=wk[:, w0, :],
                                                                           op0=ALU.mult, op1=ALU.add), reads=[z_b, cd_b, cm_b, wk_b[w0]], writes=[wk_b[w0]])
                    fw.op(pool, lambda w0=w0, tc=tc, m=m: P.tensor_tensor(out=yT[:, 4 + m, tc * 512:(tc + 1) * 512], in0=yT[:, 4 + m, tc * 512:(tc + 1) * 512], in1=wk[:, w0, :], op=ALU.mult),
                          reads=[wk_b[w0], yT_b[4 + m][tc]], writes=[yT_b[4 + m][tc]])
                if m == 0:
                    load_wgroup(l, 1, 2304, 512, [lambda o, i: (o[:, 0:512], i[:, 0:512])])

            if stop <= 8:
                return
            for m2 in range(2):
                for tc in range(4):
                    pa = nxt("pa", 4)
                    for kc in range(KC):
                        fw.op(pe, lambda kc=kc, pa=pa, m2=m2, tc=tc: T.matmul(
                            pacc[:, pa, :], lhsT=wbf[:, 1, kc, m2 * 128:(m2 + 1) * 128], rhs=hT[:, kc, tc * 512:(tc + 1) * 512],
                            start=(kc == 0), stop=(kc == KC - 1)),
                            reads=hT_b[tc * 4:(tc + 1) * 4] + [wbf_b[1]], writes=[pacc_b[pa]], sig=(kc == KC - 1))
                    fw.op(act, lambda pa=pa, m2=m2, tc=tc: A.copy(out=fuT[:, m2, tc * 512:(tc + 1) * 512], in_=pacc[:, pa, :]),
                          reads=[pacc_b[pa]], writes=[fuT_b] + PT_b)
                    pa = nxt("pa", 4)
                    for kc in range(KC):
                        fw.op(pe, lambda kc=kc, pa=pa, m2=m2, tc=tc: T.matmul(
                            pacc[:, pa, :], lhsT=wbf[:, 1, kc, 256 + m2 * 128:256 + (m2 + 1) * 128], rhs=hT[:, kc, tc * 512:(tc + 1) * 512],
                            start=(kc == 0), stop=(kc == KC - 1)),
                            reads=hT_b[tc * 4:(tc + 1) * 4] + [wbf_b[1]], writes=[pacc_b[pa]], sig=(kc == KC - 1))
                    fw.op(act, lambda pa=pa, m2=m2, tc=tc: A.activation(out=yT[:, 6 + m2, tc * 512:(tc + 1) * 512], in_=pacc[:, pa, :], func=ACTF.Silu),
                          reads=[pacc_b[pa]], writes=[yT_b[6 + m2][tc]])
            wall = wbf[:, :, :, :].rearrange("p s k c -> p (s k c)").rearrange("p (k c) -> p k c", k=KC)
            for kc in range(KC):
                sl = nxt("stg", NSTG)
                if kc < 4:
                    fw.dma(s_stg[sl], stage[0:64, sl, :], wout_d[l, kc * 64:(kc + 1) * 64, :], writes=[stg_b[sl]])
                    fw.dma(s_stg[sl], stage[64:128, sl, :], wout_d[l, (4 + kc) * 64:(5 + kc) * 64, :], writes=[stg_b[sl]])
                else:
                    fw.dma(s_stg[sl], stage[:, sl, :], wout_d[l, kc * 128:(kc + 1) * 128, :], writes=[stg_b[sl]])
                fw.op(pool, lambda kc=kc, sl=sl: P.tensor_tensor(out=wall[:, kc, :], in0=stage[:, sl, :], in1=gateb[:], op=ALU.mult),
                      reads=[stg_b[sl], gate_b], writes=wbf_b)
            for i in range(NT):
                pa = nxt("pa", 4)
                for m2 in range(2):
                    fw.op(pe, lambda m2=m2, pa=pa, i=i: T.matmul(
                        pacc[:, pa, m2 * 256:(m2 + 1) * 256], lhsT=fuT[:, m2, i * 128:(i + 1) * 128],
                        rhs=ABbd[:, m2, :, :].rearrange("p a n -> p (a n)"), start=True, stop=True),
                        reads=[fuT_b, AB_b], writes=[pacc_b[pa]], sig=(m2 == 1))
                fw.op(act, lambda pa=pa, i=i: A.copy(out=UAB[:, i, :, :, :].rearrange("p m c n -> p (m c n)"), in_=pacc[:, pa, :]),
                      reads=[pacc_b[pa]], writes=[UAB_b] + QT_b)
            for half in range(2):
                pas = [[nxt("pa", 4) for _ in range(2)] for _ in range(2)]
                for a in range(NT):
                    ts = nxt("tab", NTAB)
                    fw.dma(s_tab[ts], tabv[:, ts, :, :], dft_d[a * 128:(a + 1) * 128, :, half * 1024:(half + 1) * 1024], writes=[tab_b[ts]] + hT_b)
                    for m2 in range(2):
                        for kq in range(2):
                            for ab in range(2):
                                fw.op(pe, lambda m2=m2, kq=kq, ab=ab, a=a, ts=ts, pas=pas: T.matmul(
                                    pacc[:, pas[m2][kq], :], lhsT=UAB[:, a, m2, ab, :], rhs=tabv[:, ts, ab, kq * 512:(kq + 1) * 512],
                                    start=(a == 0 and ab == 0), stop=(a == NT - 1 and ab == 1)),
                                    reads=[UAB_b, tab_b[ts]], writes=[pacc_b[pas[m2][kq]]], sig=((a == NT - 1 and ab == 1) or (m2 == 1 and kq == 1 and ab == 1)))
                for m2 in range(2):
                    for kq in range(2):
                        tcc = half * 2 + kq
                        fw.op(dve, lambda m2=m2, kq=kq, tcc=tcc, pas=pas: V.tensor_tensor(
                            out=yT[:, 6 + m2, tcc * 512:(tcc + 1) * 512], in0=pacc[:, pas[m2][kq], :], in1=yT[:, 6 + m2, tcc * 512:(tcc + 1) * 512], op=ALU.mult),
                            reads=[pacc_b[pas[m2][kq]], yT_b[6 + m2][tcc]], writes=[yT_b[6 + m2][tcc]])

            if stop <= 9:
                return
            fw.dma(s_lng, lngb[:], lng_d[l].partition_broadcast(128), writes=[lng_b])
            fw.dma(s_lnb, lnbb[:], lnb_d[l].partition_broadcast(128), writes=[lnb_b])
            def fin_a(i):
                xr = nxt("xres", NXR)
                fw.dma(s_xres[xr], xres[:, xr, :], x_src[b, i * 128:(i + 1) * 128, :], reads=[ydram_b[b][i]], writes=[xres_b[xr]])
                for n in range(2):
                    for kc in range(KC):
                        fw.op(pe, lambda kc=kc, n=n: T.matmul(
                            pacc[:, 4 + n, :], lhsT=yT[:, kc, i * 128:(i + 1) * 128], rhs=wall[:, kc, n * 512:(n + 1) * 512],
                            start=(kc == 0), stop=(kc == KC - 1)),
                            reads=[yT_b[kc][i // 4]] + wbf_b, writes=[pacc_b[4 + n]], sig=(kc == KC - 1))
                ss = nxt("st", 4)
                for n in range(2):
                    fw.op(dve, lambda n=n: V.scalar_tensor_tensor(
                        out=xres[:, xr, n * 512:(n + 1) * 512], in0=xres[:, xr, n * 512:(n + 1) * 512], scalar=float(ALPHA),
                        in1=pacc[:, 4 + n, :], op0=ALU.mult, op1=ALU.add),
                        reads=[pacc_b[4 + n], xres_b[xr]], writes=[xres_b[xr]])
                for n in range(2):
                    fw.op(dve, lambda n=n: V.bn_stats(out=stt[:, ss, n, :], in_=xres[:, xr, n * 512:(n + 1) * 512]),
                          reads=[xres_b[xr]], writes=[stt_b[ss]])
                fw.op(dve, lambda: V.bn_aggr(out=mv[:, ss, :], in_=stt[:, ss, :, :].rearrange("p a b -> p (a b)")), reads=[stt_b[ss]], writes=[mv_b[ss]])
                fw.op(dve, lambda: V.tensor_scalar(out=sml[:, ss, 0:1], in0=mv[:, ss, 1:2], scalar1=1.0, scalar2=LN_EPS,
                                                   op0=ALU.mult, op1=ALU.add), reads=[mv_b[ss]], writes=[sml_b[ss]])
                fw.op(pool, lambda: P.tensor_tensor(out=sml[:, ss, 0:1], in0=sml[:, ss, 0:1], in1=mhalf[:, 0:1], op=ALU.pow),
                      reads=[sml_b[ss], cd_b, cm_b], writes=[sml_b[ss]])
                return xr, ss

            def fin_b(i, xr, ss):
                fw.op(dve, lambda: V.scalar_tensor_tensor(out=sml[:, ss, 1:2], in0=mv[:, ss, 0:1], scalar=-1.0, in1=sml[:, ss, 0:1],
                                                          op0=ALU.mult, op1=ALU.mult), reads=[mv_b[ss], sml_b[ss]], writes=[sml_b[ss]])
                fw.op(act, lambda: A.activation(out=xres[:, xr, :], in_=xres[:, xr, :], func=ACTF.Identity,
                                                scale=sml[:, ss, 0:1], bias=sml[:, ss, 1:2]),
                      reads=[xres_b[xr], sml_b[ss]], writes=[xres_b[xr]])
                fw.op(pool, lambda: P.tensor_tensor(out=xres[:, xr, :], in0=xres[:, xr, :], in1=lngb[:], op=ALU.mult),
                      reads=[xres_b[xr], lng_b], writes=[xres_b[xr]])

            def fin_c(i, xr, ss):
                fw.op(dve, lambda: V.tensor_tensor(out=xres[:, xr, :], in0=xres[:, xr, :], in1=lnbb[:], op=ALU.add),
                      reads=[xres_b[xr], lnb_b], writes=[xres_b[xr]])
                fw.dma(s_xres[xr], y_d[b, i * 128:(i + 1) * 128, :], xres[:, xr, :], reads=[xres_b[xr]], writes=[ydram_b[b][i]])

            st_ = {}
            st_[0] = fin_a(0)
            for i in range(NT):
                if i >= 1:
                    fin_c(i - 1, *st_[i - 1])
                if i + 1 < NT:
                    st_[i + 1] = fin_a(i + 1)
                fin_b(i, *st_[i])
            fin_c(NT - 1, *st_[NT - 1])

        for l in range(n_layers):
            for b in range(n_seq):
                if b == 0:
                    emit_mod(l)
                emit_layer(b, l, x_d if l == 0 else y_d, l == n_layers - 1)
        fw.wait_all(fw.sp, s_xres)
        build_program.stats = (fw.ninst, fw.nwait, {e.name: e.sem.count for e in (pe, act, dve, pool)})
    return nc


def _constants():
    bf = ml_dtypes.bfloat16
    t = np.arange(S)
    rows = (t // 64).astype(np.float32)
    cols = (t % 64).astype(np.float32)
    inv = (1.0 / (10000.0 ** (np.arange(0, 32, 2, dtype=np.float32) / np.float32(32)))).astype(np.float32)
    ra = (rows[:, None] * inv[None, :]).astype(np.float32)
    ca = (cols[:, None] * inv[None, :]).astype(np.float32)
    ang = np.concatenate([ra, ca], axis=1)
    cos = np.cos(ang.astype(np.float64)).astype(np.float32)
    sin = np.sin(ang.astype(np.float64)).astype(np.float32)
    rope_cos = np.ascontiguousarray(cos.reshape(NT, 128, 32).transpose(1, 0, 2))
    rope_sin = np.ascontiguousarray(sin.reshape(NT, 128, 32).transpose(1, 0, 2))
    ks = (np.outer(t, t) % S).astype(np.float64) * (2.0 * np.pi / S)
    dft_cs = np.ascontiguousarray(np.stack([np.cos(ks), -np.sin(ks)], axis=1).astype(np.float32).astype(bf))
    m = np.arange(64)
    a64 = (np.outer(m, m) % 64).astype(np.float64) * (2.0 * np.pi / 64)
    sc = 1.0 / np.sqrt(float(S) * 64.0)
    c64 = np.cos(a64) * sc
    s64 = np.sin(a64) * sc
    z = np.zeros((64, 64))
    c64bd = np.block([[c64, z], [z, c64]]).astype(np.float32)
    s64bd = np.block([[s64, z], [z, s64]]).astype(np.float32)
    def hilo(a):
        hi = a.astype(bf)
        lo = (a - hi.astype(np.float32)).astype(bf)
        return np.stack([hi, lo], axis=1)
    cs64bd = np.ascontiguousarray(np.stack([hilo(c64bd), hilo(s64bd)], axis=1))
    return {
        "ident_bf": np.eye(128, dtype=np.float32).astype(bf),
        "ident_f": np.eye(128, dtype=np.float32),
        "rope_cos": rope_cos, "rope_sin": rope_sin,
        "dft_cs": dft_cs,
        "cs64bd": cs64bd,
    }


def make_in_maps(x, c, w_ada, b_ada, w_in, q_gain, k_gain, conv_w, f_mix, w_out, ln_g, ln_b, ncores=NCORES):
    f = lambda a: np.ascontiguousarray(np.asarray(a, dtype=np.float32))
    consts = _constants()
    shared = {
        "w_ada": f(w_ada), "w_in": f(w_in), "q_gain": f(q_gain), "k_gain": f(k_gain),
        "f_mix": f(f_mix), "w_out": f(w_out), "ln_g": f(ln_g), "ln_b": f(ln_b),
        "b_adaT": f(np.asarray(b_ada).reshape(DEPTH, 24, 128).transpose(2, 0, 1)),
        "conv_wT": f(np.asarray(conv_w).reshape(DEPTH, 3, 2, 128).transpose(3, 0, 2, 1)),
    }
    shared.update(consts)
    x = np.asarray(x, dtype=np.float32)
    c = np.asarray(c, dtype=np.float32)
    maps = []
    for ci in range(ncores):
        m = dict(shared)
        m["x"] = np.ascontiguousarray(x[ci * BPC:(ci + 1) * BPC])
        m["cT"] = f(c[ci * BPC:(ci + 1) * BPC].reshape(BPC, KC, 128).transpose(2, 1, 0))
        maps.append(m)
    return maps


def kernel(x, c, w_ada, b_ada, w_in, q_gain, k_gain, conv_w, f_mix, w_out, ln_g, ln_b):
    maps = make_in_maps(x, c, w_ada, b_ada, w_in, q_gain, k_gain, conv_w, f_mix, w_out, ln_g, ln_b)
    nc = build_program()
    res = run_bass_kernel_spmd(nc, maps, core_ids=list(range(NCORES)))
    return np.concatenate([np.asarray(r["y"], dtype=np.float32) for r in res.results], axis=0)
```

```python
import contextlib
import numpy as np
import ml_dtypes
import concourse.bass as bass
import concourse.mybir as mybir
from concourse.bass_utils import run_bass_kernel_spmd

F32 = mybir.dt.float32
BF16 = mybir.dt.bfloat16
ALU = mybir.AluOpType
ACTF = mybir.ActivationFunctionType
AX = mybir.AxisListType

S = 2048
D = 1024
NT = 16
KC = 8
DEPTH = 4
IN_W = 2816
ALPHA = (2 * DEPTH) ** 0.25
RMS_EPS = 1e-6
LN_EPS = 1e-5
NCORES = 8
BPC = 2


class Buf:
    __slots__ = ("name", "w", "r")

    def __init__(self, name):
        self.name = name
        self.w = None
        self.r = []


class Sem:
    def __init__(self, h, name):
        self.h = h
        self.name = name
        self.count = 0


class Eng:
    def __init__(self, name, h, sem):
        self.name = name
        self.h = h
        self.sem = sem
        self.seen = {}


class FW:
    def __init__(self, nc, stack):
        self.nc = nc
        self.stack = stack
        self.pe = self._eng("pe", nc.tensor)
        self.act = self._eng("act", nc.scalar)
        self.dve = self._eng("dve", nc.vector)
        self.pool = self._eng("pool", nc.gpsimd)
        self.sp = self._eng("sp", nc.sync)
        self.ninst = 0
        self.nwait = 0

    def sem(self, name):
        h = self.stack.enter_context(self.nc.semaphore(name))
        return Sem(h, name)

    def _eng(self, name, h):
        return Eng(name, h, self.sem("s_" + name))

    def sbuf(self, name, shape, dt):
        return self.stack.enter_context(self.nc.sbuf_tensor(name, shape, dt))

    def psum(self, name, shape, dt):
        return self.stack.enter_context(self.nc.psum_tensor(name, shape, dt))

    def _waits(self, eng, reads, writes, is_dma=False, my_sem=None):
        need = {}

        def add(tok):
            s, v, who = tok
            if (not is_dma) and who == eng.name and eng.name == "pe":
                return
            if v > need.get(s, (None, 0))[1]:
                need[s] = (s, v)

        for b in reads:
            if b.w is not None:
                add(b.w)
        for b in writes:
            if b.w is not None and not (is_dma and b.w[0] is my_sem):
                add(b.w)
            for r in b.r:
                add(r)
        for s, v in need.values():
            if eng.seen.get(s, 0) >= v:
                continue
            eng.h.wait_ge(s.h, v)
            eng.seen[s] = v
            self.nwait += 1

    def op(self, eng, fn, reads=(), writes=(), sig=True):
        self._waits(eng, reads, writes)
        ins = fn()
        self.ninst += 1
        if sig:
            eng.sem.count += 1
            ins.then_inc(eng.sem.h, 1)
            tok = (eng.sem, eng.sem.count, eng.name)
        else:
            tok = (eng.sem, eng.sem.count + 1, eng.name)
        for b in reads:
            b.r.append(tok)
        for b in writes:
            b.w = tok
            b.r = []
        return ins

    def dma(self, sem, out, in_, reads=(), writes=(), q=None, **kw):
        q = q or self.sp
        self._waits(q, reads, writes, is_dma=True, my_sem=sem)
        ins = q.h.dma_start(out=out, in_=in_, **kw)
        ins.then_inc(sem.h, 16)
        sem.count += 16
        tok = (sem, sem.count, "dma:" + sem.name)
        for b in reads:
            b.r.append(tok)
        for b in writes:
            b.w = tok
            b.r = []
        self.ninst += 1
        return ins

    def wait_all(self, eng, sems):
        for s in sems:
            if s.count > 0 and eng.seen.get(s, 0) < s.count:
                eng.h.wait_ge(s.h, s.count)
                eng.seen[s] = s.count


def build_program(n_layers=DEPTH, n_seq=BPC, stop=99):
    nc = bass.Bass("TRN2", target_bir_lowering=False)
    dt_in = lambda name, shape, dt=F32: nc.dram_tensor(name, shape, dt, kind="ExternalInput").ap()
    x_d = dt_in("x", [BPC, S, D])
    cT_d = dt_in("cT", [128, KC, BPC])
    wada_d = dt_in("w_ada", [DEPTH, D, 3 * D])
    badaT_d = dt_in("b_adaT", [128, DEPTH, 24])
    win_d = dt_in("w_in", [DEPTH, D, IN_W])
    qg_d = dt_in("q_gain", [DEPTH, 64])
    kg_d = dt_in("k_gain", [DEPTH, 64])
    cwT_d = dt_in("conv_wT", [128, DEPTH, 2, 3])
    fmix_d = dt_in("f_mix", [DEPTH, 4, 64, 64])
    wout_d = dt_in("w_out", [DEPTH, D, D])
    lng_d = dt_in("ln_g", [DEPTH, D])
    lnb_d = dt_in("ln_b", [DEPTH, D])
    identb_d = dt_in("ident_bf", [128, 128], BF16)
    identf_d = dt_in("ident_f", [128, 128])
    ropec_d = dt_in("rope_cos", [128, NT, 32])
    ropes_d = dt_in("rope_sin", [128, NT, 32])
    dft_d = dt_in("dft_cs", [S, 2, S], BF16)
    csbd_d = dt_in("cs64bd", [128, 2, 2, 128], BF16)
    y_d = nc.dram_tensor("y", [BPC, S, D], F32, kind="ExternalOutput").ap()

    with contextlib.ExitStack() as st:
        fw = FW(nc, st)
        pe, act, dve, pool = fw.pe, fw.act, fw.dve, fw.pool
        V, A, P, T = nc.vector, nc.scalar, nc.gpsimd, nc.tensor

        hT = fw.sbuf("hT", [128, KC, S], BF16)
        yT = fw.sbuf("yT", [128, 8, S], BF16)
        QT = fw.sbuf("QT", [128, 4, S], BF16)
        KTp = fw.sbuf("KTp", [128, 2, S], BF16)
        Vaug = fw.sbuf("Vaug", [128, NT, 2, 128], BF16)
        NPT = 4
        PT = fw.sbuf("PT", [128, NPT, 2, 512], BF16)
        fuT = PT[:, :, :, :].rearrange("p a b c -> p (a b c)").rearrange("p (m s) -> p m s", m=2)
        wbf = fw.sbuf("wbf", [128, 2, KC, 512], BF16)
        NSTG = 3
        stage = fw.sbuf("stage", [128, NSTG, 1024], F32)
        NXIN = 3
        xin = fw.sbuf("xin", [128, NXIN, D], F32)
        NXR = 3
        xres = fw.sbuf("xres", [128, NXR, D], F32)
        lngb = fw.sbuf("lngb", [128, D], F32)
        lnbb = fw.sbuf("lnbb", [128, D], F32)
        gateb = fw.sbuf("gateb", [128, D], F32)
        ropec = fw.sbuf("ropec", [128, NT, 32], F32)
        ropes = fw.sbuf("ropes", [128, NT, 32], F32)
        zt = xin[:, :, :].rearrange("p a d -> p (a d)")[:, 0:S + 2]
        xh = fw.sbuf("xh", [128, 2, D], BF16)
        NWK = 6
        wkf = fw.sbuf("wk", [128, NWK, 640], F32)
        qrb = fw.sbuf("qrb", [128, 2, 640], BF16)
        identb = fw.sbuf("identb", [128, 128], BF16)
        identf = fw.sbuf("identf", [128, 128], F32)
        onesb = fw.sbuf("onesb", [128, 128], BF16)
        dg = fw.sbuf("dg", [128, 2, 128], F32)
        dgb = fw.sbuf("dgb", [128, 2, 2, 128], BF16)
        csbd = fw.sbuf("csbd", [128, 2, 2, 128], BF16)
        fm = fw.sbuf("fm", [128, 2, 64], F32)
        fmb = fw.sbuf("fmb", [128, 2, 2, 64], BF16)
        cTb = fw.sbuf("cTb", [128, 2, KC, BPC], BF16)
        whi = fw.sbuf("whi", [128, 2, 1024], BF16)
        ABbd = fw.sbuf("ABbd", [128, 2, 2, 128], BF16)
        cTs = fw.sbuf("cTs", [128, KC, BPC], F32)
        badaT = fw.sbuf("badaT", [128, DEPTH, 24], F32)
        modT = fw.sbuf("modT", [128, 24, BPC], F32)
        s1T = fw.sbuf("s1T", [128, 8, BPC], F32)
        g1T = fw.sbuf("g1T", [128, 8, BPC], F32)
        cwT = fw.sbuf("cwT", [128, DEPTH, 2, 3], F32)
        qgb = fw.sbuf("qgb", [128, 64], F32)
        kgb = fw.sbuf("kgb", [128, 64], F32)
        mhalf = fw.sbuf("mhalf", [128, 16], F32)
        stt = fw.sbuf("stt", [128, 4, 2, 6], F32)
        mv = fw.sbuf("mv", [128, 4, 2], F32)
        sml = fw.sbuf("sml", [128, 4, 32], F32)

        pacc = fw.psum("pacc", [128, 8, 512], F32)
        ptrv = [pacc[:, 6, :].bitcast(BF16), pacc[:, 7, :].bitcast(BF16)]

        B = lambda n: Buf(n)
        hT_b = [B("hT%d" % i) for i in range(NT)]
        NTAB = 6
        tab_b = [B("tab%d" % i) for i in range(NTAB)]
        yT_b = [[B("yT%d_%d" % (c, t)) for t in range(4)] for c in range(8)]
        QT_b = [B("QT%d" % j) for j in range(4)]
        UAB_b = B("UAB")
        KT_b = B("KTp")
        Va_b = B("Vaug")
        PT_b = [B("PT%d" % i) for i in range(NPT)]
        fuT_b = B("fuT")
        wbf_b = [B("wbf0"), B("wbf1")]
        stg_b = [B("stg%d" % i) for i in range(NSTG)]
        xin_b = [B("xin%d" % i) for i in range(NXIN)]
        xres_b = [B("xres%d" % i) for i in range(NXR)]
        lng_b, lnb_b, gate_b = B("lng"), B("lnb"), B("gate")
        rope_b, z_b = B("rope"), B("z")
        xh_b = [B("xh0"), B("xh1")]
        wk_b = [B("wk%d" % i) for i in range(NWK)]
        qrb_b = [B("qrb0"), B("qrb1")]
        cd_b, cm_b = B("constd"), B("constm")
        dg_b = [B("dg0"), B("dg1")]
        fm_b, AB_b = B("fm"), B("ABbd")
        whi_b = [B("whi0"), B("whi1")]
        cTb_b = B("cTb")
        mod_b, s1_b, g1_b = B("modT"), B("s1T"), B("g1T")
        gain_b = B("gains")
        stt_b = [B("stt%d" % i) for i in range(4)]
        mv_b = [B("mv%d" % i) for i in range(4)]
        sml_b = [B("sml%d" % i) for i in range(4)]
        pacc_b = [B("pacc%d" % i) for i in range(8)]
        ptr_b = [pacc_b[6], pacc_b[7]]
        ydram_b = [[B("yd%d_%d" % (b, i)) for i in range(NT)] for b in range(BPC)]

        s_const = fw.sem("d_const")
        s_stg = [fw.sem("d_stg%d" % i) for i in range(NSTG)]
        s_xin = [fw.sem("d_xin%d" % i) for i in range(NXIN)]
        s_xres = [fw.sem("d_xres%d" % i) for i in range(NXR)]
        s_tab = [fw.sem("d_tab%d" % i) for i in range(NTAB)]
        s_lng, s_lnb, s_gain, s_fm = fw.sem("d_lng"), fw.sem("d_lnb"), fw.sem("d_gain"), fw.sem("d_fm")

        tabv = hT[:, 0:NTAB, :].rearrange("p a (c k) -> p a c k", c=2)
        UAB = QT[:, :, :].rearrange("p j (a m c n) -> p (j a) m c n", a=4, m=2, c=2)

        for (dst, src) in [(identb, identb_d), (identf, identf_d), (ropec, ropec_d), (ropes, ropes_d),
                           (csbd, csbd_d), (cTs, cT_d), (badaT, badaT_d), (cwT, cwT_d)]:
            fw.dma(s_const, dst[:], src, writes=[cd_b, rope_b])
        fw.op(pool, lambda: P.memset(Vaug[:], 1.0), writes=[Va_b])
        fw.op(pool, lambda: P.memset(KTp[:], 0.0), writes=[KT_b])
        fw.op(pool, lambda: P.memset(mhalf[:], -0.5), writes=[cm_b])
        fw.op(pool, lambda: P.memset(onesb[:], 1.0), writes=[cm_b])
        fw.op(dve, lambda: V.tensor_copy(out=cTb[:, 0, :, :], in_=cTs[:]), reads=[cd_b], writes=[cTb_b])
        fw.op(dve, lambda: V.tensor_tensor(out=cTb[:, 1, :, :], in0=cTs[:], in1=cTb[:, 0, :, :], op=ALU.subtract), reads=[cd_b, cTb_b], writes=[cTb_b])
        fw.op(pool, lambda: P.memset(ABbd[:], 0.0), writes=[AB_b])

        rr = {"stg": 0, "xin": 0, "xres": 0, "tab": 0, "wk": 0, "whi": 0, "pp": 0, "st": 0, "pa": 0, "ptr": 0, "pt": 0, "xh": 0, "qrb": 0, "dg": 0}

        def nxt(key, n):
            v = rr[key]
            rr[key] = (v + 1) % n
            return v

        def rstd_from(var_ap, out_ap, slot_b, scale, eps):
            n = var_ap.shape[-1]
            fw.op(dve, lambda: V.tensor_scalar(out=out_ap, in0=var_ap, scalar1=scale, scalar2=eps,
                                               op0=ALU.mult, op1=ALU.add), reads=[slot_b], writes=[slot_b])
            fw.op(pool, lambda: P.tensor_tensor(out=out_ap, in0=out_ap, in1=mhalf[:, 0:n], op=ALU.pow),
                  reads=[slot_b, cd_b, cm_b], writes=[slot_b])

        def emit_mod(l):
            pa = 5
            for oc in range(24):
                sl = nxt("stg", NSTG)
                fw.dma(s_stg[sl], stage[:, sl, :].rearrange("p (k c) -> p k c", k=KC),
                       wada_d[l, :, oc * 128:(oc + 1) * 128].rearrange("(k p) c -> p k c", p=128),
                       writes=[stg_b[sl]])
                ws = nxt("whi", 2)
                if oc % 3 == 0:
                    fw.op(dve, lambda sl=sl, ws=ws: V.tensor_copy(out=whi[:, ws, :], in_=stage[:, sl, :]), reads=[stg_b[sl]], writes=[whi_b[ws]])
                elif oc % 3 == 1:
                    fw.op(act, lambda sl=sl, ws=ws: A.copy(out=whi[:, ws, :], in_=stage[:, sl, :]), reads=[stg_b[sl]], writes=[whi_b[ws]])
                else:
                    fw.op(pool, lambda sl=sl, ws=ws: P.tensor_copy(out=whi[:, ws, :], in_=stage[:, sl, :]), reads=[stg_b[sl]], writes=[whi_b[ws]])
                for kc in range(KC):
                    for hl in range(2):
                        fw.op(pe, lambda kc=kc, ws=ws, oc=oc, hl=hl: T.matmul(
                            pacc[:, pa, oc * 2:oc * 2 + 2], lhsT=whi[:, ws, kc * 128:(kc + 1) * 128],
                            rhs=cTb[:, hl, kc, :], start=(kc == 0 and hl == 0), stop=(kc == KC - 1 and hl == 1)),
                            reads=[whi_b[ws], cTb_b], writes=[pacc_b[pa]], sig=(kc == KC - 1 and hl == 1))
            fw.op(dve, lambda: V.tensor_tensor(
                out=modT[:], in0=pacc[:, pa, 0:48].rearrange("p (o b) -> p o b", b=BPC),
                in1=badaT[:, l, :].unsqueeze(2).to_broadcast([128, 24, BPC]), op=ALU.add),
                reads=[pacc_b[pa], cd_b, cm_b], writes=[mod_b])
            fw.op(dve, lambda: V.tensor_scalar(out=s1T[:], in0=modT[:, 8:16, :], scalar1=1.0, scalar2=None, op0=ALU.add),
                  reads=[mod_b], writes=[s1_b])
            fw.op(dve, lambda: V.tensor_scalar(out=g1T[:], in0=modT[:, 16:24, :], scalar1=1.0, scalar2=None, op0=ALU.add),
                  reads=[mod_b], writes=[g1_b])
            fw.dma(s_gain, qgb[:], qg_d[l].partition_broadcast(128), writes=[gain_b])
            fw.dma(s_gain, kgb[:], kg_d[l].partition_broadcast(128), writes=[gain_b])
            fw.dma(s_fm, fm[:], fmix_d[l].rearrange("(a g) m d -> (g m) a d", a=2), writes=[fm_b])
            fw.op(dve, lambda: V.tensor_copy(out=fmb[:, :, 0, :], in_=fm[:]), reads=[fm_b], writes=[fm_b])
            fw.op(dve, lambda: V.tensor_tensor(out=fmb[:, :, 1, :], in0=fm[:], in1=fmb[:, :, 0, :], op=ALU.subtract), reads=[fm_b], writes=[fm_b])
            for m2 in range(2):
                for ab in range(2):
                    for t3, (th, fh) in enumerate(((0, 0), (0, 1), (1, 0))):
                        fw.op(pe, lambda m2=m2, ab=ab, th=th, fh=fh, t3=t3: T.matmul(
                            pacc[:, pa, 64 + (m2 * 2 + ab) * 64: 64 + (m2 * 2 + ab + 1) * 64], lhsT=csbd[:, ab, th, :], rhs=fmb[:, m2, fh, :],
                            start=(t3 == 0), stop=(t3 == 2)), reads=[fm_b, cd_b, cm_b], writes=[pacc_b[pa]], sig=(t3 == 2))
            for m2 in range(2):
                for ab in range(2):
                    c0 = 64 + (m2 * 2 + ab) * 64
                    fw.op(dve, lambda m2=m2, ab=ab, c0=c0: V.tensor_copy(out=ABbd[0:64, m2, ab, 0:64], in_=pacc[0:64, pa, c0:c0 + 64]),
                          reads=[pacc_b[pa]], writes=[AB_b])
                    fw.op(dve, lambda m2=m2, ab=ab, c0=c0: V.tensor_copy(out=ABbd[64:128, m2, ab, 64:128], in_=pacc[64:128, pa, c0:c0 + 64]),
                          reads=[pacc_b[pa]], writes=[AB_b])

        def load_wgroup(l, slot, src0, ncol_src, casts):
            for kc in range(KC):
                sl = nxt("stg", NSTG)
                fw.dma(s_stg[sl], stage[:, sl, 0:ncol_src], win_d[l, kc * 128:(kc + 1) * 128, src0:src0 + ncol_src],
                       writes=[stg_b[sl]])
                for cf in casts:
                    o_ap, i_ap = cf(wbf[:, slot, kc, :], stage[:, sl, :])
                    fw.op(pool, lambda o_ap=o_ap, i_ap=i_ap: P.tensor_copy(out=o_ap, in_=i_ap),
                          reads=[stg_b[sl]], writes=[wbf_b[slot]])

        perm_cast = lambda o, i: (o[:, 0:512].rearrange("p (j h d) -> p j h d", j=4, h=2),
                                  i[:, 0:512].rearrange("p (h j d) -> p j h d", h=2, j=4))

        def rope(src, dst, i, H, sb, wkA, wkB, src_bufs, dst_bufs):
            sv = src.rearrange("p (h a b d) -> p h a b d", h=H, a=2, b=2)
            dv = dst.rearrange("p (h a b d) -> p h a b d", h=H, a=2, b=2)
            cs = ropec[:, i, :].rearrange("p (a d) -> p a d", a=2).unsqueeze(1).to_broadcast([128, H, 2, 16])
            sn = ropes[:, i, :].rearrange("p (a d) -> p a d", a=2).unsqueeze(1).to_broadcast([128, H, 2, 16])
            n = H * 32
            t1 = wkA[:, 0:n].rearrange("p (h a d) -> p h a d", h=H, a=2)
            t2 = wkA[:, n:2 * n].rearrange("p (h a d) -> p h a d", h=H, a=2)
            t3 = wkB[:, 0:n].rearrange("p (h a d) -> p h a d", h=H, a=2)
            t4 = wkB[:, n:2 * n].rearrange("p (h a d) -> p h a d", h=H, a=2)
            x0 = sv[:, :, :, 0, :]
            x1 = sv[:, :, :, 1, :]
            wa, wb = sb
            fw.op(dve, lambda: V.tensor_tensor(out=t1, in0=x0, in1=cs, op=ALU.mult), reads=src_bufs + [rope_b], writes=[wa])
            fw.op(dve, lambda: V.tensor_tensor(out=t2, in0=x1, in1=sn, op=ALU.mult), reads=src_bufs + [rope_b], writes=[wa])
            fw.op(dve, lambda: V.tensor_tensor(out=t3, in0=x1, in1=cs, op=ALU.mult), reads=src_bufs + [rope_b], writes=[wb])
            fw.op(dve, lambda: V.tensor_tensor(out=t4, in0=x0, in1=sn, op=ALU.mult), reads=src_bufs + [rope_b], writes=[wb])
            fw.op(dve, lambda: V.tensor_tensor(out=dv[:, :, :, 0, :], in0=t1, in1=t2, op=ALU.subtract), reads=[wa], writes=dst_bufs)
            fw.op(dve, lambda: V.tensor_tensor(out=dv[:, :, :, 1, :], in0=t3, in1=t4, op=ALU.add), reads=[wb], writes=dst_bufs)

        def emit_layer(b, l, x_src, last):
            for kc in range(KC):
                ds = nxt("dg", 2)
                pa = 4 + (kc // 4)
                fw.op(dve, lambda kc=kc, ds=ds: V.tensor_scalar(out=dg[:, ds, :], in0=identf[:], scalar1=g1T[:, kc, b:b + 1],
                                                              scalar2=None, op0=ALU.mult),
                      reads=[cd_b, cm_b, g1_b], writes=[dg_b[ds]])
                fw.op(dve, lambda ds=ds: V.tensor_copy(out=dgb[:, ds, 0, :], in_=dg[:, ds, :]), reads=[dg_b[ds]], writes=[dg_b[ds]])
                fw.op(dve, lambda ds=ds: V.tensor_tensor(out=dgb[:, ds, 1, :], in0=dg[:, ds, :], in1=dgb[:, ds, 0, :], op=ALU.subtract), reads=[dg_b[ds]], writes=[dg_b[ds]])
                for hl in range(2):
                    fw.op(pe, lambda kc=kc, ds=ds, pa=pa, hl=hl: T.matmul(pacc[:, pa, (kc % 4) * 128:(kc % 4 + 1) * 128], lhsT=onesb[:], rhs=dgb[:, ds, hl, :],
                                                                        start=(hl == 0), stop=(hl == 1)),
                          reads=[dg_b[ds], cd_b, cm_b], writes=[pacc_b[pa]], sig=(hl == 1))
            for h2 in range(2):
                fw.op(act, lambda h2=h2: A.copy(out=gateb[:, h2 * 512:(h2 + 1) * 512], in_=pacc[:, 4 + h2, :]),
                      reads=[pacc_b[4 + h2]], writes=[gate_b])

            if stop <= 1:
                return
            load_wgroup(l, 0, 512, 256, [lambda o, i: (o[:, 0:256], i[:, 0:256])])
            load_wgroup(l, 1, 0, 512, [perm_cast])

            if stop <= 1.5:
                return
            def ln1_a(i):
                xs_ = nxt("xin", NXIN)
                fw.dma(s_xin[xs_], xin[:, xs_, :], x_src[b, i * 128:(i + 1) * 128, :],
                       reads=[ydram_b[b][i]], writes=[xin_b[xs_]] + ([z_b] if i < NXIN else []))
                ss = nxt("st", 4)
                for h2 in range(2):
                    fw.op(dve, lambda h2=h2: V.bn_stats(out=stt[:, ss, h2, :], in_=xin[:, xs_, h2 * 512:(h2 + 1) * 512]),
                          reads=[xin_b[xs_]], writes=[stt_b[ss]])
                fw.op(dve, lambda: V.bn_aggr(out=mv[:, ss, :], in_=stt[:, ss, :, :].rearrange("p a b -> p (a b)")), reads=[stt_b[ss]], writes=[mv_b[ss]])
                fw.op(dve, lambda: V.tensor_scalar(out=sml[:, ss, 0:1], in0=mv[:, ss, 1:2], scalar1=1.0, scalar2=LN_EPS,
                                                   op0=ALU.mult, op1=ALU.add), reads=[mv_b[ss]], writes=[sml_b[ss]])
                fw.op(pool, lambda: P.tensor_tensor(out=sml[:, ss, 0:1], in0=sml[:, ss, 0:1], in1=mhalf[:, 0:1], op=ALU.pow),
                      reads=[sml_b[ss], cd_b, cm_b], writes=[sml_b[ss]])
                return xs_, ss

            def ln1_b(i, xs_, ss):
                hs = nxt("xh", 2)
                fw.op(dve, lambda: V.tensor_scalar(
                    out=xh[:, hs, :], in0=xin[:, xs_, :], scalar1=mv[:, ss, 0:1], scalar2=sml[:, ss, 0:1],
                    op0=ALU.subtract, op1=ALU.mult), reads=[xin_b[xs_], mv_b[ss], sml_b[ss]], writes=[xh_b[hs]])
                pb = nxt("ptr", 2)
                for kc in range(KC):
                    fw.op(pe, lambda kc=kc: T.transpose(ptrv[pb][:, kc * 128:(kc + 1) * 128], xh[:, hs, kc * 128:(kc + 1) * 128], identb[:]),
                          reads=[xh_b[hs], cd_b, cm_b], writes=[ptr_b[pb]], sig=(kc == KC - 1))
                for kc in range(KC):
                    if True:
                        fw.op(act, lambda kc=kc: A.activation(
                            out=hT[:, kc, i * 128:(i + 1) * 128], in_=ptrv[pb][:, kc * 128:(kc + 1) * 128], func=ACTF.Identity,
                            scale=s1T[:, kc, b:b + 1], bias=modT[:, kc, b:b + 1]),
                            reads=[ptr_b[pb], s1_b, mod_b], writes=[hT_b[i]] + (tab_b if kc == 0 else []))
                    else:
                        fw.op(dve, lambda kc=kc: V.tensor_scalar(
                            out=hT[:, kc, i * 128:(i + 1) * 128], in0=ptrv[pb][:, kc * 128:(kc + 1) * 128],
                            scalar1=s1T[:, kc, b:b + 1], scalar2=modT[:, kc, b:b + 1], op0=ALU.mult, op1=ALU.add),
                            reads=[ptr_b[pb], s1_b, mod_b], writes=[hT_b[i]])

            pend = ln1_a(0)
            for i in range(NT):
                nxt_pend = ln1_a(i + 1) if i + 1 < NT else None
                ln1_b(i, *pend)
                pend = nxt_pend

            if stop <= 2:
                return
            HQK = 10

            def qk_a1(i):
                pp = nxt("pp", 3)
                bq, bk = 2 * pp, 2 * pp + 1
                for kc in range(KC):
                    fw.op(pe, lambda kc=kc: T.matmul(pacc[:, bq, :], lhsT=hT[:, kc, i * 128:(i + 1) * 128], rhs=wbf[:, 1, kc, :],
                                                     start=(kc == 0), stop=(kc == KC - 1)),
                          reads=[hT_b[i], wbf_b[1]], writes=[pacc_b[bq]], sig=(kc == KC - 1))
                for kc in range(KC):
                    fw.op(pe, lambda kc=kc: T.matmul(pacc[:, bk, 0:256], lhsT=hT[:, kc, i * 128:(i + 1) * 128], rhs=wbf[:, 0, kc, 0:256],
                                                     start=(kc == 0), stop=(kc == KC - 1)),
                          reads=[hT_b[i], wbf_b[0]], writes=[pacc_b[bk]], sig=(kc == KC - 1))
                fw.op(act, lambda: A.copy(out=Vaug[:, i, 0, 0:64], in_=pacc[:, bk, 128:192]), reads=[pacc_b[bk]], writes=[Va_b])
                fw.op(act, lambda: A.copy(out=Vaug[:, i, 1, 64:128], in_=pacc[:, bk, 192:256]), reads=[pacc_b[bk]], writes=[Va_b])
                n = HQK * 64
                src = pacc[:, bq:bq + 2, :].rearrange("p a c -> p (a c)")[:, 0:n]
                srcb = [pacc_b[bq], pacc_b[bk]]
                wA = nxt("wk", NWK)
                fw.op(act, lambda: A.activation(out=wkf[:, wA, 0:n], in_=src, func=ACTF.Square), reads=srcb, writes=[wk_b[wA]])
                return bq, bk, wA

            def qk_a2(i, bq, bk, wA):
                n = HQK * 64
                wB = nxt("wk", NWK); wC = nxt("wk", NWK)
                ss = nxt("st", 4)
                fw.op(dve, lambda: V.tensor_reduce(out=sml[:, ss, 0:HQK], in_=wkf[:, wA, 0:n].rearrange("p (h d) -> p h d", h=HQK),
                                                   axis=AX.X, op=ALU.add), reads=[wk_b[wA]], writes=[sml_b[ss]])
                rstd_from(sml[:, ss, 0:HQK], sml[:, ss, 0:HQK], sml_b[ss], 1.0 / 64, RMS_EPS)
                fw.op(dve, lambda: V.tensor_tensor(out=wkf[:, wB, 0:512].rearrange("p (h d) -> p h d", h=8), in0=pacc[:, bq, :].rearrange("p (h d) -> p h d", h=8),
                                                   in1=qgb[:].unsqueeze(1).to_broadcast([128, 8, 64]), op=ALU.mult),
                      reads=[pacc_b[bq], gain_b], writes=[wk_b[wB]])
                fw.op(dve, lambda: V.tensor_tensor(out=wkf[:, wB, 512:640].rearrange("p (h d) -> p h d", h=2), in0=pacc[:, bk, 0:128].rearrange("p (h d) -> p h d", h=2),
                                                   in1=kgb[:].unsqueeze(1).to_broadcast([128, 2, 64]), op=ALU.mult),
                      reads=[pacc_b[bk], gain_b], writes=[wk_b[wB]])
                rope(wkf[:, wB, 0:n], wkf[:, wB, 0:n], i, HQK, (wk_b[wC], wk_b[wA]), wkf[:, wC, :], wkf[:, wA, :], [wk_b[wB]], [wk_b[wB]])
                return wB, ss

            def qk_b(i, wB, ss):
                n = HQK * 64
                qs = nxt("qrb", 2)
                fw.op(dve, lambda: V.tensor_tensor(out=qrb[:, qs, 0:n].rearrange("p (h d) -> p h d", h=HQK), in0=wkf[:, wB, 0:n].rearrange("p (h d) -> p h d", h=HQK),
                                                   in1=sml[:, ss, 0:HQK].unsqueeze(2).to_broadcast([128, HQK, 64]), op=ALU.mult),
                      reads=[wk_b[wB], sml_b[ss]], writes=[qrb_b[qs]])
                pb = nxt("ptr", 2)
                for j in range(5):
                    fw.op(pe, lambda j=j: T.transpose(ptrv[pb][:, j * 128:(j + 1) * 128], qrb[:, qs, j * 128:(j + 1) * 128], identb[:]),
                          reads=[qrb_b[qs], cd_b, cm_b], writes=[ptr_b[pb]], sig=(j == 4))
                fw.op(act, lambda: A.copy(out=QT[:, :, i * 128:(i + 1) * 128], in_=ptrv[pb][:, 0:512].rearrange("p (j t) -> p j t", j=4)),
                      reads=[ptr_b[pb]], writes=QT_b + [UAB_b])
                fw.op(act, lambda: A.copy(out=KTp[0:64, 0, i * 128:(i + 1) * 128], in_=ptrv[pb][0:64, 512:640]), reads=[ptr_b[pb]], writes=[KT_b])
                fw.op(act, lambda: A.copy(out=KTp[64:128, 1, i * 128:(i + 1) * 128], in_=ptrv[pb][64:128, 512:640]), reads=[ptr_b[pb]], writes=[KT_b])

            a1 = {0: qk_a1(0), 1: qk_a1(1)}
            a2 = {0: qk_a2(0, *a1[0])}
            for i in range(NT):
                if i + 2 < NT:
                    a1[i + 2] = qk_a1(i + 2)
                if i + 1 < NT:
                    a2[i + 1] = qk_a2(i + 1, *a1[i + 1])
                qk_b(i, *a2[i])
            load_wgroup(l, 0, 768, 512, [perm_cast])
            if stop <= 4:
                return
            for j in range(4):
                for tc in range(4):
                    pa = nxt("pa", 4)
                    for kc in range(KC):
                        fw.op(pe, lambda kc=kc, pa=pa, j=j, tc=tc: T.matmul(
                            pacc[:, pa, :], lhsT=wbf[:, 0, kc, j * 128:(j + 1) * 128], rhs=hT[:, kc, tc * 512:(tc + 1) * 512],
                            start=(kc == 0), stop=(kc == KC - 1)),
                            reads=hT_b[tc * 4:(tc + 1) * 4] + [wbf_b[0]], writes=[pacc_b[pa]], sig=(kc == KC - 1))
                    fw.op(act, lambda pa=pa, j=j, tc=tc: A.activation(out=yT[:, j, tc * 512:(tc + 1) * 512], in_=pacc[:, pa, :], func=ACTF.Silu),
                          reads=[pacc_b[pa]], writes=[yT_b[j][tc]])

            if stop <= 5:
                return
            conv_cast = lambda m: (lambda o, i: (o[:, 0:512].rearrange("p (g c) -> p g c", g=4),
                                                 i[:, 0:1024].rearrange("p (g m c) -> p g m c", g=4, m=2)[:, :, m, :]))
            load_wgroup(l, 1, 1280, 1024, [conv_cast(0)])
            load_wgroup(l, 0, 1280, 1024, [conv_cast(1)])

            if stop <= 6:
                return
            for j in range(4):
                for qc in range(4):
                    steps = []

                    def qk(kt, sb_):
                        for kv in range(2):
                            fw.op(pe, lambda kv=kv, kt=kt, sb_=sb_: T.matmul(
                                pacc[:, SBASE[sb_] + kv, :], lhsT=KTp[:, kv, kt * 128:(kt + 1) * 128], rhs=QT[:, j, qc * 512:(qc + 1) * 512],
                                start=True, stop=True), reads=[KT_b, QT_b[j]], writes=[pacc_b[SBASE[sb_] + kv]])

                    def ex(kt, sb_, ps):
                        fw.op(act, lambda sb_=sb_, ps=ps: A.activation(out=PT[:, ps, :, :], in_=pacc[:, SBASE[sb_]:SBASE[sb_] + 2, :], func=ACTF.Exp, scale=0.125),
                              reads=[pacc_b[SBASE[sb_]], pacc_b[SBASE[sb_] + 1]], writes=[PT_b[ps], fuT_b])

                    def pv(kt, ps):
                        for kv in range(2):
                            fw.op(pe, lambda kv=kv, kt=kt, ps=ps: T.matmul(
                                pacc[:, 4 + kv, :], lhsT=Vaug[:, kt, kv, :], rhs=PT[:, ps, kv, :], start=(kt == 0), stop=(kt == NT - 1)),
                                reads=[Va_b, PT_b[ps]], writes=[pacc_b[4 + kv]], sig=(kt == NT - 1))

                    SBASE = [0, 2, 6]
                    qk(0, 0)
                    qk(1, 1)
                    for kt in range(NT):
                        ps = nxt("pt", NPT)
                        ex(kt, kt % 3, ps)
                        if kt + 2 < NT:
                            qk(kt + 2, (kt + 2) % 3)
                        pv(kt, ps)
                    w0 = nxt("wk", NWK); w1 = nxt("wk", NWK); w2 = nxt("wk", NWK)
                    cols = slice(qc * 512, (qc + 1) * 512)
                    fw.op(dve, lambda w0=w0: V.tensor_copy(out=wkf[:, w0, 0:512], in_=pacc[:, 4, :]), reads=[pacc_b[4]], writes=[wk_b[w0]])
                    fw.op(dve, lambda w1=w1: V.tensor_copy(out=wkf[:, w1, 0:512], in_=pacc[:, 5, :]), reads=[pacc_b[5]], writes=[wk_b[w1]])
                    fw.op(dve, lambda w0=w0, w2=w2: V.reciprocal(out=wkf[0:64, w2, 0:512], in_=wkf[64:128, w0, 0:512]), reads=[wk_b[w0]], writes=[wk_b[w2]])
                    fw.op(dve, lambda w0=w0, w2=w2: V.tensor_tensor(out=wkf[0:64, w0, 0:512], in0=wkf[0:64, w0, 0:512], in1=wkf[0:64, w2, 0:512], op=ALU.mult),
                          reads=[wk_b[w0], wk_b[w2]], writes=[wk_b[w0]])
                    fw.op(pool, lambda w0=w0: P.tensor_tensor(out=yT[0:64, j, cols], in0=yT[0:64, j, cols], in1=wkf[0:64, w0, 0:512], op=ALU.mult),
                          reads=[wk_b[w0], yT_b[j][qc]], writes=[yT_b[j][qc]])
                    fw.op(dve, lambda w1=w1, w2=w2: V.reciprocal(out=wkf[64:128, w2, 0:512], in_=wkf[0:64, w1, 0:512]), reads=[wk_b[w1]], writes=[wk_b[w2]])
                    fw.op(dve, lambda w1=w1, w2=w2: V.tensor_tensor(out=wkf[64:128, w1, 0:512], in0=wkf[64:128, w1, 0:512], in1=wkf[64:128, w2, 0:512], op=ALU.mult),
                          reads=[wk_b[w1], wk_b[w2]], writes=[wk_b[w1]])
                    fw.op(pool, lambda w1=w1: P.tensor_tensor(out=yT[64:128, j, cols], in0=yT[64:128, j, cols], in1=wkf[64:128, w1, 0:512], op=ALU.mult),
                          reads=[wk_b[w1], yT_b[j][qc]], writes=[yT_b[j][qc]])

            if stop <= 7:
                return
            fw.op(pool, lambda: P.memset(zt[:, 0:1], 0.0), writes=[z_b] + xin_b)
            fw.op(pool, lambda: P.memset(zt[:, S + 1:S + 2], 0.0), writes=[z_b] + xin_b)
            for m in range(2):
                slot = 1 - m
                for tc in range(4):
                    pas = [nxt("pa", 4) for _ in range(4)]
                    for g in range(4):
                        for kc in range(KC):
                            fw.op(pe, lambda kc=kc, g=g, tc=tc, slot=slot, pas=pas: T.matmul(
                                pacc[:, pas[g], :], lhsT=wbf[:, slot, kc, g * 128:(g + 1) * 128], rhs=hT[:, kc, tc * 512:(tc + 1) * 512],
                                start=(kc == 0), stop=(kc == KC - 1)),
                                reads=hT_b[tc * 4:(tc + 1) * 4] + [wbf_b[slot]], writes=[pacc_b[pas[g]]], sig=(kc == KC - 1))
                    w0 = nxt("wk", NWK); w1 = nxt("wk", NWK)
                    fw.op(act, lambda w0=w0, pas=pas: A.copy(out=wkf[:, w0, 0:512], in_=pacc[:, pas[2], :]), reads=[pacc_b[pas[2]]], writes=[wk_b[w0]])
                    fw.op(dve, lambda w0=w0, pas=pas, tc=tc: V.tensor_tensor(out=zt[:, 1 + tc * 512:1 + (tc + 1) * 512], in0=pacc[:, pas[0], :], in1=wkf[:, w0, 0:512], op=ALU.mult),
                          reads=[pacc_b[pas[0]], wk_b[w0]], writes=[z_b])
                    fw.op(act, lambda w1=w1, pas=pas: A.activation(out=wkf[:, w1, 0:512], in_=pacc[:, pas[3], :], func=ACTF.Silu), reads=[pacc_b[pas[3]]], writes=[wk_b[w1]])
                    fw.op(dve, lambda w1=w1, pas=pas, tc=tc, m=m: V.tensor_tensor(out=yT[:, 4 + m, tc * 512:(tc + 1) * 512], in0=pacc[:, pas[1], :], in1=wkf[:, w1, 0:512], op=ALU.mult),
                          reads=[pacc_b[pas[1]], wk_b[w1]], writes=[yT_b[4 + m][tc]])
                for tc in range(4):
                    w0 = nxt("wk", NWK)
                    c0 = 1 + tc * 512
                    fw.op(dve, lambda w0=w0, c0=c0, m=m: V.tensor_scalar(out=wkf[:, w0, 0:512], in0=zt[:, c0:c0 + 512], scalar1=cwT[:, l, m, 1:2], scalar2=None, op0=ALU.mult),
                          reads=[z_b, cd_b, cm_b], writes=[wk_b[w0]])
                    fw.op(dve, lambda w0=w0, c0=c0, m=m: V.scalar_tensor_tensor(out=wkf[:, w0, 0:512], in0=zt[:, c0 - 1:c0 + 511], scalar=cwT[:, l, m, 0:1], in1=wkf[:, w0, 0:512],
                                                                           op0=ALU.mult, op1=ALU.add), reads=[z_b, cd_b, cm_b, wk_b[w0]], writes=[wk_b[w0]])
                    fw.op(dve, lambda w0=w0, c0=c0, m=m: V.scalar_tensor_tensor(out=wkf[:, w0, 0:512], in0=zt[:, c0 + 1:c0 + 513], scalar=cwT[:, l, m, 2:3], in1=wkf[:, w0, 0:512],
                                                                           op0=ALU.mult, op1=ALU.add), reads=[z_b, cd_b, cm_b, wk_b[w0]], writes=[wk_b[w0]])
                    fw.op(pool, lambda w0=w0, tc=tc, m=m: P.tensor_tensor(out=yT[:, 4 + m, tc * 512:(tc + 1) * 512], in0=yT[:, 4 + m, tc * 512:(tc + 1) * 512], in1=wkf[:, w0, 0:512], op=ALU.mult),
                          reads=[wk_b[w0], yT_b[4 + m][tc]], writes=[yT_b[4 + m][tc]])
                if m == 0:
                    load_wgroup(l, 1, 2304, 512, [lambda o, i: (o[:, 0:512], i[:, 0:512])])

            if stop <= 8:
                return
            for m2 in range(2):
                for tc in range(4):
                    pa = nxt("pa", 4)
                    for kc in range(KC):
                        fw.op(pe, lambda kc=kc, pa=pa, m2=m2, tc=tc: T.matmul(
                            pacc[:, pa, :], lhsT=wbf[:, 1, kc, m2 * 128:(m2 + 1) * 128], rhs=hT[:, kc, tc * 512:(tc + 1) * 512],
                            start=(kc == 0), stop=(kc == KC - 1)),
                            reads=hT_b[tc * 4:(tc + 1) * 4] + [wbf_b[1]], writes=[pacc_b[pa]], sig=(kc == KC - 1))
                    fw.op(act, lambda pa=pa, m2=m2, tc=tc: A.copy(out=fuT[:, m2, tc * 512:(tc + 1) * 512], in_=pacc[:, pa, :]),
                          reads=[pacc_b[pa]], writes=[fuT_b] + PT_b)
                    pa = nxt("pa", 4)
                    for kc in range(KC):
                        fw.op(pe, lambda kc=kc, pa=pa, m2=m2, tc=tc: T.matmul(
                            pacc[:, pa, :], lhsT=wbf[:, 1, kc, 256 + m2 * 128:256 + (m2 + 1) * 128], rhs=hT[:, kc, tc * 512:(tc + 1) * 512],
                            start=(kc == 0), stop=(kc == KC - 1)),
                            reads=hT_b[tc * 4:(tc + 1) * 4] + [wbf_b[1]], writes=[pacc_b[pa]], sig=(kc == KC - 1))
                    fw.op(act, lambda pa=pa, m2=m2, tc=tc: A.activation(out=yT[:, 6 + m2, tc * 512:(tc + 1) * 512], in_=pacc[:, pa, :], func=ACTF.Silu),
                          reads=[pacc_b[pa]], writes=[yT_b[6 + m2][tc]])
            wall = wbf[:, :, :, :].rearrange("p s k c -> p (s k c)").rearrange("p (k c) -> p k c", k=KC)
            for kc in range(KC):
                sl = nxt("stg", NSTG)
                if kc < 4:
                    fw.dma(s_stg[sl], stage[0:64, sl, :], wout_d[l, kc * 64:(kc + 1) * 64, :], writes=[stg_b[sl]])
                    fw.dma(s_stg[sl], stage[64:128, sl, :], wout_d[l, (4 + kc) * 64:(5 + kc) * 64, :], writes=[stg_b[sl]])
                else:
                    fw.dma(s_stg[sl], stage[:, sl, :], wout_d[l, kc * 128:(kc + 1) * 128, :], writes=[stg_b[sl]])
                fw.op(pool, lambda kc=kc, sl=sl: P.tensor_tensor(out=wall[:, kc, :], in0=stage[:, sl, :], in1=gateb[:], op=ALU.mult),
                      reads=[stg_b[sl], gate_b], writes=wbf_b)
            for i in range(NT):
                pa = nxt("pa", 4)
                for m2 in range(2):
                    fw.op(pe, lambda m2=m2, pa=pa, i=i: T.matmul(
                        pacc[:, pa, m2 * 256:(m2 + 1) * 256], lhsT=fuT[:, m2, i * 128:(i + 1) * 128],
                        rhs=ABbd[:, m2, :, :].rearrange("p a n -> p (a n)"), start=True, stop=True),
                        reads=[fuT_b, AB_b], writes=[pacc_b[pa]], sig=(m2 == 1))
                fw.op(act, lambda pa=pa, i=i: A.copy(out=UAB[:, i, :, :, :].rearrange("p m c n -> p (m c n)"), in_=pacc[:, pa, :]),
                      reads=[pacc_b[pa]], writes=[UAB_b] + QT_b)
            for half in range(2):
                pas = [[nxt("pa", 4) for _ in range(2)] for _ in range(2)]
                for a in range(NT):
                    ts = nxt("tab", NTAB)
                    fw.dma(s_tab[ts], tabv[:, ts, :, :], dft_d[a * 128:(a + 1) * 128, :, half * 1024:(half + 1) * 1024], writes=[tab_b[ts]] + hT_b)
                    for m2 in range(2):
                        for kq in range(2):
                            for ab in range(2):
                                fw.op(pe, lambda m2=m2, kq=kq, ab=ab, a=a, ts=ts, pas=pas: T.matmul(
                                    pacc[:, pas[m2][kq], :], lhsT=UAB[:, a, m2, ab, :], rhs=tabv[:, ts, ab, kq * 512:(kq + 1) * 512],
                                    start=(a == 0 and ab == 0), stop=(a == NT - 1 and ab == 1)),
                                    reads=[UAB_b, tab_b[ts]], writes=[pacc_b[pas[m2][kq]]], sig=((a == NT - 1 and ab == 1) or (m2 == 1 and kq == 1 and ab == 1)))
                for m2 in range(2):
                    for kq in range(2):
                        tcc = half * 2 + kq
                        fw.op(dve, lambda m2=m2, kq=kq, tcc=tcc, pas=pas: V.tensor_tensor(
                            out=yT[:, 6 + m2, tcc * 512:(tcc + 1) * 512], in0=pacc[:, pas[m2][kq], :], in1=yT[:, 6 + m2, tcc * 512:(tcc + 1) * 512], op=ALU.mult),
                            reads=[pacc_b[pas[m2][kq]], yT_b[6 + m2][tcc]], writes=[yT_b[6 + m2][tcc]])

            if stop <= 9:
                return
            fw.dma(s_lng, lngb[:], lng_d[l].partition_broadcast(128), writes=[lng_b])
            fw.dma(s_lnb, lnbb[:], lnb_d[l].partition_broadcast(128), writes=[lnb_b])
            def fin_a(i):
                bp = [4, 0, 2][i % 3]
                xr = nxt("xres", NXR)
                fw.dma(s_xres[xr], xres[:, xr, :], x_src[b, i * 128:(i + 1) * 128, :], reads=[ydram_b[b][i]], writes=[xres_b[xr]])
                for n in range(2):
                    for kc in range(KC):
                        fw.op(pe, lambda kc=kc, n=n: T.matmul(
                            pacc[:, bp + n, :], lhsT=yT[:, kc, i * 128:(i + 1) * 128], rhs=wall[:, kc, n * 512:(n + 1) * 512],
                            start=(kc == 0), stop=(kc == KC - 1)),
                            reads=[yT_b[kc][i // 4]] + wbf_b, writes=[pacc_b[bp + n]], sig=(kc == KC - 1))
                ss = nxt("st", 4)
                for n in range(2):
                    fw.op(dve, lambda n=n: V.scalar_tensor_tensor(
                        out=xres[:, xr, n * 512:(n + 1) * 512], in0=xres[:, xr, n * 512:(n + 1) * 512], scalar=float(ALPHA),
                        in1=pacc[:, bp + n, :], op0=ALU.mult, op1=ALU.add),
                        reads=[pacc_b[bp + n], xres_b[xr]], writes=[xres_b[xr]])
                for n in range(2):
                    fw.op(dve, lambda n=n: V.bn_stats(out=stt[:, ss, n, :], in_=xres[:, xr, n * 512:(n + 1) * 512]),
                          reads=[xres_b[xr]], writes=[stt_b[ss]])
                fw.op(dve, lambda: V.bn_aggr(out=mv[:, ss, :], in_=stt[:, ss, :, :].rearrange("p a b -> p (a b)")), reads=[stt_b[ss]], writes=[mv_b[ss]])
                fw.op(dve, lambda: V.tensor_scalar(out=sml[:, ss, 0:1], in0=mv[:, ss, 1:2], scalar1=1.0, scalar2=LN_EPS,
                                                   op0=ALU.mult, op1=ALU.add), reads=[mv_b[ss]], writes=[sml_b[ss]])
                fw.op(pool, lambda: P.tensor_tensor(out=sml[:, ss, 0:1], in0=sml[:, ss, 0:1], in1=mhalf[:, 0:1], op=ALU.pow),
                      reads=[sml_b[ss], cd_b, cm_b], writes=[sml_b[ss]])
                return xr, ss

            def fin_b(i, xr, ss):
                fw.op(dve, lambda: V.scalar_tensor_tensor(out=xres[:, xr, :], in0=xres[:, xr, :], scalar=mv[:, ss, 0:1], in1=lngb[:],
                                                          op0=ALU.subtract, op1=ALU.mult),
                      reads=[xres_b[xr], mv_b[ss], lng_b], writes=[xres_b[xr]])
                fw.op(dve, lambda: V.scalar_tensor_tensor(out=xres[:, xr, :], in0=xres[:, xr, :], scalar=sml[:, ss, 0:1], in1=lnbb[:],
                                                          op0=ALU.mult, op1=ALU.add),
                      reads=[xres_b[xr], sml_b[ss], lnb_b], writes=[xres_b[xr]])
                fw.dma(s_xres[xr], y_d[b, i * 128:(i + 1) * 128, :], xres[:, xr, :], reads=[xres_b[xr]], writes=[ydram_b[b][i]])

            pend = fin_a(0)
            for i in range(NT):
                nxt_pend = fin_a(i + 1) if i + 1 < NT else None
                fin_b(i, *pend)
                pend = nxt_pend

        for l in range(n_layers):
            for b in range(n_seq):
                if b == 0:
                    emit_mod(l)
                emit_layer(b, l, x_d if l == 0 else y_d, l == n_layers - 1)
        fw.wait_all(fw.sp, s_xres)
        build_program.stats = (fw.ninst, fw.nwait, {e.name: e.sem.count for e in (pe, act, dve, pool)})
    return nc


def _constants():
    bf = ml_dtypes.bfloat16
    t = np.arange(S)
    rows = (t // 64).astype(np.float32)
    cols = (t % 64).astype(np.float32)
    inv = (1.0 / (10000.0 ** (np.arange(0, 32, 2, dtype=np.float32) / np.float32(32)))).astype(np.float32)
    ra = (rows[:, None] * inv[None, :]).astype(np.float32)
    ca = (cols[:, None] * inv[None, :]).astype(np.float32)
    ang = np.concatenate([ra, ca], axis=1)
    cos = np.cos(ang.astype(np.float64)).astype(np.float32)
    sin = np.sin(ang.astype(np.float64)).astype(np.float32)
    rope_cos = np.ascontiguousarray(cos.reshape(NT, 128, 32).transpose(1, 0, 2))
    rope_sin = np.ascontiguousarray(sin.reshape(NT, 128, 32).transpose(1, 0, 2))
    ks = (np.outer(t, t) % S).astype(np.float64) * (2.0 * np.pi / S)
    dft_cs = np.ascontiguousarray(np.stack([np.cos(ks), -np.sin(ks)], axis=1).astype(np.float32).astype(bf))
    m = np.arange(64)
    a64 = (np.outer(m, m) % 64).astype(np.float64) * (2.0 * np.pi / 64)
    sc = 1.0 / np.sqrt(float(S) * 64.0)
    c64 = np.cos(a64) * sc
    s64 = np.sin(a64) * sc
    z = np.zeros((64, 64))
    c64bd = np.block([[c64, z], [z, c64]]).astype(np.float32)
    s64bd = np.block([[s64, z], [z, s64]]).astype(np.float32)
    def hilo(a):
        hi = a.astype(bf)
        lo = (a - hi.astype(np.float32)).astype(bf)
        return np.stack([hi, lo], axis=1)
    cs64bd = np.ascontiguousarray(np.stack([hilo(c64bd), hilo(s64bd)], axis=1))
    return {
        "ident_bf": np.eye(128, dtype=np.float32).astype(bf),
        "ident_f": np.eye(128, dtype=np.float32),
        "rope_cos": rope_cos, "rope_sin": rope_sin,
        "dft_cs": dft_cs,
        "cs64bd": cs64bd,
    }


def make_in_maps(x, c, w_ada, b_ada, w_in, q_gain, k_gain, conv_w, f_mix, w_out, ln_g, ln_b, ncores=NCORES):
    f = lambda a: np.ascontiguousarray(np.asarray(a, dtype=np.float32))
    consts = _constants()
    shared = {
        "w_ada": f(w_ada), "w_in": f(w_in), "q_gain": f(q_gain), "k_gain": f(k_gain),
        "f_mix": f(f_mix), "w_out": f(w_out), "ln_g": f(ln_g), "ln_b": f(ln_b),
        "b_adaT": f(np.asarray(b_ada).reshape(DEPTH, 24, 128).transpose(2, 0, 1)),
        "conv_wT": f(np.asarray(conv_w).reshape(DEPTH, 3, 2, 128).transpose(3, 0, 2, 1)),
    }
    shared.update(consts)
    x = np.asarray(x, dtype=np.float32)
    c = np.asarray(c, dtype=np.float32)
    maps = []
    for ci in range(ncores):
        m = dict(shared)
        m["x"] = np.ascontiguousarray(x[ci * BPC:(ci + 1) * BPC])
        m["cT"] = f(c[ci * BPC:(ci + 1) * BPC].reshape(BPC, KC, 128).transpose(2, 1, 0))
        maps.append(m)
    return maps


def kernel(x, c, w_ada, b_ada, w_in, q_gain, k_gain, conv_w, f_mix, w_out, ln_g, ln_b):
    maps = make_in_maps(x, c, w_ada, b_ada, w_in, q_gain, k_gain, conv_w, f_mix, w_out, ln_g, ln_b)
    nc = build_program()
    res = run_bass_kernel_spmd(nc, maps, core_ids=list(range(NCORES)))
    return np.concatenate([np.asarray(r["y"], dtype=np.float32) for r in res.results], axis=0)
```

```python
import contextlib
import numpy as np
import ml_dtypes
import concourse.bass as bass
import concourse.mybir as mybir
from concourse.bass_utils import run_bass_kernel_spmd

F32 = mybir.dt.float32
BF16 = mybir.dt.bfloat16
ALU = mybir.AluOpType
ACTF = mybir.ActivationFunctionType
AX = mybir.AxisListType

S = 2048
D = 1024
NT = 16
KC = 8
DEPTH = 4
IN_W = 2816
ALPHA = (2 * DEPTH) ** 0.25
RMS_EPS = 1e-6
LN_EPS = 1e-5
NCORES = 8
BPC = 2


class Buf:
    __slots__ = ("name", "w", "r")

    def __init__(self, name):
        self.name = name
        self.w = None
        self.r = []


class Sem:
    def __init__(self, h, name):
        self.h = h
        self.name = name
        self.count = 0


class Eng:
    def __init__(self, name, h, sem):
        self.name = name
        self.h = h
        self.sem = sem
        self.seen = {}


class FW:
    def __init__(self, nc, stack):
        self.nc = nc
        self.stack = stack
        self.pe = self._eng("pe", nc.tensor)
        self.act = self._eng("act", nc.scalar)
        self.dve = self._eng("dve", nc.vector)
        self.pool = self._eng("pool", nc.gpsimd)
        self.sp = self._eng("sp", nc.sync)
        self.ninst = 0
        self.nwait = 0

    def sem(self, name):
        h = self.stack.enter_context(self.nc.semaphore(name))
        return Sem(h, name)

    def _eng(self, name, h):
        return Eng(name, h, self.sem("s_" + name))

    def sbuf(self, name, shape, dt):
        return self.stack.enter_context(self.nc.sbuf_tensor(name, shape, dt))

    def psum(self, name, shape, dt):
        return self.stack.enter_context(self.nc.psum_tensor(name, shape, dt))

    def _waits(self, eng, reads, writes, is_dma=False, my_sem=None):
        need = {}

        def add(tok):
            s, v, who = tok
            if (not is_dma) and who == eng.name and eng.name == "pe":
                return
            if v > need.get(s, (None, 0))[1]:
                need[s] = (s, v)

        for b in reads:
            if b.w is not None:
                add(b.w)
        for b in writes:
            if b.w is not None and not (is_dma and b.w[0] is my_sem):
                add(b.w)
            for r in b.r:
                add(r)
        for s, v in need.values():
            if eng.seen.get(s, 0) >= v:
                continue
            eng.h.wait_ge(s.h, v)
            eng.seen[s] = v
            self.nwait += 1

    def op(self, eng, fn, reads=(), writes=(), sig=True):
        self._waits(eng, reads, writes)
        ins = fn()
        self.ninst += 1
        if sig:
            eng.sem.count += 1
            ins.then_inc(eng.sem.h, 1)
            tok = (eng.sem, eng.sem.count, eng.name)
        else:
            tok = (eng.sem, eng.sem.count + 1, eng.name)
        for b in reads:
            b.r.append(tok)
        for b in writes:
            b.w = tok
            b.r = []
        return ins

    def dma(self, sem, out, in_, reads=(), writes=(), q=None, **kw):
        q = q or self.sp
        self._waits(q, reads, writes, is_dma=True, my_sem=sem)
        ins = q.h.dma_start(out=out, in_=in_, **kw)
        ins.then_inc(sem.h, 16)
        sem.count += 16
        tok = (sem, sem.count, "dma:" + sem.name)
        for b in reads:
            b.r.append(tok)
        for b in writes:
            b.w = tok
            b.r = []
        self.ninst += 1
        return ins

    def wait_all(self, eng, sems):
        for s in sems:
            if s.count > 0 and eng.seen.get(s, 0) < s.count:
                eng.h.wait_ge(s.h, s.count)
                eng.seen[s] = s.count


def build_program(n_layers=DEPTH, n_seq=BPC, stop=99):
    nc = bass.Bass("TRN2", target_bir_lowering=False)
    dt_in = lambda name, shape, dt=F32: nc.dram_tensor(name, shape, dt, kind="ExternalInput").ap()
    x_d = dt_in("x", [BPC, S, D])
    cT_d = dt_in("cT", [128, KC, BPC])
    wada_d = dt_in("w_ada", [DEPTH, D, 3 * D])
    badaT_d = dt_in("b_adaT", [128, DEPTH, 24])
    win_d = dt_in("w_in", [DEPTH, D, IN_W])
    qg_d = dt_in("q_gain", [DEPTH, 64])
    kg_d = dt_in("k_gain", [DEPTH, 64])
    cwT_d = dt_in("conv_wT", [128, DEPTH, 2, 3])
    fmix_d = dt_in("f_mix", [DEPTH, 4, 64, 64])
    wout_d = dt_in("w_out", [DEPTH, D, D])
    lng_d = dt_in("ln_g", [DEPTH, D])
    lnb_d = dt_in("ln_b", [DEPTH, D])
    identb_d = dt_in("ident_bf", [128, 128], BF16)
    identf_d = dt_in("ident_f", [128, 128])
    ropec_d = dt_in("rope_cos", [128, NT, 32])
    ropes_d = dt_in("rope_sin", [128, NT, 32])
    dft_d = dt_in("dft_cs", [S, 2, S], BF16)
    csbd_d = dt_in("cs64bd", [128, 2, 2, 128], BF16)
    y_d = nc.dram_tensor("y", [BPC, S, D], F32, kind="ExternalOutput").ap()

    with contextlib.ExitStack() as st:
        fw = FW(nc, st)
        pe, act, dve, pool = fw.pe, fw.act, fw.dve, fw.pool
        V, A, P, T = nc.vector, nc.scalar, nc.gpsimd, nc.tensor

        hT = fw.sbuf("hT", [128, KC, S], BF16)
        yT = fw.sbuf("yT", [128, 8, S], BF16)
        QT = fw.sbuf("QT", [128, 4, S], BF16)
        KTp = fw.sbuf("KTp", [128, 2, S], BF16)
        Vaug = fw.sbuf("Vaug", [128, NT, 2, 128], BF16)
        NPT = 4
        PT = fw.sbuf("PT", [128, NPT, 2, 512], BF16)
        fuT = PT[:, :, :, :].rearrange("p a b c -> p (a b c)").rearrange("p (m s) -> p m s", m=2)
        wbf = fw.sbuf("wbf", [128, 2, KC, 512], BF16)
        NSTG = 3
        stage = fw.sbuf("stage", [128, NSTG, 1024], F32)
        NXIN = 3
        xin = fw.sbuf("xin", [128, NXIN, D], F32)
        NXR = 3
        xres = fw.sbuf("xres", [128, NXR, D], F32)
        lngb = fw.sbuf("lngb", [128, D], F32)
        lnbb = fw.sbuf("lnbb", [128, D], F32)
        gateb = fw.sbuf("gateb", [128, D], F32)
        ropec = fw.sbuf("ropec", [128, NT, 32], F32)
        ropes = fw.sbuf("ropes", [128, NT, 32], F32)
        zt = xin[:, :, :].rearrange("p a d -> p (a d)")[:, 0:S + 2]
        xh = fw.sbuf("xh", [128, 2, D], BF16)
        NWK = 6
        wkf = fw.sbuf("wk", [128, NWK, 640], F32)
        qrb = fw.sbuf("qrb", [128, 2, 640], BF16)
        identb = fw.sbuf("identb", [128, 128], BF16)
        identf = fw.sbuf("identf", [128, 128], F32)
        onesb = fw.sbuf("onesb", [128, 128], BF16)
        dg = fw.sbuf("dg", [128, 2, 128], F32)
        dgb = fw.sbuf("dgb", [128, 2, 2, 128], BF16)
        csbd = fw.sbuf("csbd", [128, 2, 2, 128], BF16)
        fm = fw.sbuf("fm", [128, 2, 64], F32)
        fmb = fw.sbuf("fmb", [128, 2, 2, 64], BF16)
        cTb = fw.sbuf("cTb", [128, 2, KC, BPC], BF16)
        whi = fw.sbuf("whi", [128, 2, 1024], BF16)
        ABbd = fw.sbuf("ABbd", [128, 2, 2, 128], BF16)
        cTs = fw.sbuf("cTs", [128, KC, BPC], F32)
        badaT = fw.sbuf("badaT", [128, DEPTH, 24], F32)
        modT = fw.sbuf("modT", [128, 24, BPC], F32)
        s1T = fw.sbuf("s1T", [128, 8, BPC], F32)
        g1T = fw.sbuf("g1T", [128, 8, BPC], F32)
        cwT = fw.sbuf("cwT", [128, DEPTH, 2, 3], F32)
        qgb = fw.sbuf("qgb", [128, 64], F32)
        kgb = fw.sbuf("kgb", [128, 64], F32)
        mhalf = fw.sbuf("mhalf", [128, 16], F32)
        stt = fw.sbuf("stt", [128, 4, 2, 6], F32)
        mv = fw.sbuf("mv", [128, 4, 2], F32)
        sml = fw.sbuf("sml", [128, 4, 32], F32)

        pacc = fw.psum("pacc", [128, 8, 512], F32)
        ptrv = [pacc[:, 6, :].bitcast(BF16), pacc[:, 7, :].bitcast(BF16)]

        B = lambda n: Buf(n)
        hT_b = [B("hT%d" % i) for i in range(NT)]
        NTAB = 6
        tab_b = [B("tab%d" % i) for i in range(NTAB)]
        yT_b = [[B("yT%d_%d" % (c, t)) for t in range(4)] for c in range(8)]
        QT_b = [B("QT%d" % j) for j in range(4)]
        UAB_b = B("UAB")
        KT_b = B("KTp")
        Va_b = B("Vaug")
        PT_b = [B("PT%d" % i) for i in range(NPT)]
        fuT_b = B("fuT")
        wbf_b = [B("wbf0"), B("wbf1")]
        stg_b = [B("stg%d" % i) for i in range(NSTG)]
        xin_b = [B("xin%d" % i) for i in range(NXIN)]
        xres_b = [B("xres%d" % i) for i in range(NXR)]
        lng_b, lnb_b, gate_b = B("lng"), B("lnb"), B("gate")
        rope_b, z_b = B("rope"), B("z")
        xh_b = [B("xh0"), B("xh1")]
        wk_b = [B("wk%d" % i) for i in range(NWK)]
        qrb_b = [B("qrb0"), B("qrb1")]
        cd_b, cm_b = B("constd"), B("constm")
        dg_b = [B("dg0"), B("dg1")]
        fm_b, AB_b = B("fm"), B("ABbd")
        whi_b = [B("whi0"), B("whi1")]
        cTb_b = B("cTb")
        mod_b, s1_b, g1_b = B("modT"), B("s1T"), B("g1T")
        gain_b = B("gains")
        stt_b = [B("stt%d" % i) for i in range(4)]
        mv_b = [B("mv%d" % i) for i in range(4)]
        sml_b = [B("sml%d" % i) for i in range(4)]
        pacc_b = [B("pacc%d" % i) for i in range(8)]
        ptr_b = [pacc_b[6], pacc_b[7]]
        ydram_b = [[B("yd%d_%d" % (b, i)) for i in range(NT)] for b in range(BPC)]

        s_const = fw.sem("d_const")
        s_stg = [fw.sem("d_stg%d" % i) for i in range(NSTG)]
        s_xin = [fw.sem("d_xin%d" % i) for i in range(NXIN)]
        s_xres = [fw.sem("d_xres%d" % i) for i in range(NXR)]
        s_tab = [fw.sem("d_tab%d" % i) for i in range(NTAB)]
        s_lng, s_lnb, s_gain, s_fm = fw.sem("d_lng"), fw.sem("d_lnb"), fw.sem("d_gain"), fw.sem("d_fm")

        tabv = hT[:, 0:NTAB, :].rearrange("p a (c k) -> p a c k", c=2)
        UAB = QT[:, :, :].rearrange("p j (a m c n) -> p (j a) m c n", a=4, m=2, c=2)

        for (dst, src) in [(identb, identb_d), (identf, identf_d), (ropec, ropec_d), (ropes, ropes_d),
                           (csbd, csbd_d), (cTs, cT_d), (badaT, badaT_d), (cwT, cwT_d)]:
            fw.dma(s_const, dst[:], src, writes=[cd_b, rope_b])
        fw.op(pool, lambda: P.memset(Vaug[:], 1.0), writes=[Va_b])
        fw.op(pool, lambda: P.memset(KTp[:], 0.0), writes=[KT_b])
        fw.op(pool, lambda: P.memset(mhalf[:], -0.5), writes=[cm_b])
        fw.op(pool, lambda: P.memset(onesb[:], 1.0), writes=[cm_b])
        fw.op(dve, lambda: V.tensor_copy(out=cTb[:, 0, :, :], in_=cTs[:]), reads=[cd_b], writes=[cTb_b])
        fw.op(dve, lambda: V.tensor_tensor(out=cTb[:, 1, :, :], in0=cTs[:], in1=cTb[:, 0, :, :], op=ALU.subtract), reads=[cd_b, cTb_b], writes=[cTb_b])
        fw.op(pool, lambda: P.memset(ABbd[:], 0.0), writes=[AB_b])

        rr = {"stg": 0, "xin": 0, "xres": 0, "tab": 0, "wk": 0, "whi": 0, "pp": 0, "st": 0, "pa": 0, "ptr": 0, "pt": 0, "xh": 0, "qrb": 0, "dg": 0}

        def nxt(key, n):
            v = rr[key]
            rr[key] = (v + 1) % n
            return v

        def rstd_from(var_ap, out_ap, slot_b, scale, eps):
            n = var_ap.shape[-1]
            fw.op(dve, lambda: V.tensor_scalar(out=out_ap, in0=var_ap, scalar1=scale, scalar2=eps,
                                               op0=ALU.mult, op1=ALU.add), reads=[slot_b], writes=[slot_b])
            fw.op(pool, lambda: P.tensor_tensor(out=out_ap, in0=out_ap, in1=mhalf[:, 0:n], op=ALU.pow),
                  reads=[slot_b, cd_b, cm_b], writes=[slot_b])

        def emit_mod(l):
            pa = 5
            for oc in range(24):
                sl = nxt("stg", NSTG)
                fw.dma(s_stg[sl], stage[:, sl, :].rearrange("p (k c) -> p k c", k=KC),
                       wada_d[l, :, oc * 128:(oc + 1) * 128].rearrange("(k p) c -> p k c", p=128),
                       writes=[stg_b[sl]])
                ws = nxt("whi", 2)
                if oc % 3 == 0:
                    fw.op(dve, lambda sl=sl, ws=ws: V.tensor_copy(out=whi[:, ws, :], in_=stage[:, sl, :]), reads=[stg_b[sl]], writes=[whi_b[ws]])
                elif oc % 3 == 1:
                    fw.op(act, lambda sl=sl, ws=ws: A.copy(out=whi[:, ws, :], in_=stage[:, sl, :]), reads=[stg_b[sl]], writes=[whi_b[ws]])
                else:
                    fw.op(pool, lambda sl=sl, ws=ws: P.tensor_copy(out=whi[:, ws, :], in_=stage[:, sl, :]), reads=[stg_b[sl]], writes=[whi_b[ws]])
                for kc in range(KC):
                    for hl in range(2):
                        fw.op(pe, lambda kc=kc, ws=ws, oc=oc, hl=hl: T.matmul(
                            pacc[:, pa, oc * 2:oc * 2 + 2], lhsT=whi[:, ws, kc * 128:(kc + 1) * 128],
                            rhs=cTb[:, hl, kc, :], start=(kc == 0 and hl == 0), stop=(kc == KC - 1 and hl == 1)),
                            reads=[whi_b[ws], cTb_b], writes=[pacc_b[pa]], sig=(kc == KC - 1 and hl == 1))
            fw.op(dve, lambda: V.tensor_tensor(
                out=modT[:], in0=pacc[:, pa, 0:48].rearrange("p (o b) -> p o b", b=BPC),
                in1=badaT[:, l, :].unsqueeze(2).to_broadcast([128, 24, BPC]), op=ALU.add),
                reads=[pacc_b[pa], cd_b, cm_b], writes=[mod_b])
            fw.op(dve, lambda: V.tensor_scalar(out=s1T[:], in0=modT[:, 8:16, :], scalar1=1.0, scalar2=None, op0=ALU.add),
                  reads=[mod_b], writes=[s1_b])
            fw.op(dve, lambda: V.tensor_scalar(out=g1T[:], in0=modT[:, 16:24, :], scalar1=1.0, scalar2=None, op0=ALU.add),
                  reads=[mod_b], writes=[g1_b])
            fw.dma(s_gain, qgb[:], qg_d[l].partition_broadcast(128), writes=[gain_b])
            fw.dma(s_gain, kgb[:], kg_d[l].partition_broadcast(128), writes=[gain_b])
            fw.dma(s_fm, fm[:], fmix_d[l].rearrange("(a g) m d -> (g m) a d", a=2), writes=[fm_b])
            fw.op(dve, lambda: V.tensor_copy(out=fmb[:, :, 0, :], in_=fm[:]), reads=[fm_b], writes=[fm_b])
            fw.op(dve, lambda: V.tensor_tensor(out=fmb[:, :, 1, :], in0=fm[:], in1=fmb[:, :, 0, :], op=ALU.subtract), reads=[fm_b], writes=[fm_b])
            for m2 in range(2):
                for ab in range(2):
                    for t3, (th, fh) in enumerate(((0, 0), (0, 1), (1, 0))):
                        fw.op(pe, lambda m2=m2, ab=ab, th=th, fh=fh, t3=t3: T.matmul(
                            pacc[:, pa, 64 + (m2 * 2 + ab) * 64: 64 + (m2 * 2 + ab + 1) * 64], lhsT=csbd[:, ab, th, :], rhs=fmb[:, m2, fh, :],
                            start=(t3 == 0), stop=(t3 == 2)), reads=[fm_b, cd_b, cm_b], writes=[pacc_b[pa]], sig=(t3 == 2))
            for m2 in range(2):
                for ab in range(2):
                    c0 = 64 + (m2 * 2 + ab) * 64
                    fw.op(dve, lambda m2=m2, ab=ab, c0=c0: V.tensor_copy(out=ABbd[0:64, m2, ab, 0:64], in_=pacc[0:64, pa, c0:c0 + 64]),
                          reads=[pacc_b[pa]], writes=[AB_b])
                    fw.op(dve, lambda m2=m2, ab=ab, c0=c0: V.tensor_copy(out=ABbd[64:128, m2, ab, 64:128], in_=pacc[64:128, pa, c0:c0 + 64]),
                          reads=[pacc_b[pa]], writes=[AB_b])

        def load_wgroup(l, slot, src0, ncol_src, casts):
            for kc in range(KC):
                sl = nxt("stg", NSTG)
                fw.dma(s_stg[sl], stage[:, sl, 0:ncol_src], win_d[l, kc * 128:(kc + 1) * 128, src0:src0 + ncol_src],
                       writes=[stg_b[sl]])
                for cf in casts:
                    o_ap, i_ap = cf(wbf[:, slot, kc, :], stage[:, sl, :])
                    fw.op(pool, lambda o_ap=o_ap, i_ap=i_ap: P.tensor_copy(out=o_ap, in_=i_ap),
                          reads=[stg_b[sl]], writes=[wbf_b[slot]])

        perm_cast = lambda o, i: (o[:, 0:512].rearrange("p (j h d) -> p j h d", j=4, h=2),
                                  i[:, 0:512].rearrange("p (h j d) -> p j h d", h=2, j=4))

        def rope(src, dst, i, H, sb, wkA, wkB, src_bufs, dst_bufs):
            sv = src.rearrange("p (h a b d) -> p h a b d", h=H, a=2, b=2)
            dv = dst.rearrange("p (h a b d) -> p h a b d", h=H, a=2, b=2)
            cs = ropec[:, i, :].rearrange("p (a d) -> p a d", a=2).unsqueeze(1).to_broadcast([128, H, 2, 16])
            sn = ropes[:, i, :].rearrange("p (a d) -> p a d", a=2).unsqueeze(1).to_broadcast([128, H, 2, 16])
            n = H * 32
            t1 = wkA[:, 0:n].rearrange("p (h a d) -> p h a d", h=H, a=2)
            t2 = wkA[:, n:2 * n].rearrange("p (h a d) -> p h a d", h=H, a=2)
            t3 = wkB[:, 0:n].rearrange("p (h a d) -> p h a d", h=H, a=2)
            t4 = wkB[:, n:2 * n].rearrange("p (h a d) -> p h a d", h=H, a=2)
            x0 = sv[:, :, :, 0, :]
            x1 = sv[:, :, :, 1, :]
            wa, wb = sb
            fw.op(dve, lambda: V.tensor_tensor(out=t1, in0=x0, in1=cs, op=ALU.mult), reads=src_bufs + [rope_b], writes=[wa])
            fw.op(dve, lambda: V.tensor_tensor(out=t2, in0=x1, in1=sn, op=ALU.mult), reads=src_bufs + [rope_b], writes=[wa])
            fw.op(dve, lambda: V.tensor_tensor(out=t3, in0=x1, in1=cs, op=ALU.mult), reads=src_bufs + [rope_b], writes=[wb])
            fw.op(dve, lambda: V.tensor_tensor(out=t4, in0=x0, in1=sn, op=ALU.mult), reads=src_bufs + [rope_b], writes=[wb])
            fw.op(dve, lambda: V.tensor_tensor(out=dv[:, :, :, 0, :], in0=t1, in1=t2, op=ALU.subtract), reads=[wa], writes=dst_bufs)
            fw.op(dve, lambda: V.tensor_tensor(out=dv[:, :, :, 1, :], in0=t3, in1=t4, op=ALU.add), reads=[wb], writes=dst_bufs)

        def emit_layer(b, l, x_src, last):
            for kc in range(KC):
                ds = nxt("dg", 2)
                pa = 4 + (kc // 4)
                fw.op(dve, lambda kc=kc, ds=ds: V.tensor_scalar(out=dg[:, ds, :], in0=identf[:], scalar1=g1T[:, kc, b:b + 1],
                                                              scalar2=None, op0=ALU.mult),
                      reads=[cd_b, cm_b, g1_b], writes=[dg_b[ds]])
                fw.op(dve, lambda ds=ds: V.tensor_copy(out=dgb[:, ds, 0, :], in_=dg[:, ds, :]), reads=[dg_b[ds]], writes=[dg_b[ds]])
                fw.op(dve, lambda ds=ds: V.tensor_tensor(out=dgb[:, ds, 1, :], in0=dg[:, ds, :], in1=dgb[:, ds, 0, :], op=ALU.subtract), reads=[dg_b[ds]], writes=[dg_b[ds]])
                for hl in range(2):
                    fw.op(pe, lambda kc=kc, ds=ds, pa=pa, hl=hl: T.matmul(pacc[:, pa, (kc % 4) * 128:(kc % 4 + 1) * 128], lhsT=onesb[:], rhs=dgb[:, ds, hl, :],
                                                                        start=(hl == 0), stop=(hl == 1)),
                          reads=[dg_b[ds], cd_b, cm_b], writes=[pacc_b[pa]], sig=(hl == 1))
            for h2 in range(2):
                fw.op(act, lambda h2=h2: A.copy(out=gateb[:, h2 * 512:(h2 + 1) * 512], in_=pacc[:, 4 + h2, :]),
                      reads=[pacc_b[4 + h2]], writes=[gate_b])

            if stop <= 1:
                return
            load_wgroup(l, 0, 512, 256, [lambda o, i: (o[:, 0:256], i[:, 0:256])])
            load_wgroup(l, 1, 0, 512, [perm_cast])

            if stop <= 1.5:
                return
            def ln1_a(i):
                xs_ = nxt("xin", NXIN)
                fw.dma(s_xin[xs_], xin[:, xs_, :], x_src[b, i * 128:(i + 1) * 128, :],
                       reads=[ydram_b[b][i]], writes=[xin_b[xs_]] + ([z_b] if i < NXIN else []))
                ss = nxt("st", 4)
                for h2 in range(2):
                    fw.op(dve, lambda h2=h2: V.bn_stats(out=stt[:, ss, h2, :], in_=xin[:, xs_, h2 * 512:(h2 + 1) * 512]),
                          reads=[xin_b[xs_]], writes=[stt_b[ss]])
                fw.op(dve, lambda: V.bn_aggr(out=mv[:, ss, :], in_=stt[:, ss, :, :].rearrange("p a b -> p (a b)")), reads=[stt_b[ss]], writes=[mv_b[ss]])
                fw.op(dve, lambda: V.tensor_scalar(out=sml[:, ss, 0:1], in0=mv[:, ss, 1:2], scalar1=1.0, scalar2=LN_EPS,
                                                   op0=ALU.mult, op1=ALU.add), reads=[mv_b[ss]], writes=[sml_b[ss]])
                fw.op(pool, lambda: P.tensor_tensor(out=sml[:, ss, 0:1], in0=sml[:, ss, 0:1], in1=mhalf[:, 0:1], op=ALU.pow),
                      reads=[sml_b[ss], cd_b, cm_b], writes=[sml_b[ss]])
                return xs_, ss

            def ln1_b(i, xs_, ss):
                hs = nxt("xh", 2)
                fw.op(dve, lambda: V.tensor_scalar(
                    out=xh[:, hs, :], in0=xin[:, xs_, :], scalar1=mv[:, ss, 0:1], scalar2=sml[:, ss, 0:1],
                    op0=ALU.subtract, op1=ALU.mult), reads=[xin_b[xs_], mv_b[ss], sml_b[ss]], writes=[xh_b[hs]])
                pb = nxt("ptr", 2)
                for kc in range(KC):
                    fw.op(pe, lambda kc=kc: T.transpose(ptrv[pb][:, kc * 128:(kc + 1) * 128], xh[:, hs, kc * 128:(kc + 1) * 128], identb[:]),
                          reads=[xh_b[hs], cd_b, cm_b], writes=[ptr_b[pb]], sig=(kc == KC - 1))
                for kc in range(KC):
                    if True:
                        fw.op(act, lambda kc=kc: A.activation(
                            out=hT[:, kc, i * 128:(i + 1) * 128], in_=ptrv[pb][:, kc * 128:(kc + 1) * 128], func=ACTF.Identity,
                            scale=s1T[:, kc, b:b + 1], bias=modT[:, kc, b:b + 1]),
                            reads=[ptr_b[pb], s1_b, mod_b], writes=[hT_b[i]] + (tab_b if kc == 0 else []))
                    else:
                        fw.op(dve, lambda kc=kc: V.tensor_scalar(
                            out=hT[:, kc, i * 128:(i + 1) * 128], in0=ptrv[pb][:, kc * 128:(kc + 1) * 128],
                            scalar1=s1T[:, kc, b:b + 1], scalar2=modT[:, kc, b:b + 1], op0=ALU.mult, op1=ALU.add),
                            reads=[ptr_b[pb], s1_b, mod_b], writes=[hT_b[i]])

            pend = ln1_a(0)
            for i in range(NT):
                nxt_pend = ln1_a(i + 1) if i + 1 < NT else None
                ln1_b(i, *pend)
                pend = nxt_pend

            if stop <= 2:
                return
            HQK = 10

            def qk_a1(i):
                pp = nxt("pp", 3)
                bq, bk = 2 * pp, 2 * pp + 1
                for kc in range(KC):
                    fw.op(pe, lambda kc=kc: T.matmul(pacc[:, bq, :], lhsT=hT[:, kc, i * 128:(i + 1) * 128], rhs=wbf[:, 1, kc, :],
                                                     start=(kc == 0), stop=(kc == KC - 1)),
                          reads=[hT_b[i], wbf_b[1]], writes=[pacc_b[bq]], sig=(kc == KC - 1))
                for kc in range(KC):
                    fw.op(pe, lambda kc=kc: T.matmul(pacc[:, bk, 0:256], lhsT=hT[:, kc, i * 128:(i + 1) * 128], rhs=wbf[:, 0, kc, 0:256],
                                                     start=(kc == 0), stop=(kc == KC - 1)),
                          reads=[hT_b[i], wbf_b[0]], writes=[pacc_b[bk]], sig=(kc == KC - 1))
                fw.op(act, lambda: A.copy(out=Vaug[:, i, 0, 0:64], in_=pacc[:, bk, 128:192]), reads=[pacc_b[bk]], writes=[Va_b])
                fw.op(act, lambda: A.copy(out=Vaug[:, i, 1, 64:128], in_=pacc[:, bk, 192:256]), reads=[pacc_b[bk]], writes=[Va_b])
                n = HQK * 64
                src = pacc[:, bq:bq + 2, :].rearrange("p a c -> p (a c)")[:, 0:n]
                srcb = [pacc_b[bq], pacc_b[bk]]
                wA = nxt("wk", NWK)
                fw.op(act, lambda: A.activation(out=wkf[:, wA, 0:n], in_=src, func=ACTF.Square), reads=srcb, writes=[wk_b[wA]])
                return bq, bk, wA

            def qk_a2(i, bq, bk, wA):
                n = HQK * 64
                wB = nxt("wk", NWK); wC = nxt("wk", NWK)
                ss = nxt("st", 4)
                fw.op(dve, lambda: V.tensor_reduce(out=sml[:, ss, 0:HQK], in_=wkf[:, wA, 0:n].rearrange("p (h d) -> p h d", h=HQK),
                                                   axis=AX.X, op=ALU.add), reads=[wk_b[wA]], writes=[sml_b[ss]])
                rstd_from(sml[:, ss, 0:HQK], sml[:, ss, 0:HQK], sml_b[ss], 1.0 / 64, RMS_EPS)
                fw.op(dve, lambda: V.tensor_tensor(out=wkf[:, wB, 0:512].rearrange("p (h d) -> p h d", h=8), in0=pacc[:, bq, :].rearrange("p (h d) -> p h d", h=8),
                                                   in1=qgb[:].unsqueeze(1).to_broadcast([128, 8, 64]), op=ALU.mult),
                      reads=[pacc_b[bq], gain_b], writes=[wk_b[wB]])
                fw.op(dve, lambda: V.tensor_tensor(out=wkf[:, wB, 512:640].rearrange("p (h d) -> p h d", h=2), in0=pacc[:, bk, 0:128].rearrange("p (h d) -> p h d", h=2),
                                                   in1=kgb[:].unsqueeze(1).to_broadcast([128, 2, 64]), op=ALU.mult),
                      reads=[pacc_b[bk], gain_b], writes=[wk_b[wB]])
                rope(wkf[:, wB, 0:n], wkf[:, wB, 0:n], i, HQK, (wk_b[wC], wk_b[wA]), wkf[:, wC, :], wkf[:, wA, :], [wk_b[wB]], [wk_b[wB]])
                return wB, ss

            def qk_b(i, wB, ss):
                n = HQK * 64
                qs = nxt("qrb", 2)
                fw.op(dve, lambda: V.tensor_tensor(out=qrb[:, qs, 0:n].rearrange("p (h d) -> p h d", h=HQK), in0=wkf[:, wB, 0:n].rearrange("p (h d) -> p h d", h=HQK),
                                                   in1=sml[:, ss, 0:HQK].unsqueeze(2).to_broadcast([128, HQK, 64]), op=ALU.mult),
                      reads=[wk_b[wB], sml_b[ss]], writes=[qrb_b[qs]])
                pb = nxt("ptr", 2)
                for j in range(5):
                    fw.op(pe, lambda j=j: T.transpose(ptrv[pb][:, j * 128:(j + 1) * 128], qrb[:, qs, j * 128:(j + 1) * 128], identb[:]),
                          reads=[qrb_b[qs], cd_b, cm_b], writes=[ptr_b[pb]], sig=(j == 4))
                fw.op(act, lambda: A.copy(out=QT[:, :, i * 128:(i + 1) * 128], in_=ptrv[pb][:, 0:512].rearrange("p (j t) -> p j t", j=4)),
                      reads=[ptr_b[pb]], writes=QT_b + [UAB_b])
                fw.op(act, lambda: A.copy(out=KTp[0:64, 0, i * 128:(i + 1) * 128], in_=ptrv[pb][0:64, 512:640]), reads=[ptr_b[pb]], writes=[KT_b])
                fw.op(act, lambda: A.copy(out=KTp[64:128, 1, i * 128:(i + 1) * 128], in_=ptrv[pb][64:128, 512:640]), reads=[ptr_b[pb]], writes=[KT_b])

            a1 = {0: qk_a1(0), 1: qk_a1(1)}
            a2 = {0: qk_a2(0, *a1[0])}
            for i in range(NT):
                if i + 2 < NT:
                    a1[i + 2] = qk_a1(i + 2)
                if i + 1 < NT:
                    a2[i + 1] = qk_a2(i + 1, *a1[i + 1])
                qk_b(i, *a2[i])
            load_wgroup(l, 0, 768, 512, [perm_cast])
            if stop <= 4:
                return
            for j in range(4):
                for tc in range(4):
                    pa = nxt("pa", 4)
                    for kc in range(KC):
                        fw.op(pe, lambda kc=kc, pa=pa, j=j, tc=tc: T.matmul(
                            pacc[:, pa, :], lhsT=wbf[:, 0, kc, j * 128:(j + 1) * 128], rhs=hT[:, kc, tc * 512:(tc + 1) * 512],
                            start=(kc == 0), stop=(kc == KC - 1)),
                            reads=hT_b[tc * 4:(tc + 1) * 4] + [wbf_b[0]], writes=[pacc_b[pa]], sig=(kc == KC - 1))
                    fw.op(act, lambda pa=pa, j=j, tc=tc: A.activation(out=yT[:, j, tc * 512:(tc + 1) * 512], in_=pacc[:, pa, :], func=ACTF.Silu),
                          reads=[pacc_b[pa]], writes=[yT_b[j][tc]])

            if stop <= 5:
                return
            conv_cast = lambda m: (lambda o, i: (o[:, 0:512].rearrange("p (g c) -> p g c", g=4),
                                                 i[:, 0:1024].rearrange("p (g m c) -> p g m c", g=4, m=2)[:, :, m, :]))
            load_wgroup(l, 1, 1280, 1024, [conv_cast(0)])
            load_wgroup(l, 0, 1280, 1024, [conv_cast(1)])

            if stop <= 6:
                return
            for j in range(4):
                for qc in range(4):
                    steps = []

                    def qk(kt, sb_):
                        for kv in range(2):
                            fw.op(pe, lambda kv=kv, kt=kt, sb_=sb_: T.matmul(
                                pacc[:, SBASE[sb_] + kv, :], lhsT=KTp[:, kv, kt * 128:(kt + 1) * 128], rhs=QT[:, j, qc * 512:(qc + 1) * 512],
                                start=True, stop=True), reads=[KT_b, QT_b[j]], writes=[pacc_b[SBASE[sb_] + kv]])

                    def ex(kt, sb_, ps):
                        fw.op(act, lambda sb_=sb_, ps=ps: A.activation(out=PT[:, ps, :, :], in_=pacc[:, SBASE[sb_]:SBASE[sb_] + 2, :], func=ACTF.Exp, scale=0.125),
                              reads=[pacc_b[SBASE[sb_]], pacc_b[SBASE[sb_] + 1]], writes=[PT_b[ps], fuT_b])

                    def pv(kt, ps):
                        for kv in range(2):
                            fw.op(pe, lambda kv=kv, kt=kt, ps=ps: T.matmul(
                                pacc[:, 4 + kv, :], lhsT=Vaug[:, kt, kv, :], rhs=PT[:, ps, kv, :], start=(kt == 0), stop=(kt == NT - 1)),
                                reads=[Va_b, PT_b[ps]], writes=[pacc_b[4 + kv]], sig=(kt == NT - 1))

                    SBASE = [0, 2, 6]
                    qk(0, 0)
                    qk(1, 1)
                    for kt in range(NT):
                        ps = nxt("pt", NPT)
                        ex(kt, kt % 3, ps)
                        if kt + 2 < NT:
                            qk(kt + 2, (kt + 2) % 3)
                        pv(kt, ps)
                    w0 = nxt("wk", NWK); w1 = nxt("wk", NWK); w2 = nxt("wk", NWK)
                    cols = slice(qc * 512, (qc + 1) * 512)
                    fw.op(dve, lambda w0=w0: V.tensor_copy(out=wkf[:, w0, 0:512], in_=pacc[:, 4, :]), reads=[pacc_b[4]], writes=[wk_b[w0]])
                    fw.op(dve, lambda w1=w1: V.tensor_copy(out=wkf[:, w1, 0:512], in_=pacc[:, 5, :]), reads=[pacc_b[5]], writes=[wk_b[w1]])
                    fw.op(dve, lambda w0=w0, w2=w2: V.reciprocal(out=wkf[0:64, w2, 0:512], in_=wkf[64:128, w0, 0:512]), reads=[wk_b[w0]], writes=[wk_b[w2]])
                    fw.op(dve, lambda w0=w0, w2=w2: V.tensor_tensor(out=wkf[0:64, w0, 0:512], in0=wkf[0:64, w0, 0:512], in1=wkf[0:64, w2, 0:512], op=ALU.mult),
                          reads=[wk_b[w0], wk_b[w2]], writes=[wk_b[w0]])
                    fw.op(pool, lambda w0=w0: P.tensor_tensor(out=yT[0:64, j, cols], in0=yT[0:64, j, cols], in1=wkf[0:64, w0, 0:512], op=ALU.mult),
                          reads=[wk_b[w0], yT_b[j][qc]], writes=[yT_b[j][qc]])
                    fw.op(dve, lambda w1=w1, w2=w2: V.reciprocal(out=wkf[64:128, w2, 0:512], in_=wkf[0:64, w1, 0:512]), reads=[wk_b[w1]], writes=[wk_b[w2]])
                    fw.op(dve, lambda w1=w1, w2=w2: V.tensor_tensor(out=wkf[64:128, w1, 0:512], in0=wkf[64:128, w1, 0:512], in1=wkf[64:128, w2, 0:512], op=ALU.mult),
                          reads=[wk_b[w1], wk_b[w2]], writes=[wk_b[w1]])
                    fw.op(pool, lambda w1=w1: P.tensor_tensor(out=yT[64:128, j, cols], in0=yT[64:128, j, cols], in1=wkf[64:128, w1, 0:512], op=ALU.mult),
                          reads=[wk_b[w1], yT_b[j][qc]], writes=[yT_b[j][qc]])

            if stop <= 7:
                return
            fw.op(pool, lambda: P.memset(zt[:, 0:1], 0.0), writes=[z_b] + xin_b)
            fw.op(pool, lambda: P.memset(zt[:, S + 1:S + 2], 0.0), writes=[z_b] + xin_b)
            for m in range(2):
                slot = 1 - m
                for tc in range(4):
                    pas = [nxt("pa", 4) for _ in range(4)]
                    for g in range(4):
                        for kc in range(KC):
                            fw.op(pe, lambda kc=kc, g=g, tc=tc, slot=slot, pas=pas: T.matmul(
                                pacc[:, pas[g], :], lhsT=wbf[:, slot, kc, g * 128:(g + 1) * 128], rhs=hT[:, kc, tc * 512:(tc + 1) * 512],
                                start=(kc == 0), stop=(kc == KC - 1)),
                                reads=hT_b[tc * 4:(tc + 1) * 4] + [wbf_b[slot]], writes=[pacc_b[pas[g]]], sig=(kc == KC - 1))
                    w0 = nxt("wk", NWK); w1 = nxt("wk", NWK)
                    fw.op(act, lambda w0=w0, pas=pas: A.copy(out=wkf[:, w0, 0:512], in_=pacc[:, pas[2], :]), reads=[pacc_b[pas[2]]], writes=[wk_b[w0]])
                    fw.op(dve, lambda w0=w0, pas=pas, tc=tc: V.tensor_tensor(out=zt[:, 1 + tc * 512:1 + (tc + 1) * 512], in0=pacc[:, pas[0], :], in1=wkf[:, w0, 0:512], op=ALU.mult),
                          reads=[pacc_b[pas[0]], wk_b[w0]], writes=[z_b])
                    fw.op(act, lambda w1=w1, pas=pas: A.activation(out=wkf[:, w1, 0:512], in_=pacc[:, pas[3], :], func=ACTF.Silu), reads=[pacc_b[pas[3]]], writes=[wk_b[w1]])
                    fw.op(dve, lambda w1=w1, pas=pas, tc=tc, m=m: V.tensor_tensor(out=yT[:, 4 + m, tc * 512:(tc + 1) * 512], in0=pacc[:, pas[1], :], in1=wkf[:, w1, 0:512], op=ALU.mult),
                          reads=[pacc_b[pas[1]], wk_b[w1]], writes=[yT_b[4 + m][tc]])
                for tc in range(4):
                    w0 = nxt("wk", NWK)
                    c0 = 1 + tc * 512
                    fw.op(dve, lambda w0=w0, c0=c0, m=m: V.tensor_scalar(out=wkf[:, w0, 0:512], in0=zt[:, c0:c0 + 512], scalar1=cwT[:, l, m, 1:2], scalar2=None, op0=ALU.mult),
                          reads=[z_b, cd_b, cm_b], writes=[wk_b[w0]])
                    fw.op(dve, lambda w0=w0, c0=c0, m=m: V.scalar_tensor_tensor(out=wkf[:, w0, 0:512], in0=zt[:, c0 - 1:c0 + 511], scalar=cwT[:, l, m, 0:1], in1=wkf[:, w0, 0:512],
                                                                           op0=ALU.mult, op1=ALU.add), reads=[z_b, cd_b, cm_b, wk_b[w0]], writes=[wk_b[w0]])
                    fw.op(dve, lambda w0=w0, c0=c0, m=m: V.scalar_tensor_tensor(out=wkf[:, w0, 0:512], in0=zt[:, c0 + 1:c0 + 513], scalar=cwT[:, l, m, 2:3], in1=wkf[:, w0, 0:512],
                                                                           op0=ALU.mult, op1=ALU.add), reads=[z_b, cd_b, cm_b, wk_b[w0]], writes=[wk_b[w0]])
                    fw.op(pool, lambda w0=w0, tc=tc, m=m: P.tensor_tensor(out=yT[:, 4 + m, tc * 512:(tc + 1) * 512], in0=yT[:, 4 + m, tc * 512:(tc + 1) * 512], in1=wkf[:, w0, 0:512], op=ALU.mult),
                          reads=[wk_b[w0], yT_b[4 + m][tc]], writes=[yT_b[4 + m][tc]])
                if m == 0:
                    load_wgroup(l, 1, 2304, 512, [lambda o, i: (o[:, 0:512], i[:, 0:512])])

            if stop <= 8:
                return
            for m2 in range(2):
                for tc in range(4):
                    pa = nxt("pa", 4)
                    for kc in range(KC):
                        fw.op(pe, lambda kc=kc, pa=pa, m2=m2, tc=tc: T.matmul(
                            pacc[:, pa, :], lhsT=wbf[:, 1, kc, m2 * 128:(m2 + 1) * 128], rhs=hT[:, kc, tc * 512:(tc + 1) * 512],
                            start=(kc == 0), stop=(kc == KC - 1)),
                            reads=hT_b[tc * 4:(tc + 1) * 4] + [wbf_b[1]], writes=[pacc_b[pa]], sig=(kc == KC - 1))
                    fw.op(act, lambda pa=pa, m2=m2, tc=tc: A.copy(out=fuT[:, m2, tc * 512:(tc + 1) * 512], in_=pacc[:, pa, :]),
                          reads=[pacc_b[pa]], writes=[fuT_b] + PT_b)
                    pa = nxt("pa", 4)
                    for kc in range(KC):
                        fw.op(pe, lambda kc=kc, pa=pa, m2=m2, tc=tc: T.matmul(
                            pacc[:, pa, :], lhsT=wbf[:, 1, kc, 256 + m2 * 128:256 + (m2 + 1) * 128], rhs=hT[:, kc, tc * 512:(tc + 1) * 512],
                            start=(kc == 0), stop=(kc == KC - 1)),
                            reads=hT_b[tc * 4:(tc + 1) * 4] + [wbf_b[1]], writes=[pacc_b[pa]], sig=(kc == KC - 1))
                    fw.op(act, lambda pa=pa, m2=m2, tc=tc: A.activation(out=yT[:, 6 + m2, tc * 512:(tc + 1) * 512], in_=pacc[:, pa, :], func=ACTF.Silu),
                          reads=[pacc_b[pa]], writes=[yT_b[6 + m2][tc]])
            wall = wbf[:, :, :, :].rearrange("p s k c -> p (s k c)").rearrange("p (k c) -> p k c", k=KC)
            for kc in range(KC):
                sl = nxt("stg", NSTG)
                if kc < 4:
                    fw.dma(s_stg[sl], stage[0:64, sl, :], wout_d[l, kc * 64:(kc + 1) * 64, :], writes=[stg_b[sl]])
                    fw.dma(s_stg[sl], stage[64:128, sl, :], wout_d[l, (4 + kc) * 64:(5 + kc) * 64, :], writes=[stg_b[sl]])
                else:
                    fw.dma(s_stg[sl], stage[:, sl, :], wout_d[l, kc * 128:(kc + 1) * 128, :], writes=[stg_b[sl]])
                fw.op(pool, lambda kc=kc, sl=sl: P.tensor_tensor(out=wall[:, kc, :], in0=stage[:, sl, :], in1=gateb[:], op=ALU.mult),
                      reads=[stg_b[sl], gate_b], writes=wbf_b)
            for i in range(NT):
                pa = nxt("pa", 4)
                for m2 in range(2):
                    fw.op(pe, lambda m2=m2, pa=pa, i=i: T.matmul(
                        pacc[:, pa, m2 * 256:(m2 + 1) * 256], lhsT=fuT[:, m2, i * 128:(i + 1) * 128],
                        rhs=ABbd[:, m2, :, :].rearrange("p a n -> p (a n)"), start=True, stop=True),
                        reads=[fuT_b, AB_b], writes=[pacc_b[pa]], sig=(m2 == 1))
                fw.op(act, lambda pa=pa, i=i: A.copy(out=UAB[:, i, :, :, :].rearrange("p m c n -> p (m c n)"), in_=pacc[:, pa, :]),
                      reads=[pacc_b[pa]], writes=[UAB_b] + QT_b)
            for half in range(2):
                pas = [[nxt("pa", 4) for _ in range(2)] for _ in range(2)]
                for a in range(NT):
                    ts = nxt("tab", NTAB)
                    fw.dma(s_tab[ts], tabv[:, ts, :, :], dft_d[a * 128:(a + 1) * 128, :, half * 1024:(half + 1) * 1024], writes=[tab_b[ts]] + hT_b)
                    for m2 in range(2):
                        for kq in range(2):
                            for ab in range(2):
                                fw.op(pe, lambda m2=m2, kq=kq, ab=ab, a=a, ts=ts, pas=pas: T.matmul(
                                    pacc[:, pas[m2][kq], :], lhsT=UAB[:, a, m2, ab, :], rhs=tabv[:, ts, ab, kq * 512:(kq + 1) * 512],
                                    start=(a == 0 and ab == 0), stop=(a == NT - 1 and ab == 1)),
                                    reads=[UAB_b, tab_b[ts]], writes=[pacc_b[pas[m2][kq]]], sig=((a == NT - 1 and ab == 1) or (m2 == 1 and kq == 1 and ab == 1)))
                for m2 in range(2):
                    for kq in range(2):
                        tcc = half * 2 + kq
                        fw.op(dve, lambda m2=m2, kq=kq, tcc=tcc, pas=pas: V.tensor_tensor(
                            out=yT[:, 6 + m2, tcc * 512:(tcc + 1) * 512], in0=pacc[:, pas[m2][kq], :], in1=yT[:, 6 + m2, tcc * 512:(tcc + 1) * 512], op=ALU.mult),
                            reads=[pacc_b[pas[m2][kq]], yT_b[6 + m2][tcc]], writes=[yT_b[6 + m2][tcc]])

            if stop <= 9:
                return
            fw.dma(s_lng, lngb[:], lng_d[l].partition_broadcast(128), writes=[lng_b])
            fw.dma(s_lnb, lnbb[:], lnb_d[l].partition_broadcast(128), writes=[lnb_b])
            def fin_a(i):
                bp = [4, 0, 2][i % 3]
                xr = nxt("xres", NXR)
                fw.dma(s_xres[xr], xres[:, xr, :], x_src[b, i * 128:(i + 1) * 128, :], reads=[ydram_b[b][i]], writes=[xres_b[xr]])
                for n in range(2):
                    for kc in range(KC):
                        fw.op(pe, lambda kc=kc, n=n: T.matmul(
                            pacc[:, bp + n, :], lhsT=yT[:, kc, i * 128:(i + 1) * 128], rhs=wall[:, kc, n * 512:(n + 1) * 512],
                            start=(kc == 0), stop=(kc == KC - 1)),
                            reads=[yT_b[kc][i // 4]] + wbf_b, writes=[pacc_b[bp + n]], sig=(kc == KC - 1))
                ss = nxt("st", 4)
                for n in range(2):
                    fw.op(dve, lambda n=n: V.scalar_tensor_tensor(
                        out=xres[:, xr, n * 512:(n + 1) * 512], in0=xres[:, xr, n * 512:(n + 1) * 512], scalar=float(ALPHA),
                        in1=pacc[:, bp + n, :], op0=ALU.mult, op1=ALU.add),
                        reads=[pacc_b[bp + n], xres_b[xr]], writes=[xres_b[xr]])
                for n in range(2):
                    fw.op(dve, lambda n=n: V.bn_stats(out=stt[:, ss, n, :], in_=xres[:, xr, n * 512:(n + 1) * 512]),
                          reads=[xres_b[xr]], writes=[stt_b[ss]])
                fw.op(dve, lambda: V.bn_aggr(out=mv[:, ss, :], in_=stt[:, ss, :, :].rearrange("p a b -> p (a b)")), reads=[stt_b[ss]], writes=[mv_b[ss]])
                fw.op(dve, lambda: V.tensor_scalar(out=sml[:, ss, 0:1], in0=mv[:, ss, 1:2], scalar1=1.0, scalar2=LN_EPS,
                                                   op0=ALU.mult, op1=ALU.add), reads=[mv_b[ss]], writes=[sml_b[ss]])
                fw.op(pool, lambda: P.tensor_tensor(out=sml[:, ss, 0:1], in0=sml[:, ss, 0:1], in1=mhalf[:, 0:1], op=ALU.pow),
                      reads=[sml_b[ss], cd_b, cm_b], writes=[sml_b[ss]])
                return xr, ss

            def fin_b(i, xr, ss):
                fw.op(dve, lambda: V.scalar_tensor_tensor(out=xres[:, xr, :], in0=xres[:, xr, :], scalar=mv[:, ss, 0:1], in1=lngb[:],
                                                          op0=ALU.subtract, op1=ALU.mult),
                      reads=[xres_b[xr], mv_b[ss], lng_b], writes=[xres_b[xr]])
                fw.op(dve, lambda: V.scalar_tensor_tensor(out=xres[:, xr, :], in0=xres[:, xr, :], scalar=sml[:, ss, 0:1], in1=lnbb[:],
                                                          op0=ALU.mult, op1=ALU.add),
                      reads=[xres_b[xr], sml_b[ss], lnb_b], writes=[xres_b[xr]])
                fw.dma(s_xres[xr], y_d[b, i * 128:(i + 1) * 128, :], xres[:, xr, :], reads=[xres_b[xr]], writes=[ydram_b[b][i]], q=fw.act)

            pend = fin_a(0)
            for i in range(NT):
                nxt_pend = fin_a(i + 1) if i + 1 < NT else None
                fin_b(i, *pend)
                pend = nxt_pend

        for l in range(n_layers):
            for b in range(n_seq):
                if b == 0:
                    emit_mod(l)
                emit_layer(b, l, x_d if l == 0 else y_d, l == n_layers - 1)
        fw.wait_all(fw.sp, s_xres)
        build_program.stats = (fw.ninst, fw.nwait, {e.name: e.sem.count for e in (pe, act, dve, pool)})
    return nc


def _constants():
    bf = ml_dtypes.bfloat16
    t = np.arange(S)
    rows = (t // 64).astype(np.float32)
    cols = (t % 64).astype(np.float32)
    inv = (1.0 / (10000.0 ** (np.arange(0, 32, 2, dtype=np.float32) / np.float32(32)))).astype(np.float32)
    ra = (rows[:, None] * inv[None, :]).astype(np.float32)
    ca = (cols[:, None] * inv[None, :]).astype(np.float32)
    ang = np.concatenate([ra, ca], axis=1)
    cos = np.cos(ang.astype(np.float64)).astype(np.float32)
    sin = np.sin(ang.astype(np.float64)).astype(np.float32)
    rope_cos = np.ascontiguousarray(cos.reshape(NT, 128, 32).transpose(1, 0, 2))
    rope_sin = np.ascontiguousarray(sin.reshape(NT, 128, 32).transpose(1, 0, 2))
    ks = (np.outer(t, t) % S).astype(np.float64) * (2.0 * np.pi / S)
    dft_cs = np.ascontiguousarray(np.stack([np.cos(ks), -np.sin(ks)], axis=1).astype(np.float32).astype(bf))
    m = np.arange(64)
    a64 = (np.outer(m, m) % 64).astype(np.float64) * (2.0 * np.pi / 64)
    sc = 1.0 / np.sqrt(float(S) * 64.0)
    c64 = np.cos(a64) * sc
    s64 = np.sin(a64) * sc
    z = np.zeros((64, 64))
    c64bd = np.block([[c64, z], [z, c64]]).astype(np.float32)
    s64bd = np.block([[s64, z], [z, s64]]).astype(np.float32)
    def hilo(a):
        hi = a.astype(bf)
        lo = (a - hi.astype(np.float32)).astype(bf)
        return np.stack([hi, lo], axis=1)
    cs64bd = np.ascontiguousarray(np.stack([hilo(c64bd), hilo(s64bd)], axis=1))
    return {
        "ident_bf": np.eye(128, dtype=np.float32).astype(bf),
        "ident_f": np.eye(128, dtype=np.float32),
        "rope_cos": rope_cos, "rope_sin": rope_sin,
        "dft_cs": dft_cs,
        "cs64bd": cs64bd,
    }


def make_in_maps(x, c, w_ada, b_ada, w_in, q_gain, k_gain, conv_w, f_mix, w_out, ln_g, ln_b, ncores=NCORES):
    f = lambda a: np.ascontiguousarray(np.asarray(a, dtype=np.float32))
    consts = _constants()
    shared = {
        "w_ada": f(w_ada), "w_in": f(w_in), "q_gain": f(q_gain), "k_gain": f(k_gain),
        "f_mix": f(f_mix), "w_out": f(w_out), "ln_g": f(ln_g), "ln_b": f(ln_b),
        "b_adaT": f(np.asarray(b_ada).reshape(DEPTH, 24, 128).transpose(2, 0, 1)),
        "conv_wT": f(np.asarray(conv_w).reshape(DEPTH, 3, 2, 128).transpose(3, 0, 2, 1)),
    }
    shared.update(consts)
    x = np.asarray(x, dtype=np.float32)
    c = np.asarray(c, dtype=np.float32)
    maps = []
    for ci in range(ncores):
        m = dict(shared)
        m["x"] = np.ascontiguousarray(x[ci * BPC:(ci + 1) * BPC])
        m["cT"] = f(c[ci * BPC:(ci + 1) * BPC].reshape(BPC, KC, 128).transpose(2, 1, 0))
        maps.append(m)
    return maps


def kernel(x, c, w_ada, b_ada, w_in, q_gain, k_gain, conv_w, f_mix, w_out, ln_g, ln_b):
    maps = make_in_maps(x, c, w_ada, b_ada, w_in, q_gain, k_gain, conv_w, f_mix, w_out, ln_g, ln_b)
    nc = build_program()
    res = run_bass_kernel_spmd(nc, maps, core_ids=list(range(NCORES)))
    return np.concatenate([np.asarray(r["y"], dtype=np.float32) for r in res.results], axis=0)
```

```python
import contextlib
import numpy as np
import ml_dtypes
import concourse.bass as bass
import concourse.mybir as mybir
from concourse.bass_utils import run_bass_kernel_spmd

F32 = mybir.dt.float32
BF16 = mybir.dt.bfloat16
ALU = mybir.AluOpType
ACTF = mybir.ActivationFunctionType
AX = mybir.AxisListType

S = 2048
D = 1024
NT = 16
KC = 8
DEPTH = 4
IN_W = 2816
ALPHA = (2 * DEPTH) ** 0.25
RMS_EPS = 1e-6
LN_EPS = 1e-5
NCORES = 8
BPC = 2


class Buf:
    __slots__ = ("name", "w", "r")

    def __init__(self, name):
        self.name = name
        self.w = None
        self.r = []


class Sem:
    def __init__(self, h, name):
        self.h = h
        self.name = name
        self.count = 0


class Eng:
    def __init__(self, name, h, sem):
        self.name = name
        self.h = h
        self.sem = sem
        self.seen = {}


class FW:
    def __init__(self, nc, stack):
        self.nc = nc
        self.stack = stack
        self.pe = self._eng("pe", nc.tensor)
        self.act = self._eng("act", nc.scalar)
        self.dve = self._eng("dve", nc.vector)
        self.pool = self._eng("pool", nc.gpsimd)
        self.sp = self._eng("sp", nc.sync)
        self.ninst = 0
        self.nwait = 0

    def sem(self, name):
        h = self.stack.enter_context(self.nc.semaphore(name))
        return Sem(h, name)

    def _eng(self, name, h):
        return Eng(name, h, self.sem("s_" + name))

    def sbuf(self, name, shape, dt):
        return self.stack.enter_context(self.nc.sbuf_tensor(name, shape, dt))

    def psum(self, name, shape, dt):
        return self.stack.enter_context(self.nc.psum_tensor(name, shape, dt))

    def _waits(self, eng, reads, writes, is_dma=False, my_sem=None):
        need = {}

        def add(tok):
            s, v, who = tok
            if (not is_dma) and who == eng.name and eng.name == "pe":
                return
            if v > need.get(s, (None, 0))[1]:
                need[s] = (s, v)

        for b in reads:
            if b.w is not None:
                add(b.w)
        for b in writes:
            if b.w is not None and not (is_dma and b.w[0] is my_sem):
                add(b.w)
            for r in b.r:
                add(r)
        for s, v in need.values():
            if eng.seen.get(s, 0) >= v:
                continue
            eng.h.wait_ge(s.h, v)
            eng.seen[s] = v
            self.nwait += 1

    def op(self, eng, fn, reads=(), writes=(), sig=True):
        self._waits(eng, reads, writes)
        ins = fn()
        self.ninst += 1
        if sig:
            eng.sem.count += 1
            ins.then_inc(eng.sem.h, 1)
            tok = (eng.sem, eng.sem.count, eng.name)
        else:
            tok = (eng.sem, eng.sem.count + 1, eng.name)
        for b in reads:
            b.r.append(tok)
        for b in writes:
            b.w = tok
            b.r = []
        return ins

    def dma(self, sem, out, in_, reads=(), writes=(), q=None, **kw):
        q = q or self.sp
        self._waits(q, reads, writes, is_dma=True, my_sem=sem)
        ins = q.h.dma_start(out=out, in_=in_, **kw)
        ins.then_inc(sem.h, 16)
        sem.count += 16
        tok = (sem, sem.count, "dma:" + sem.name)
        for b in reads:
            b.r.append(tok)
        for b in writes:
            b.w = tok
            b.r = []
        self.ninst += 1
        return ins

    def wait_all(self, eng, sems):
        for s in sems:
            if s.count > 0 and eng.seen.get(s, 0) < s.count:
                eng.h.wait_ge(s.h, s.count)
                eng.seen[s] = s.count


def build_program(n_layers=DEPTH, n_seq=BPC, stop=99):
    nc = bass.Bass("TRN2", target_bir_lowering=False)
    dt_in = lambda name, shape, dt=F32: nc.dram_tensor(name, shape, dt, kind="ExternalInput").ap()
    x_d = dt_in("x", [BPC, S, D])
    cT_d = dt_in("cT", [128, KC, BPC])
    wada_d = dt_in("w_ada", [DEPTH, D, 3 * D])
    badaT_d = dt_in("b_adaT", [128, DEPTH, 24])
    win_d = dt_in("w_in", [DEPTH, D, IN_W])
    qg_d = dt_in("q_gain", [DEPTH, 64])
    kg_d = dt_in("k_gain", [DEPTH, 64])
    cwT_d = dt_in("conv_wT", [128, DEPTH, 2, 3])
    fmix_d = dt_in("f_mix", [DEPTH, 4, 64, 64])
    wout_d = dt_in("w_out", [DEPTH, D, D])
    lng_d = dt_in("ln_g", [DEPTH, D])
    lnb_d = dt_in("ln_b", [DEPTH, D])
    identb_d = dt_in("ident_bf", [128, 128], BF16)
    identf_d = dt_in("ident_f", [128, 128])
    ropec_d = dt_in("rope_cos", [128, NT, 32])
    ropes_d = dt_in("rope_sin", [128, NT, 32])
    dft_d = dt_in("dft_cs", [S, 2, S], BF16)
    csbd_d = dt_in("cs64bd", [128, 2, 2, 128], BF16)
    y_d = nc.dram_tensor("y", [BPC, S, D], F32, kind="ExternalOutput").ap()

    with contextlib.ExitStack() as st:
        fw = FW(nc, st)
        pe, act, dve, pool = fw.pe, fw.act, fw.dve, fw.pool
        V, A, P, T = nc.vector, nc.scalar, nc.gpsimd, nc.tensor

        hT = fw.sbuf("hT", [128, KC, S], BF16)
        yT = fw.sbuf("yT", [128, 8, S], BF16)
        QT = fw.sbuf("QT", [128, 4, S], BF16)
        KTp = fw.sbuf("KTp", [128, 2, S], BF16)
        Vaug = fw.sbuf("Vaug", [128, NT, 2, 128], BF16)
        NPT = 4
        PT = fw.sbuf("PT", [128, NPT, 2, 512], BF16)
        fuT = PT[:, :, :, :].rearrange("p a b c -> p (a b c)").rearrange("p (m s) -> p m s", m=2)
        wbf = fw.sbuf("wbf", [128, 2, KC, 512], BF16)
        NSTG = 3
        stage = fw.sbuf("stage", [128, NSTG, 1024], F32)
        NXIN = 3
        xin = fw.sbuf("xin", [128, NXIN, D], F32)
        NXR = 3
        xres = fw.sbuf("xres", [128, NXR, D], F32)
        lngb = fw.sbuf("lngb", [128, D], F32)
        lnbb = fw.sbuf("lnbb", [128, D], F32)
        gateb = fw.sbuf("gateb", [128, D], F32)
        ropec = fw.sbuf("ropec", [128, NT, 32], F32)
        ropes = fw.sbuf("ropes", [128, NT, 32], F32)
        zt = xin[:, :, :].rearrange("p a d -> p (a d)")[:, 0:S + 2]
        xh = fw.sbuf("xh", [128, 2, D], BF16)
        NWK = 6
        wkf = fw.sbuf("wk", [128, NWK, 640], F32)
        qrb = fw.sbuf("qrb", [128, 2, 640], BF16)
        identb = fw.sbuf("identb", [128, 128], BF16)
        identf = fw.sbuf("identf", [128, 128], F32)
        onesb = fw.sbuf("onesb", [128, 128], BF16)
        dg = fw.sbuf("dg", [128, 2, 128], F32)
        dgb = fw.sbuf("dgb", [128, 2, 2, 128], BF16)
        csbd = fw.sbuf("csbd", [128, 2, 2, 128], BF16)
        fm = fw.sbuf("fm", [128, 2, 64], F32)
        fmb = fw.sbuf("fmb", [128, 2, 2, 64], BF16)
        cTb = fw.sbuf("cTb", [128, 2, KC, BPC], BF16)
        whi = fw.sbuf("whi", [128, 2, 1024], BF16)
        ABbd = fw.sbuf("ABbd", [128, 2, 2, 128], BF16)
        cTs = fw.sbuf("cTs", [128, KC, BPC], F32)
        badaT = fw.sbuf("badaT", [128, DEPTH, 24], F32)
        modT = fw.sbuf("modT", [128, 24, BPC], F32)
        s1T = fw.sbuf("s1T", [128, 8, BPC], F32)
        g1T = fw.sbuf("g1T", [128, 8, BPC], F32)
        cwT = fw.sbuf("cwT", [128, DEPTH, 2, 3], F32)
        qgb = fw.sbuf("qgb", [128, 64], F32)
        kgb = fw.sbuf("kgb", [128, 64], F32)
        mhalf = fw.sbuf("mhalf", [128, 16], F32)
        stt = fw.sbuf("stt", [128, 4, 2, 6], F32)
        mv = fw.sbuf("mv", [128, 4, 2], F32)
        sml = fw.sbuf("sml", [128, 4, 32], F32)

        pacc = fw.psum("pacc", [128, 8, 512], F32)
        ptrv = [pacc[:, 6, :].bitcast(BF16), pacc[:, 7, :].bitcast(BF16)]

        B = lambda n: Buf(n)
        hT_b = [B("hT%d" % i) for i in range(NT)]
        NTAB = 6
        tab_b = [B("tab%d" % i) for i in range(NTAB)]
        yT_b = [[B("yT%d_%d" % (c, t)) for t in range(4)] for c in range(8)]
        QT_b = [B("QT%d" % j) for j in range(4)]
        UAB_b = B("UAB")
        KT_b = B("KTp")
        Va_b = B("Vaug")
        PT_b = [B("PT%d" % i) for i in range(NPT)]
        fuT_b = B("fuT")
        wbf_b = [B("wbf0"), B("wbf1")]
        stg_b = [B("stg%d" % i) for i in range(NSTG)]
        xin_b = [B("xin%d" % i) for i in range(NXIN)]
        xres_b = [B("xres%d" % i) for i in range(NXR)]
        lng_b, lnb_b, gate_b = B("lng"), B("lnb"), B("gate")
        rope_b, z_b = B("rope"), B("z")
        xh_b = [B("xh0"), B("xh1")]
        wk_b = [B("wk%d" % i) for i in range(NWK)]
        qrb_b = [B("qrb0"), B("qrb1")]
        cd_b, cm_b = B("constd"), B("constm")
        dg_b = [B("dg0"), B("dg1")]
        fm_b, AB_b = B("fm"), B("ABbd")
        whi_b = [B("whi0"), B("whi1")]
        cTb_b = B("cTb")
        mod_b, s1_b, g1_b = B("modT"), B("s1T"), B("g1T")
        gain_b = B("gains")
        stt_b = [B("stt%d" % i) for i in range(4)]
        mv_b = [B("mv%d" % i) for i in range(4)]
        sml_b = [B("sml%d" % i) for i in range(4)]
        pacc_b = [B("pacc%d" % i) for i in range(8)]
        ptr_b = [pacc_b[6], pacc_b[7]]
        ydram_b = [[B("yd%d_%d" % (b, i)) for i in range(NT)] for b in range(BPC)]

        s_const = fw.sem("d_const")
        s_stg = [fw.sem("d_stg%d" % i) for i in range(NSTG)]
        s_xin = [fw.sem("d_xin%d" % i) for i in range(NXIN)]
        s_xres = [fw.sem("d_xres%d" % i) for i in range(NXR)]
        s_tab = [fw.sem("d_tab%d" % i) for i in range(NTAB)]
        s_lng, s_lnb, s_gain, s_fm = fw.sem("d_lng"), fw.sem("d_lnb"), fw.sem("d_gain"), fw.sem("d_fm")

        tabv = hT[:, 0:NTAB, :].rearrange("p a (c k) -> p a c k", c=2)
        UAB = QT[:, :, :].rearrange("p j (a m c n) -> p (j a) m c n", a=4, m=2, c=2)

        for (dst, src) in [(identb, identb_d), (identf, identf_d), (ropec, ropec_d), (ropes, ropes_d),
                           (csbd, csbd_d), (cTs, cT_d), (badaT, badaT_d), (cwT, cwT_d)]:
            fw.dma(s_const, dst[:], src, writes=[cd_b, rope_b])
        fw.op(pool, lambda: P.memset(Vaug[:], 1.0), writes=[Va_b])
        fw.op(pool, lambda: P.memset(KTp[:], 0.0), writes=[KT_b])
        fw.op(pool, lambda: P.memset(mhalf[:], -0.5), writes=[cm_b])
        fw.op(pool, lambda: P.memset(onesb[:], 1.0), writes=[cm_b])
        fw.op(dve, lambda: V.tensor_copy(out=cTb[:, 0, :, :], in_=cTs[:]), reads=[cd_b], writes=[cTb_b])
        fw.op(dve, lambda: V.tensor_tensor(out=cTb[:, 1, :, :], in0=cTs[:], in1=cTb[:, 0, :, :], op=ALU.subtract), reads=[cd_b, cTb_b], writes=[cTb_b])
        fw.op(pool, lambda: P.memset(ABbd[:], 0.0), writes=[AB_b])

        rr = {"stg": 0, "xin": 0, "xres": 0, "tab": 0, "wk": 0, "whi": 0, "pp": 0, "st": 0, "pa": 0, "ptr": 0, "pt": 0, "xh": 0, "qrb": 0, "dg": 0}

        def nxt(key, n):
            v = rr[key]
            rr[key] = (v + 1) % n
            return v

        def rstd_from(var_ap, out_ap, slot_b, scale, eps):
            n = var_ap.shape[-1]
            fw.op(dve, lambda: V.tensor_scalar(out=out_ap, in0=var_ap, scalar1=scale, scalar2=eps,
                                               op0=ALU.mult, op1=ALU.add), reads=[slot_b], writes=[slot_b])
            fw.op(pool, lambda: P.tensor_tensor(out=out_ap, in0=out_ap, in1=mhalf[:, 0:n], op=ALU.pow),
                  reads=[slot_b, cd_b, cm_b], writes=[slot_b])

        def emit_mod(l):
            pa = 5
            for oc in range(24):
                sl = nxt("stg", NSTG)
                fw.dma(s_stg[sl], stage[:, sl, :].rearrange("p (k c) -> p k c", k=KC),
                       wada_d[l, :, oc * 128:(oc + 1) * 128].rearrange("(k p) c -> p k c", p=128),
                       writes=[stg_b[sl]])
                ws = nxt("whi", 2)
                if oc % 3 == 0:
                    fw.op(dve, lambda sl=sl, ws=ws: V.tensor_copy(out=whi[:, ws, :], in_=stage[:, sl, :]), reads=[stg_b[sl]], writes=[whi_b[ws]])
                elif oc % 3 == 1:
                    fw.op(act, lambda sl=sl, ws=ws: A.copy(out=whi[:, ws, :], in_=stage[:, sl, :]), reads=[stg_b[sl]], writes=[whi_b[ws]])
                else:
                    fw.op(pool, lambda sl=sl, ws=ws: P.tensor_copy(out=whi[:, ws, :], in_=stage[:, sl, :]), reads=[stg_b[sl]], writes=[whi_b[ws]])
                for kc in range(KC):
                    for hl in range(2):
                        fw.op(pe, lambda kc=kc, ws=ws, oc=oc, hl=hl: T.matmul(
                            pacc[:, pa, oc * 2:oc * 2 + 2], lhsT=whi[:, ws, kc * 128:(kc + 1) * 128],
                            rhs=cTb[:, hl, kc, :], start=(kc == 0 and hl == 0), stop=(kc == KC - 1 and hl == 1)),
                            reads=[whi_b[ws], cTb_b], writes=[pacc_b[pa]], sig=(kc == KC - 1 and hl == 1))
            fw.op(dve, lambda: V.tensor_tensor(
                out=modT[:], in0=pacc[:, pa, 0:48].rearrange("p (o b) -> p o b", b=BPC),
                in1=badaT[:, l, :].unsqueeze(2).to_broadcast([128, 24, BPC]), op=ALU.add),
                reads=[pacc_b[pa], cd_b, cm_b], writes=[mod_b])
            fw.op(dve, lambda: V.tensor_scalar(out=s1T[:], in0=modT[:, 8:16, :], scalar1=1.0, scalar2=None, op0=ALU.add),
                  reads=[mod_b], writes=[s1_b])
            fw.op(dve, lambda: V.tensor_scalar(out=g1T[:], in0=modT[:, 16:24, :], scalar1=1.0, scalar2=None, op0=ALU.add),
                  reads=[mod_b], writes=[g1_b])
            fw.dma(s_gain, qgb[:], qg_d[l].partition_broadcast(128), writes=[gain_b])
            fw.dma(s_gain, kgb[:], kg_d[l].partition_broadcast(128), writes=[gain_b])
            fw.dma(s_fm, fm[:], fmix_d[l].rearrange("(a g) m d -> (g m) a d", a=2), writes=[fm_b])
            fw.op(dve, lambda: V.tensor_copy(out=fmb[:, :, 0, :], in_=fm[:]), reads=[fm_b], writes=[fm_b])
            fw.op(dve, lambda: V.tensor_tensor(out=fmb[:, :, 1, :], in0=fm[:], in1=fmb[:, :, 0, :], op=ALU.subtract), reads=[fm_b], writes=[fm_b])
            for m2 in range(2):
                for ab in range(2):
                    for t3, (th, fh) in enumerate(((0, 0), (0, 1), (1, 0))):
                        fw.op(pe, lambda m2=m2, ab=ab, th=th, fh=fh, t3=t3: T.matmul(
                            pacc[:, pa, 64 + (m2 * 2 + ab) * 64: 64 + (m2 * 2 + ab + 1) * 64], lhsT=csbd[:, ab, th, :], rhs=fmb[:, m2, fh, :],
                            start=(t3 == 0), stop=(t3 == 2)), reads=[fm_b, cd_b, cm_b], writes=[pacc_b[pa]], sig=(t3 == 2))
            for m2 in range(2):
                for ab in range(2):
                    c0 = 64 + (m2 * 2 + ab) * 64
                    fw.op(dve, lambda m2=m2, ab=ab, c0=c0: V.tensor_copy(out=ABbd[0:64, m2, ab, 0:64], in_=pacc[0:64, pa, c0:c0 + 64]),
                          reads=[pacc_b[pa]], writes=[AB_b])
                    fw.op(dve, lambda m2=m2, ab=ab, c0=c0: V.tensor_copy(out=ABbd[64:128, m2, ab, 64:128], in_=pacc[64:128, pa, c0:c0 + 64]),
                          reads=[pacc_b[pa]], writes=[AB_b])

        def load_wgroup(l, slot, src0, ncol_src, casts):
            for _ in wgroup_gen(l, slot, src0, ncol_src, casts):
                pass

        def wgroup_gen(l, slot, src0, ncol_src, casts):
            for kc in range(KC):
                sl = nxt("stg", NSTG)
                fw.dma(s_stg[sl], stage[:, sl, 0:ncol_src], win_d[l, kc * 128:(kc + 1) * 128, src0:src0 + ncol_src],
                       writes=[stg_b[sl]])
                for cf in casts:
                    o_ap, i_ap = cf(wbf[:, slot, kc, :], stage[:, sl, :])
                    fw.op(pool, lambda o_ap=o_ap, i_ap=i_ap: P.tensor_copy(out=o_ap, in_=i_ap),
                          reads=[stg_b[sl]], writes=[wbf_b[slot]])
                yield

        perm_cast = lambda o, i: (o[:, 0:512].rearrange("p (j h d) -> p j h d", j=4, h=2),
                                  i[:, 0:512].rearrange("p (h j d) -> p j h d", h=2, j=4))

        def rope(src, dst, i, H, sb, wkA, wkB, src_bufs, dst_bufs):
            sv = src.rearrange("p (h a b d) -> p h a b d", h=H, a=2, b=2)
            dv = dst.rearrange("p (h a b d) -> p h a b d", h=H, a=2, b=2)
            cs = ropec[:, i, :].rearrange("p (a d) -> p a d", a=2).unsqueeze(1).to_broadcast([128, H, 2, 16])
            sn = ropes[:, i, :].rearrange("p (a d) -> p a d", a=2).unsqueeze(1).to_broadcast([128, H, 2, 16])
            n = H * 32
            t1 = wkA[:, 0:n].rearrange("p (h a d) -> p h a d", h=H, a=2)
            t2 = wkA[:, n:2 * n].rearrange("p (h a d) -> p h a d", h=H, a=2)
            t3 = wkB[:, 0:n].rearrange("p (h a d) -> p h a d", h=H, a=2)
            t4 = wkB[:, n:2 * n].rearrange("p (h a d) -> p h a d", h=H, a=2)
            x0 = sv[:, :, :, 0, :]
            x1 = sv[:, :, :, 1, :]
            wa, wb = sb
            fw.op(dve, lambda: V.tensor_tensor(out=t1, in0=x0, in1=cs, op=ALU.mult), reads=src_bufs + [rope_b], writes=[wa])
            fw.op(dve, lambda: V.tensor_tensor(out=t2, in0=x1, in1=sn, op=ALU.mult), reads=src_bufs + [rope_b], writes=[wa])
            fw.op(dve, lambda: V.tensor_tensor(out=t3, in0=x1, in1=cs, op=ALU.mult), reads=src_bufs + [rope_b], writes=[wb])
            fw.op(dve, lambda: V.tensor_tensor(out=t4, in0=x0, in1=sn, op=ALU.mult), reads=src_bufs + [rope_b], writes=[wb])
            fw.op(dve, lambda: V.tensor_tensor(out=dv[:, :, :, 0, :], in0=t1, in1=t2, op=ALU.subtract), reads=[wa], writes=dst_bufs)
            fw.op(dve, lambda: V.tensor_tensor(out=dv[:, :, :, 1, :], in0=t3, in1=t4, op=ALU.add), reads=[wb], writes=dst_bufs)

        def emit_layer(b, l, x_src, last):
            for kc in range(KC):
                ds = nxt("dg", 2)
                pa = 4 + (kc // 4)
                fw.op(dve, lambda kc=kc, ds=ds: V.tensor_scalar(out=dg[:, ds, :], in0=identf[:], scalar1=g1T[:, kc, b:b + 1],
                                                              scalar2=None, op0=ALU.mult),
                      reads=[cd_b, cm_b, g1_b], writes=[dg_b[ds]])
                fw.op(dve, lambda ds=ds: V.tensor_copy(out=dgb[:, ds, 0, :], in_=dg[:, ds, :]), reads=[dg_b[ds]], writes=[dg_b[ds]])
                fw.op(dve, lambda ds=ds: V.tensor_tensor(out=dgb[:, ds, 1, :], in0=dg[:, ds, :], in1=dgb[:, ds, 0, :], op=ALU.subtract), reads=[dg_b[ds]], writes=[dg_b[ds]])
                for hl in range(2):
                    fw.op(pe, lambda kc=kc, ds=ds, pa=pa, hl=hl: T.matmul(pacc[:, pa, (kc % 4) * 128:(kc % 4 + 1) * 128], lhsT=onesb[:], rhs=dgb[:, ds, hl, :],
                                                                        start=(hl == 0), stop=(hl == 1)),
                          reads=[dg_b[ds], cd_b, cm_b], writes=[pacc_b[pa]], sig=(hl == 1))
            for h2 in range(2):
                fw.op(act, lambda h2=h2: A.copy(out=gateb[:, h2 * 512:(h2 + 1) * 512], in_=pacc[:, 4 + h2, :]),
                      reads=[pacc_b[4 + h2]], writes=[gate_b])

            if stop <= 1:
                return
            import itertools
            wpre = itertools.chain(wgroup_gen(l, 0, 512, 256, [lambda o, i: (o[:, 0:256], i[:, 0:256])]),
                                   wgroup_gen(l, 1, 0, 512, [perm_cast]))

            if stop <= 1.5:
                return
            def ln1_a(i):
                xs_ = nxt("xin", NXIN)
                fw.dma(s_xin[xs_], xin[:, xs_, :], x_src[b, i * 128:(i + 1) * 128, :],
                       reads=[ydram_b[b][i]], writes=[xin_b[xs_]] + ([z_b] if i < NXIN else []))
                ss = nxt("st", 4)
                for h2 in range(2):
                    fw.op(dve, lambda h2=h2: V.bn_stats(out=stt[:, ss, h2, :], in_=xin[:, xs_, h2 * 512:(h2 + 1) * 512]),
                          reads=[xin_b[xs_]], writes=[stt_b[ss]])
                fw.op(dve, lambda: V.bn_aggr(out=mv[:, ss, :], in_=stt[:, ss, :, :].rearrange("p a b -> p (a b)")), reads=[stt_b[ss]], writes=[mv_b[ss]])
                fw.op(dve, lambda: V.tensor_scalar(out=sml[:, ss, 0:1], in0=mv[:, ss, 1:2], scalar1=1.0, scalar2=LN_EPS,
                                                   op0=ALU.mult, op1=ALU.add), reads=[mv_b[ss]], writes=[sml_b[ss]])
                fw.op(pool, lambda: P.tensor_tensor(out=sml[:, ss, 0:1], in0=sml[:, ss, 0:1], in1=mhalf[:, 0:1], op=ALU.pow),
                      reads=[sml_b[ss], cd_b, cm_b], writes=[sml_b[ss]])
                return xs_, ss

            def ln1_b(i, xs_, ss):
                hs = nxt("xh", 2)
                fw.op(dve, lambda: V.tensor_scalar(
                    out=xh[:, hs, :], in0=xin[:, xs_, :], scalar1=mv[:, ss, 0:1], scalar2=sml[:, ss, 0:1],
                    op0=ALU.subtract, op1=ALU.mult), reads=[xin_b[xs_], mv_b[ss], sml_b[ss]], writes=[xh_b[hs]])
                pb = nxt("ptr", 2)
                for kc in range(KC):
                    fw.op(pe, lambda kc=kc: T.transpose(ptrv[pb][:, kc * 128:(kc + 1) * 128], xh[:, hs, kc * 128:(kc + 1) * 128], identb[:]),
                          reads=[xh_b[hs], cd_b, cm_b], writes=[ptr_b[pb]], sig=(kc == KC - 1))
                for kc in range(KC):
                    if True:
                        fw.op(act, lambda kc=kc: A.activation(
                            out=hT[:, kc, i * 128:(i + 1) * 128], in_=ptrv[pb][:, kc * 128:(kc + 1) * 128], func=ACTF.Identity,
                            scale=s1T[:, kc, b:b + 1], bias=modT[:, kc, b:b + 1]),
                            reads=[ptr_b[pb], s1_b, mod_b], writes=[hT_b[i]] + (tab_b if kc == 0 else []))
                    else:
                        fw.op(dve, lambda kc=kc: V.tensor_scalar(
                            out=hT[:, kc, i * 128:(i + 1) * 128], in0=ptrv[pb][:, kc * 128:(kc + 1) * 128],
                            scalar1=s1T[:, kc, b:b + 1], scalar2=modT[:, kc, b:b + 1], op0=ALU.mult, op1=ALU.add),
                            reads=[ptr_b[pb], s1_b, mod_b], writes=[hT_b[i]])

            pend = ln1_a(0)
            for i in range(NT):
                nxt_pend = ln1_a(i + 1) if i + 1 < NT else None
                next(wpre, None)
                ln1_b(i, *pend)
                pend = nxt_pend
            for _ in wpre:
                pass

            if stop <= 2:
                return
            HQK = 10

            def qk_a1(i):
                pp = nxt("pp", 3)
                bq, bk = 2 * pp, 2 * pp + 1
                for kc in range(KC):
                    fw.op(pe, lambda kc=kc: T.matmul(pacc[:, bq, :], lhsT=hT[:, kc, i * 128:(i + 1) * 128], rhs=wbf[:, 1, kc, :],
                                                     start=(kc == 0), stop=(kc == KC - 1)),
                          reads=[hT_b[i], wbf_b[1]], writes=[pacc_b[bq]], sig=(kc == KC - 1))
                for kc in range(KC):
                    fw.op(pe, lambda kc=kc: T.matmul(pacc[:, bk, 0:256], lhsT=hT[:, kc, i * 128:(i + 1) * 128], rhs=wbf[:, 0, kc, 0:256],
                                                     start=(kc == 0), stop=(kc == KC - 1)),
                          reads=[hT_b[i], wbf_b[0]], writes=[pacc_b[bk]], sig=(kc == KC - 1))
                fw.op(act, lambda: A.copy(out=Vaug[:, i, 0, 0:64], in_=pacc[:, bk, 128:192]), reads=[pacc_b[bk]], writes=[Va_b])
                fw.op(act, lambda: A.copy(out=Vaug[:, i, 1, 64:128], in_=pacc[:, bk, 192:256]), reads=[pacc_b[bk]], writes=[Va_b])
                n = HQK * 64
                src = pacc[:, bq:bq + 2, :].rearrange("p a c -> p (a c)")[:, 0:n]
                srcb = [pacc_b[bq], pacc_b[bk]]
                wA = nxt("wk", NWK)
                fw.op(act, lambda: A.activation(out=wkf[:, wA, 0:n], in_=src, func=ACTF.Square), reads=srcb, writes=[wk_b[wA]])
                return bq, bk, wA

            def qk_a2(i, bq, bk, wA):
                n = HQK * 64
                wB = nxt("wk", NWK); wC = nxt("wk", NWK)
                ss = nxt("st", 4)
                fw.op(dve, lambda: V.tensor_reduce(out=sml[:, ss, 0:HQK], in_=wkf[:, wA, 0:n].rearrange("p (h d) -> p h d", h=HQK),
                                                   axis=AX.X, op=ALU.add), reads=[wk_b[wA]], writes=[sml_b[ss]])
                rstd_from(sml[:, ss, 0:HQK], sml[:, ss, 0:HQK], sml_b[ss], 1.0 / 64, RMS_EPS)
                fw.op(dve, lambda: V.tensor_tensor(out=wkf[:, wB, 0:512].rearrange("p (h d) -> p h d", h=8), in0=pacc[:, bq, :].rearrange("p (h d) -> p h d", h=8),
                                                   in1=qgb[:].unsqueeze(1).to_broadcast([128, 8, 64]), op=ALU.mult),
                      reads=[pacc_b[bq], gain_b], writes=[wk_b[wB]])
                fw.op(dve, lambda: V.tensor_tensor(out=wkf[:, wB, 512:640].rearrange("p (h d) -> p h d", h=2), in0=pacc[:, bk, 0:128].rearrange("p (h d) -> p h d", h=2),
                                                   in1=kgb[:].unsqueeze(1).to_broadcast([128, 2, 64]), op=ALU.mult),
                      reads=[pacc_b[bk], gain_b], writes=[wk_b[wB]])
                rope(wkf[:, wB, 0:n], wkf[:, wB, 0:n], i, HQK, (wk_b[wC], wk_b[wA]), wkf[:, wC, :], wkf[:, wA, :], [wk_b[wB]], [wk_b[wB]])
                return wB, ss

            def qk_b(i, wB, ss):
                n = HQK * 64
                qs = nxt("qrb", 2)
                fw.op(dve, lambda: V.tensor_tensor(out=qrb[:, qs, 0:n].rearrange("p (h d) -> p h d", h=HQK), in0=wkf[:, wB, 0:n].rearrange("p (h d) -> p h d", h=HQK),
                                                   in1=sml[:, ss, 0:HQK].unsqueeze(2).to_broadcast([128, HQK, 64]), op=ALU.mult),
                      reads=[wk_b[wB], sml_b[ss]], writes=[qrb_b[qs]])
                pb = nxt("ptr", 2)
                for j in range(5):
                    fw.op(pe, lambda j=j: T.transpose(ptrv[pb][:, j * 128:(j + 1) * 128], qrb[:, qs, j * 128:(j + 1) * 128], identb[:]),
                          reads=[qrb_b[qs], cd_b, cm_b], writes=[ptr_b[pb]], sig=(j == 4))
                fw.op(act, lambda: A.copy(out=QT[:, :, i * 128:(i + 1) * 128], in_=ptrv[pb][:, 0:512].rearrange("p (j t) -> p j t", j=4)),
                      reads=[ptr_b[pb]], writes=QT_b + [UAB_b])
                fw.op(act, lambda: A.copy(out=KTp[0:64, 0, i * 128:(i + 1) * 128], in_=ptrv[pb][0:64, 512:640]), reads=[ptr_b[pb]], writes=[KT_b])
                fw.op(act, lambda: A.copy(out=KTp[64:128, 1, i * 128:(i + 1) * 128], in_=ptrv[pb][64:128, 512:640]), reads=[ptr_b[pb]], writes=[KT_b])

            a1 = {0: qk_a1(0), 1: qk_a1(1)}
            a2 = {0: qk_a2(0, *a1[0])}
            for i in range(NT):
                if i + 2 < NT:
                    a1[i + 2] = qk_a1(i + 2)
                if i + 1 < NT:
                    a2[i + 1] = qk_a2(i + 1, *a1[i + 1])
                qk_b(i, *a2[i])
            load_wgroup(l, 0, 768, 512, [perm_cast])
            if stop <= 4:
                return
            for j in range(4):
                for tc in range(4):
                    pa = nxt("pa", 4)
                    for kc in range(KC):
                        fw.op(pe, lambda kc=kc, pa=pa, j=j, tc=tc: T.matmul(
                            pacc[:, pa, :], lhsT=wbf[:, 0, kc, j * 128:(j + 1) * 128], rhs=hT[:, kc, tc * 512:(tc + 1) * 512],
                            start=(kc == 0), stop=(kc == KC - 1)),
                            reads=hT_b[tc * 4:(tc + 1) * 4] + [wbf_b[0]], writes=[pacc_b[pa]], sig=(kc == KC - 1))
                    fw.op(act, lambda pa=pa, j=j, tc=tc: A.activation(out=yT[:, j, tc * 512:(tc + 1) * 512], in_=pacc[:, pa, :], func=ACTF.Silu),
                          reads=[pacc_b[pa]], writes=[yT_b[j][tc]])

            if stop <= 5:
                return
            conv_cast = lambda m: (lambda o, i: (o[:, 0:512].rearrange("p (g c) -> p g c", g=4),
                                                 i[:, 0:1024].rearrange("p (g m c) -> p g m c", g=4, m=2)[:, :, m, :]))
            load_wgroup(l, 1, 1280, 1024, [conv_cast(0)])
            load_wgroup(l, 0, 1280, 1024, [conv_cast(1)])

            if stop <= 6:
                return
            for j in range(4):
                for qc in range(4):
                    steps = []

                    def qk(kt, sb_):
                        for kv in range(2):
                            fw.op(pe, lambda kv=kv, kt=kt, sb_=sb_: T.matmul(
                                pacc[:, SBASE[sb_] + kv, :], lhsT=KTp[:, kv, kt * 128:(kt + 1) * 128], rhs=QT[:, j, qc * 512:(qc + 1) * 512],
                                start=True, stop=True), reads=[KT_b, QT_b[j]], writes=[pacc_b[SBASE[sb_] + kv]])

                    def ex(kt, sb_, ps):
                        fw.op(act, lambda sb_=sb_, ps=ps: A.activation(out=PT[:, ps, :, :], in_=pacc[:, SBASE[sb_]:SBASE[sb_] + 2, :], func=ACTF.Exp, scale=0.125),
                              reads=[pacc_b[SBASE[sb_]], pacc_b[SBASE[sb_] + 1]], writes=[PT_b[ps], fuT_b])

                    def pv(kt, ps):
                        for kv in range(2):
                            fw.op(pe, lambda kv=kv, kt=kt, ps=ps: T.matmul(
                                pacc[:, 4 + kv, :], lhsT=Vaug[:, kt, kv, :], rhs=PT[:, ps, kv, :], start=(kt == 0), stop=(kt == NT - 1)),
                                reads=[Va_b, PT_b[ps]], writes=[pacc_b[4 + kv]], sig=(kt == NT - 1))

                    SBASE = [0, 2, 6]
                    qk(0, 0)
                    qk(1, 1)
                    for kt in range(NT):
                        ps = nxt("pt", NPT)
                        ex(kt, kt % 3, ps)
                        if kt + 2 < NT:
                            qk(kt + 2, (kt + 2) % 3)
                        pv(kt, ps)
                    w0 = nxt("wk", NWK); w1 = nxt("wk", NWK); w2 = nxt("wk", NWK)
                    cols = slice(qc * 512, (qc + 1) * 512)
                    fw.op(dve, lambda w0=w0: V.tensor_copy(out=wkf[:, w0, 0:512], in_=pacc[:, 4, :]), reads=[pacc_b[4]], writes=[wk_b[w0]])
                    fw.op(dve, lambda w1=w1: V.tensor_copy(out=wkf[:, w1, 0:512], in_=pacc[:, 5, :]), reads=[pacc_b[5]], writes=[wk_b[w1]])
                    fw.op(dve, lambda w0=w0, w2=w2: V.reciprocal(out=wkf[0:64, w2, 0:512], in_=wkf[64:128, w0, 0:512]), reads=[wk_b[w0]], writes=[wk_b[w2]])
                    fw.op(dve, lambda w0=w0, w2=w2: V.tensor_tensor(out=wkf[0:64, w0, 0:512], in0=wkf[0:64, w0, 0:512], in1=wkf[0:64, w2, 0:512], op=ALU.mult),
                          reads=[wk_b[w0], wk_b[w2]], writes=[wk_b[w0]])
                    fw.op(pool, lambda w0=w0: P.tensor_tensor(out=yT[0:64, j, cols], in0=yT[0:64, j, cols], in1=wkf[0:64, w0, 0:512], op=ALU.mult),
                          reads=[wk_b[w0], yT_b[j][qc]], writes=[yT_b[j][qc]])
                    fw.op(dve, lambda w1=w1, w2=w2: V.reciprocal(out=wkf[64:128, w2, 0:512], in_=wkf[0:64, w1, 0:512]), reads=[wk_b[w1]], writes=[wk_b[w2]])
                    fw.op(dve, lambda w1=w1, w2=w2: V.tensor_tensor(out=wkf[64:128, w1, 0:512], in0=wkf[64:128, w1, 0:512], in1=wkf[64:128, w2, 0:512], op=ALU.mult),
                          reads=[wk_b[w1], wk_b[w2]], writes=[wk_b[w1]])
                    fw.op(pool, lambda w1=w1: P.tensor_tensor(out=yT[64:128, j, cols], in0=yT[64:128, j, cols], in1=wkf[64:128, w1, 0:512], op=ALU.mult),
                          reads=[wk_b[w1], yT_b[j][qc]], writes=[yT_b[j][qc]])

            if stop <= 7:
                return
            fw.op(pool, lambda: P.memset(zt[:, 0:1], 0.0), writes=[z_b] + xin_b)
            fw.op(pool, lambda: P.memset(zt[:, S + 1:S + 2], 0.0), writes=[z_b] + xin_b)
            for m in range(2):
                slot = 1 - m
                for tc in range(4):
                    pas = [nxt("pa", 4) for _ in range(4)]
                    for g in range(4):
                        for kc in range(KC):
                            fw.op(pe, lambda kc=kc, g=g, tc=tc, slot=slot, pas=pas: T.matmul(
                                pacc[:, pas[g], :], lhsT=wbf[:, slot, kc, g * 128:(g + 1) * 128], rhs=hT[:, kc, tc * 512:(tc + 1) * 512],
                                start=(kc == 0), stop=(kc == KC - 1)),
                                reads=hT_b[tc * 4:(tc + 1) * 4] + [wbf_b[slot]], writes=[pacc_b[pas[g]]], sig=(kc == KC - 1))
                    w0 = nxt("wk", NWK); w1 = nxt("wk", NWK)
                    fw.op(act, lambda w0=w0, pas=pas: A.copy(out=wkf[:, w0, 0:512], in_=pacc[:, pas[2], :]), reads=[pacc_b[pas[2]]], writes=[wk_b[w0]])
                    fw.op(dve, lambda w0=w0, pas=pas, tc=tc: V.tensor_tensor(out=zt[:, 1 + tc * 512:1 + (tc + 1) * 512], in0=pacc[:, pas[0], :], in1=wkf[:, w0, 0:512], op=ALU.mult),
                          reads=[pacc_b[pas[0]], wk_b[w0]], writes=[z_b])
                    fw.op(act, lambda w1=w1, pas=pas: A.activation(out=wkf[:, w1, 0:512], in_=pacc[:, pas[3], :], func=ACTF.Silu), reads=[pacc_b[pas[3]]], writes=[wk_b[w1]])
                    fw.op(dve, lambda w1=w1, pas=pas, tc=tc, m=m: V.tensor_tensor(out=yT[:, 4 + m, tc * 512:(tc + 1) * 512], in0=pacc[:, pas[1], :], in1=wkf[:, w1, 0:512], op=ALU.mult),
                          reads=[pacc_b[pas[1]], wk_b[w1]], writes=[yT_b[4 + m][tc]])
                for tc in range(4):
                    w0 = nxt("wk", NWK)
                    c0 = 1 + tc * 512
                    fw.op(dve, lambda w0=w0, c0=c0, m=m: V.tensor_scalar(out=wkf[:, w0, 0:512], in0=zt[:, c0:c0 + 512], scalar1=cwT[:, l, m, 1:2], scalar2=None, op0=ALU.mult),
                          reads=[z_b, cd_b, cm_b], writes=[wk_b[w0]])
                    fw.op(dve, lambda w0=w0, c0=c0, m=m: V.scalar_tensor_tensor(out=wkf[:, w0, 0:512], in0=zt[:, c0 - 1:c0 + 511], scalar=cwT[:, l, m, 0:1], in1=wkf[:, w0, 0:512],
                                                                           op0=ALU.mult, op1=ALU.add), reads=[z_b, cd_b, cm_b, wk_b[w0]], writes=[wk_b[w0]])
                    fw.op(dve, lambda w0=w0, c0=c0, m=m: V.scalar_tensor_tensor(out=wkf[:, w0, 0:512], in0=zt[:, c0 + 1:c0 + 513], scalar=cwT[:, l, m, 2:3], in1=wkf[:, w0, 0:512],
                                                                           op0=ALU.mult, op1=ALU.add), reads=[z_b, cd_b, cm_b, wk_b[w0]], writes=[wk_b[w0]])
                    fw.op(pool, lambda w0=w0, tc=tc, m=m: P.tensor_tensor(out=yT[:, 4 + m, tc * 512:(tc + 1) * 512], in0=yT[:, 4 + m, tc * 512:(tc + 1) * 512], in1=wkf[:, w0, 0:512], op=ALU.mult),
                          reads=[wk_b[w0], yT_b[4 + m][tc]], writes=[yT_b[4 + m][tc]])
                if m == 0:
                    load_wgroup(l, 1, 2304, 512, [lambda o, i: (o[:, 0:512], i[:, 0:512])])

            if stop <= 8:
                return
            for m2 in range(2):
                for tc in range(4):
                    pa = nxt("pa", 4)
                    for kc in range(KC):
                        fw.op(pe, lambda kc=kc, pa=pa, m2=m2, tc=tc: T.matmul(
                            pacc[:, pa, :], lhsT=wbf[:, 1, kc, m2 * 128:(m2 + 1) * 128], rhs=hT[:, kc, tc * 512:(tc + 1) * 512],
                            start=(kc == 0), stop=(kc == KC - 1)),
                            reads=hT_b[tc * 4:(tc + 1) * 4] + [wbf_b[1]], writes=[pacc_b[pa]], sig=(kc == KC - 1))
                    fw.op(act, lambda pa=pa, m2=m2, tc=tc: A.copy(out=fuT[:, m2, tc * 512:(tc + 1) * 512], in_=pacc[:, pa, :]),
                          reads=[pacc_b[pa]], writes=[fuT_b] + PT_b)
                    pa = nxt("pa", 4)
                    for kc in range(KC):
                        fw.op(pe, lambda kc=kc, pa=pa, m2=m2, tc=tc: T.matmul(
                            pacc[:, pa, :], lhsT=wbf[:, 1, kc, 256 + m2 * 128:256 + (m2 + 1) * 128], rhs=hT[:, kc, tc * 512:(tc + 1) * 512],
                            start=(kc == 0), stop=(kc == KC - 1)),
                            reads=hT_b[tc * 4:(tc + 1) * 4] + [wbf_b[1]], writes=[pacc_b[pa]], sig=(kc == KC - 1))
                    fw.op(act, lambda pa=pa, m2=m2, tc=tc: A.activation(out=yT[:, 6 + m2, tc * 512:(tc + 1) * 512], in_=pacc[:, pa, :], func=ACTF.Silu),
                          reads=[pacc_b[pa]], writes=[yT_b[6 + m2][tc]])
            wall = wbf[:, :, :, :].rearrange("p s k c -> p (s k c)").rearrange("p (k c) -> p k c", k=KC)
            for kc in range(KC):
                sl = nxt("stg", NSTG)
                if kc < 4:
                    fw.dma(s_stg[sl], stage[0:64, sl, :], wout_d[l, kc * 64:(kc + 1) * 64, :], writes=[stg_b[sl]])
                    fw.dma(s_stg[sl], stage[64:128, sl, :], wout_d[l, (4 + kc) * 64:(5 + kc) * 64, :], writes=[stg_b[sl]])
                else:
                    fw.dma(s_stg[sl], stage[:, sl, :], wout_d[l, kc * 128:(kc + 1) * 128, :], writes=[stg_b[sl]])
                fw.op(pool, lambda kc=kc, sl=sl: P.tensor_tensor(out=wall[:, kc, :], in0=stage[:, sl, :], in1=gateb[:], op=ALU.mult),
                      reads=[stg_b[sl], gate_b], writes=wbf_b)
            for i in range(NT):
                pa = nxt("pa", 4)
                for m2 in range(2):
                    fw.op(pe, lambda m2=m2, pa=pa, i=i: T.matmul(
                        pacc[:, pa, m2 * 256:(m2 + 1) * 256], lhsT=fuT[:, m2, i * 128:(i + 1) * 128],
                        rhs=ABbd[:, m2, :, :].rearrange("p a n -> p (a n)"), start=True, stop=True),
                        reads=[fuT_b, AB_b], writes=[pacc_b[pa]], sig=(m2 == 1))
                fw.op(act, lambda pa=pa, i=i: A.copy(out=UAB[:, i, :, :, :].rearrange("p m c n -> p (m c n)"), in_=pacc[:, pa, :]),
                      reads=[pacc_b[pa]], writes=[UAB_b] + QT_b)
            for half in range(2):
                pas = [[nxt("pa", 4) for _ in range(2)] for _ in range(2)]
                for a in range(NT):
                    ts = nxt("tab", NTAB)
                    fw.dma(s_tab[ts], tabv[:, ts, :, :], dft_d[a * 128:(a + 1) * 128, :, half * 1024:(half + 1) * 1024], writes=[tab_b[ts]] + hT_b)
                    for m2 in range(2):
                        for kq in range(2):
                            for ab in range(2):
                                fw.op(pe, lambda m2=m2, kq=kq, ab=ab, a=a, ts=ts, pas=pas: T.matmul(
                                    pacc[:, pas[m2][kq], :], lhsT=UAB[:, a, m2, ab, :], rhs=tabv[:, ts, ab, kq * 512:(kq + 1) * 512],
                                    start=(a == 0 and ab == 0), stop=(a == NT - 1 and ab == 1)),
                                    reads=[UAB_b, tab_b[ts]], writes=[pacc_b[pas[m2][kq]]], sig=((a == NT - 1 and ab == 1) or (m2 == 1 and kq == 1 and ab == 1)))
                for m2 in range(2):
                    for kq in range(2):
                        tcc = half * 2 + kq
                        fw.op(dve, lambda m2=m2, kq=kq, tcc=tcc, pas=pas: V.tensor_tensor(
                            out=yT[:, 6 + m2, tcc * 512:(tcc + 1) * 512], in0=pacc[:, pas[m2][kq], :], in1=yT[:, 6 + m2, tcc * 512:(tcc + 1) * 512], op=ALU.mult),
                            reads=[pacc_b[pas[m2][kq]], yT_b[6 + m2][tcc]], writes=[yT_b[6 + m2][tcc]])

            if stop <= 9:
                return
            fw.dma(s_lng, lngb[:], lng_d[l].partition_broadcast(128), writes=[lng_b])
            fw.dma(s_lnb, lnbb[:], lnb_d[l].partition_broadcast(128), writes=[lnb_b])
            def fin_a(i):
                bp = [4, 0, 2][i % 3]
                xr = nxt("xres", NXR)
                fw.dma(s_xres[xr], xres[:, xr, :], x_src[b, i * 128:(i + 1) * 128, :], reads=[ydram_b[b][i]], writes=[xres_b[xr]])
                for n in range(2):
                    for kc in range(KC):
                        fw.op(pe, lambda kc=kc, n=n: T.matmul(
                            pacc[:, bp + n, :], lhsT=yT[:, kc, i * 128:(i + 1) * 128], rhs=wall[:, kc, n * 512:(n + 1) * 512],
                            start=(kc == 0), stop=(kc == KC - 1)),
                            reads=[yT_b[kc][i // 4]] + wbf_b, writes=[pacc_b[bp + n]], sig=(kc == KC - 1))
                ss = nxt("st", 4)
                for n in range(2):
                    fw.op(dve, lambda n=n: V.scalar_tensor_tensor(
                        out=xres[:, xr, n * 512:(n + 1) * 512], in0=xres[:, xr, n * 512:(n + 1) * 512], scalar=float(ALPHA),
                        in1=pacc[:, bp + n, :], op0=ALU.mult, op1=ALU.add),
                        reads=[pacc_b[bp + n], xres_b[xr]], writes=[xres_b[xr]])
                for n in range(2):
                    fw.op(dve, lambda n=n: V.bn_stats(out=stt[:, ss, n, :], in_=xres[:, xr, n * 512:(n + 1) * 512]),
                          reads=[xres_b[xr]], writes=[stt_b[ss]])
                fw.op(dve, lambda: V.bn_aggr(out=mv[:, ss, :], in_=stt[:, ss, :, :].rearrange("p a b -> p (a b)")), reads=[stt_b[ss]], writes=[mv_b[ss]])
                fw.op(dve, lambda: V.tensor_scalar(out=sml[:, ss, 0:1], in0=mv[:, ss, 1:2], scalar1=1.0, scalar2=LN_EPS,
                                                   op0=ALU.mult, op1=ALU.add), reads=[mv_b[ss]], writes=[sml_b[ss]])
                fw.op(pool, lambda: P.tensor_tensor(out=sml[:, ss, 0:1], in0=sml[:, ss, 0:1], in1=mhalf[:, 0:1], op=ALU.pow),
                      reads=[sml_b[ss], cd_b, cm_b], writes=[sml_b[ss]])
                return xr, ss

            def fin_b(i, xr, ss):
                fw.op(dve, lambda: V.scalar_tensor_tensor(out=xres[:, xr, :], in0=xres[:, xr, :], scalar=mv[:, ss, 0:1], in1=lngb[:],
                                                          op0=ALU.subtract, op1=ALU.mult),
                      reads=[xres_b[xr], mv_b[ss], lng_b], writes=[xres_b[xr]])
                fw.op(dve, lambda: V.scalar_tensor_tensor(out=xres[:, xr, :], in0=xres[:, xr, :], scalar=sml[:, ss, 0:1], in1=lnbb[:],
                                                          op0=ALU.mult, op1=ALU.add),
                      reads=[xres_b[xr], sml_b[ss], lnb_b], writes=[xres_b[xr]])
                fw.dma(s_xres[xr], y_d[b, i * 128:(i + 1) * 128, :], xres[:, xr, :], reads=[xres_b[xr]], writes=[ydram_b[b][i]], q=fw.act)

            pend = fin_a(0)
            for i in range(NT):
                nxt_pend = fin_a(i + 1) if i + 1 < NT else None
                fin_b(i, *pend)
                pend = nxt_pend

        for l in range(n_layers):
            for b in range(n_seq):
                if b == 0:
                    emit_mod(l)
                emit_layer(b, l, x_d if l == 0 else y_d, l == n_layers - 1)
        fw.wait_all(fw.sp, s_xres)
        build_program.stats = (fw.ninst, fw.nwait, {e.name: e.sem.count for e in (pe, act, dve, pool)})
    return nc


def _constants():
    bf = ml_dtypes.bfloat16
    t = np.arange(S)
    rows = (t // 64).astype(np.float32)
    cols = (t % 64).astype(np.float32)
    inv = (1.0 / (10000.0 ** (np.arange(0, 32, 2, dtype=np.float32) / np.float32(32)))).astype(np.float32)
    ra = (rows[:, None] * inv[None, :]).astype(np.float32)
    ca = (cols[:, None] * inv[None, :]).astype(np.float32)
    ang = np.concatenate([ra, ca], axis=1)
    cos = np.cos(ang.astype(np.float64)).astype(np.float32)
    sin = np.sin(ang.astype(np.float64)).astype(np.float32)
    rope_cos = np.ascontiguousarray(cos.reshape(NT, 128, 32).transpose(1, 0, 2))
    rope_sin = np.ascontiguousarray(sin.reshape(NT, 128, 32).transpose(1, 0, 2))
    ks = (np.outer(t, t) % S).astype(np.float64) * (2.0 * np.pi / S)
    dft_cs = np.ascontiguousarray(np.stack([np.cos(ks), -np.sin(ks)], axis=1).astype(np.float32).astype(bf))
    m = np.arange(64)
    a64 = (np.outer(m, m) % 64).astype(np.float64) * (2.0 * np.pi / 64)
    sc = 1.0 / np.sqrt(float(S) * 64.0)
    c64 = np.cos(a64) * sc
    s64 = np.sin(a64) * sc
    z = np.zeros((64, 64))
    c64bd = np.block([[c64, z], [z, c64]]).astype(np.float32)
    s64bd = np.block([[s64, z], [z, s64]]).astype(np.float32)
    def hilo(a):
        hi = a.astype(bf)
        lo = (a - hi.astype(np.float32)).astype(bf)
        return np.stack([hi, lo], axis=1)
    cs64bd = np.ascontiguousarray(np.stack([hilo(c64bd), hilo(s64bd)], axis=1))
    return {
        "ident_bf": np.eye(128, dtype=np.float32).astype(bf),
        "ident_f": np.eye(128, dtype=np.float32),
        "rope_cos": rope_cos, "rope_sin": rope_sin,
        "dft_cs": dft_cs,
        "cs64bd": cs64bd,
    }


def make_in_maps(x, c, w_ada, b_ada, w_in, q_gain, k_gain, conv_w, f_mix, w_out, ln_g, ln_b, ncores=NCORES):
    f = lambda a: np.ascontiguousarray(np.asarray(a, dtype=np.float32))
    consts = _constants()
    shared = {
        "w_ada": f(w_ada), "w_in": f(w_in), "q_gain": f(q_gain), "k_gain": f(k_gain),
        "f_mix": f(f_mix), "w_out": f(w_out), "ln_g": f(ln_g), "ln_b": f(ln_b),
        "b_adaT": f(np.asarray(b_ada).reshape(DEPTH, 24, 128).transpose(2, 0, 1)),
        "conv_wT": f(np.asarray(conv_w).reshape(DEPTH, 3, 2, 128).transpose(3, 0, 2, 1)),
    }
    shared.update(consts)
    x = np.asarray(x, dtype=np.float32)
    c = np.asarray(c, dtype=np.float32)
    maps = []
    for ci in range(ncores):
        m = dict(shared)
        m["x"] = np.ascontiguousarray(x[ci * BPC:(ci + 1) * BPC])
        m["cT"] = f(c[ci * BPC:(ci + 1) * BPC].reshape(BPC, KC, 128).transpose(2, 1, 0))
        maps.append(m)
    return maps


def kernel(x, c, w_ada, b_ada, w_in, q_gain, k_gain, conv_w, f_mix, w_out, ln_g, ln_b):
    maps = make_in_maps(x, c, w_ada, b_ada, w_in, q_gain, k_gain, conv_w, f_mix, w_out, ln_g, ln_b)
    nc = build_program()
    res = run_bass_kernel_spmd(nc, maps, core_ids=list(range(NCORES)))
    return np.concatenate([np.asarray(r["y"], dtype=np.float32) for r in res.results], axis=0)
```

```python
import contextlib
import numpy as np
import ml_dtypes
import concourse.bass as bass
import concourse.mybir as mybir
from concourse.bass_utils import run_bass_kernel_spmd

F32 = mybir.dt.float32
BF16 = mybir.dt.bfloat16
ALU = mybir.AluOpType
ACTF = mybir.ActivationFunctionType
AX = mybir.AxisListType

S = 2048
D = 1024
NT = 16
KC = 8
DEPTH = 4
IN_W = 2816
ALPHA = (2 * DEPTH) ** 0.25
RMS_EPS = 1e-6
LN_EPS = 1e-5
NCORES = 8
BPC = 2


class Buf:
    __slots__ = ("name", "w", "r")

    def __init__(self, name):
        self.name = name
        self.w = None
        self.r = []


class Sem:
    def __init__(self, h, name):
        self.h = h
        self.name = name
        self.count = 0


class Eng:
    def __init__(self, name, h, sem):
        self.name = name
        self.h = h
        self.sem = sem
        self.seen = {}


class FW:
    def __init__(self, nc, stack):
        self.nc = nc
        self.stack = stack
        self.pe = self._eng("pe", nc.tensor)
        self.act = self._eng("act", nc.scalar)
        self.dve = self._eng("dve", nc.vector)
        self.pool = self._eng("pool", nc.gpsimd)
        self.sp = self._eng("sp", nc.sync)
        self.ninst = 0
        self.nwait = 0

    def sem(self, name):
        h = self.stack.enter_context(self.nc.semaphore(name))
        return Sem(h, name)

    def _eng(self, name, h):
        return Eng(name, h, self.sem("s_" + name))

    def sbuf(self, name, shape, dt):
        return self.stack.enter_context(self.nc.sbuf_tensor(name, shape, dt))

    def psum(self, name, shape, dt):
        return self.stack.enter_context(self.nc.psum_tensor(name, shape, dt))

    def _waits(self, eng, reads, writes, is_dma=False, my_sem=None):
        need = {}

        def add(tok):
            s, v, who = tok
            if (not is_dma) and who == eng.name and eng.name == "pe":
                return
            if v > need.get(s, (None, 0))[1]:
                need[s] = (s, v)

        for b in reads:
            if b.w is not None:
                add(b.w)
        for b in writes:
            if b.w is not None and not (is_dma and b.w[0] is my_sem):
                add(b.w)
            for r in b.r:
                add(r)
        for s, v in need.values():
            if eng.seen.get(s, 0) >= v:
                continue
            eng.h.wait_ge(s.h, v)
            eng.seen[s] = v
            self.nwait += 1

    def op(self, eng, fn, reads=(), writes=(), sig=True):
        self._waits(eng, reads, writes)
        ins = fn()
        self.ninst += 1
        if sig:
            eng.sem.count += 1
            ins.then_inc(eng.sem.h, 1)
            tok = (eng.sem, eng.sem.count, eng.name)
        else:
            tok = (eng.sem, eng.sem.count + 1, eng.name)
        for b in reads:
            b.r.append(tok)
        for b in writes:
            b.w = tok
            b.r = []
        return ins

    def dma(self, sem, out, in_, reads=(), writes=(), q=None, **kw):
        q = q or self.sp
        self._waits(q, reads, writes, is_dma=True, my_sem=sem)
        ins = q.h.dma_start(out=out, in_=in_, **kw)
        ins.then_inc(sem.h, 16)
        sem.count += 16
        tok = (sem, sem.count, "dma:" + sem.name)
        for b in reads:
            b.r.append(tok)
        for b in writes:
            b.w = tok
            b.r = []
        self.ninst += 1
        return ins

    def wait_all(self, eng, sems):
        for s in sems:
            if s.count > 0 and eng.seen.get(s, 0) < s.count:
                eng.h.wait_ge(s.h, s.count)
                eng.seen[s] = s.count


def build_program(n_layers=DEPTH, n_seq=BPC, stop=99):
    nc = bass.Bass("TRN2", target_bir_lowering=False)
    dt_in = lambda name, shape, dt=F32: nc.dram_tensor(name, shape, dt, kind="ExternalInput").ap()
    x_d = dt_in("x", [BPC, S, D])
    cT_d = dt_in("cT", [128, KC, BPC])
    wada_d = dt_in("w_ada", [DEPTH, D, 3 * D])
    badaT_d = dt_in("b_adaT", [128, DEPTH, 24])
    win_d = dt_in("w_in", [DEPTH, D, IN_W])
    qg_d = dt_in("q_gain", [DEPTH, 64])
    kg_d = dt_in("k_gain", [DEPTH, 64])
    cwT_d = dt_in("conv_wT", [128, DEPTH, 2, 3])
    fmix_d = dt_in("f_mix", [DEPTH, 4, 64, 64])
    wout_d = dt_in("w_out", [DEPTH, D, D])
    lng_d = dt_in("ln_g", [DEPTH, D])
    lnb_d = dt_in("ln_b", [DEPTH, D])
    identb_d = dt_in("ident_bf", [128, 128], BF16)
    identf_d = dt_in("ident_f", [128, 128])
    ropec_d = dt_in("rope_cos", [128, NT, 32])
    ropes_d = dt_in("rope_sin", [128, NT, 32])
    dft_d = dt_in("dft_cs", [S, 2, S], BF16)
    csbd_d = dt_in("cs64bd", [128, 2, 2, 128], BF16)
    y_d = nc.dram_tensor("y", [BPC, S, D], F32, kind="ExternalOutput").ap()

    with contextlib.ExitStack() as st:
        fw = FW(nc, st)
        pe, act, dve, pool = fw.pe, fw.act, fw.dve, fw.pool
        V, A, P, T = nc.vector, nc.scalar, nc.gpsimd, nc.tensor

        hT = fw.sbuf("hT", [128, KC, S], BF16)
        yT = fw.sbuf("yT", [128, 8, S], BF16)
        QT = fw.sbuf("QT", [128, 4, S], BF16)
        KTp = fw.sbuf("KTp", [128, 2, S], BF16)
        Vaug = fw.sbuf("Vaug", [128, NT, 2, 128], BF16)
        NPT = 4
        PT = fw.sbuf("PT", [128, NPT, 2, 512], BF16)
        fuT = PT[:, :, :, :].rearrange("p a b c -> p (a b c)").rearrange("p (m s) -> p m s", m=2)
        wbf = fw.sbuf("wbf", [128, 2, KC, 512], BF16)
        NSTG = 3
        stage = fw.sbuf("stage", [128, NSTG, 1024], F32)
        NXIN = 3
        xin = fw.sbuf("xin", [128, NXIN, D], F32)
        NXR = 3
        xres = fw.sbuf("xres", [128, NXR, D], F32)
        lngb = fw.sbuf("lngb", [128, D], F32)
        lnbb = fw.sbuf("lnbb", [128, D], F32)
        gateb = fw.sbuf("gateb", [128, D], F32)
        ropec = fw.sbuf("ropec", [128, NT, 32], F32)
        ropes = fw.sbuf("ropes", [128, NT, 32], F32)
        zt = xin[:, :, :].rearrange("p a d -> p (a d)")[:, 0:S + 2]
        xh = fw.sbuf("xh", [128, 2, D], BF16)
        NWK = 6
        wkf = fw.sbuf("wk", [128, NWK, 640], F32)
        qrb = fw.sbuf("qrb", [128, 2, 640], BF16)
        identb = fw.sbuf("identb", [128, 128], BF16)
        identf = fw.sbuf("identf", [128, 128], F32)
        onesb = fw.sbuf("onesb", [128, 128], BF16)
        dg = fw.sbuf("dg", [128, 2, 128], F32)
        dgb = fw.sbuf("dgb", [128, 2, 2, 128], BF16)
        csbd = fw.sbuf("csbd", [128, 2, 2, 128], BF16)
        fm = fw.sbuf("fm", [128, 2, 64], F32)
        fmb = fw.sbuf("fmb", [128, 2, 2, 64], BF16)
        cTb = fw.sbuf("cTb", [128, 2, KC, BPC], BF16)
        whi = fw.sbuf("whi", [128, 2, 1024], BF16)
        ABbd = fw.sbuf("ABbd", [128, 2, 2, 128], BF16)
        cTs = fw.sbuf("cTs", [128, KC, BPC], F32)
        badaT = fw.sbuf("badaT", [128, DEPTH, 24], F32)
        modT = fw.sbuf("modT", [128, 24, BPC], F32)
        s1T = fw.sbuf("s1T", [128, 8, BPC], F32)
        g1T = fw.sbuf("g1T", [128, 8, BPC], F32)
        cwT = fw.sbuf("cwT", [128, DEPTH, 2, 3], F32)
        qgb = fw.sbuf("qgb", [128, 64], F32)
        kgb = fw.sbuf("kgb", [128, 64], F32)
        mhalf = fw.sbuf("mhalf", [128, 16], F32)
        stt = fw.sbuf("stt", [128, 4, 2, 6], F32)
        mv = fw.sbuf("mv", [128, 4, 2], F32)
        sml = fw.sbuf("sml", [128, 4, 32], F32)

        pacc = fw.psum("pacc", [128, 8, 512], F32)
        ptrv = [pacc[:, 6, :].bitcast(BF16), pacc[:, 7, :].bitcast(BF16)]

        B = lambda n: Buf(n)
        hT_b = [B("hT%d" % i) for i in range(NT)]
        NTAB = 6
        tab_b = [B("tab%d" % i) for i in range(NTAB)]
        yT_b = [[B("yT%d_%d" % (c, t)) for t in range(4)] for c in range(8)]
        QT_b = [B("QT%d" % j) for j in range(4)]
        UAB_b = B("UAB")
        KT_b = B("KTp")
        Va_b = B("Vaug")
        PT_b = [B("PT%d" % i) for i in range(NPT)]
        fuT_b = B("fuT")
        wbf_b = [B("wbf0"), B("wbf1")]
        stg_b = [B("stg%d" % i) for i in range(NSTG)]
        xin_b = [B("xin%d" % i) for i in range(NXIN)]
        xres_b = [B("xres%d" % i) for i in range(NXR)]
        lng_b, lnb_b, gate_b = B("lng"), B("lnb"), B("gate")
        rope_b, z_b = B("rope"), B("z")
        xh_b = [B("xh0"), B("xh1")]
        wk_b = [B("wk%d" % i) for i in range(NWK)]
        qrb_b = [B("qrb0"), B("qrb1")]
        cd_b, cm_b = B("constd"), B("constm")
        dg_b = [B("dg0"), B("dg1")]
        fm_b, AB_b = B("fm"), B("ABbd")
        whi_b = [B("whi0"), B("whi1")]
        cTb_b = B("cTb")
        mod_b, s1_b, g1_b = B("modT"), B("s1T"), B("g1T")
        gain_b = B("gains")
        stt_b = [B("stt%d" % i) for i in range(4)]
        mv_b = [B("mv%d" % i) for i in range(4)]
        sml_b = [B("sml%d" % i) for i in range(4)]
        pacc_b = [B("pacc%d" % i) for i in range(8)]
        ptr_b = [pacc_b[6], pacc_b[7]]
        ydram_b = [[B("yd%d_%d" % (b, i)) for i in range(NT)] for b in range(BPC)]

        s_const = fw.sem("d_const")
        s_stg = [fw.sem("d_stg%d" % i) for i in range(NSTG)]
        s_xin = [fw.sem("d_xin%d" % i) for i in range(NXIN)]
        s_xres = [fw.sem("d_xres%d" % i) for i in range(NXR)]
        s_tab = [fw.sem("d_tab%d" % i) for i in range(NTAB)]
        s_lng, s_lnb, s_gain, s_fm = fw.sem("d_lng"), fw.sem("d_lnb"), fw.sem("d_gain"), fw.sem("d_fm")

        tabv = hT[:, 0:NTAB, :].rearrange("p a (c k) -> p a c k", c=2)
        UAB = QT[:, :, :].rearrange("p j (a m c n) -> p (j a) m c n", a=4, m=2, c=2)

        for (dst, src) in [(identb, identb_d), (identf, identf_d), (ropec, ropec_d), (ropes, ropes_d),
                           (csbd, csbd_d), (cTs, cT_d), (badaT, badaT_d), (cwT, cwT_d)]:
            fw.dma(s_const, dst[:], src, writes=[cd_b, rope_b])
        fw.op(pool, lambda: P.memset(Vaug[:], 1.0), writes=[Va_b])
        fw.op(pool, lambda: P.memset(KTp[:], 0.0), writes=[KT_b])
        fw.op(pool, lambda: P.memset(mhalf[:], -0.5), writes=[cm_b])
        fw.op(pool, lambda: P.memset(onesb[:], 1.0), writes=[cm_b])
        fw.op(dve, lambda: V.tensor_copy(out=cTb[:, 0, :, :], in_=cTs[:]), reads=[cd_b], writes=[cTb_b])
        fw.op(dve, lambda: V.tensor_tensor(out=cTb[:, 1, :, :], in0=cTs[:], in1=cTb[:, 0, :, :], op=ALU.subtract), reads=[cd_b, cTb_b], writes=[cTb_b])
        fw.op(pool, lambda: P.memset(ABbd[:], 0.0), writes=[AB_b])

        rr = {"stg": 0, "xin": 0, "xres": 0, "tab": 0, "wk": 0, "whi": 0, "pp": 0, "st": 0, "pa": 0, "ptr": 0, "pt": 0, "xh": 0, "qrb": 0, "dg": 0}

        def nxt(key, n):
            v = rr[key]
            rr[key] = (v + 1) % n
            return v

        def rstd_from(var_ap, out_ap, slot_b, scale, eps):
            n = var_ap.shape[-1]
            fw.op(dve, lambda: V.tensor_scalar(out=out_ap, in0=var_ap, scalar1=scale, scalar2=eps,
                                               op0=ALU.mult, op1=ALU.add), reads=[slot_b], writes=[slot_b])
            fw.op(pool, lambda: P.tensor_tensor(out=out_ap, in0=out_ap, in1=mhalf[:, 0:n], op=ALU.pow),
                  reads=[slot_b, cd_b, cm_b], writes=[slot_b])

        def emit_mod(l):
            pa = 5
            for oc in range(24):
                sl = nxt("stg", NSTG)
                fw.dma(s_stg[sl], stage[:, sl, :].rearrange("p (k c) -> p k c", k=KC),
                       wada_d[l, :, oc * 128:(oc + 1) * 128].rearrange("(k p) c -> p k c", p=128),
                       writes=[stg_b[sl]])
                ws = nxt("whi", 2)
                if oc % 3 == 0:
                    fw.op(dve, lambda sl=sl, ws=ws: V.tensor_copy(out=whi[:, ws, :], in_=stage[:, sl, :]), reads=[stg_b[sl]], writes=[whi_b[ws]])
                elif oc % 3 == 1:
                    fw.op(act, lambda sl=sl, ws=ws: A.copy(out=whi[:, ws, :], in_=stage[:, sl, :]), reads=[stg_b[sl]], writes=[whi_b[ws]])
                else:
                    fw.op(pool, lambda sl=sl, ws=ws: P.tensor_copy(out=whi[:, ws, :], in_=stage[:, sl, :]), reads=[stg_b[sl]], writes=[whi_b[ws]])
                for kc in range(KC):
                    for hl in range(2):
                        fw.op(pe, lambda kc=kc, ws=ws, oc=oc, hl=hl: T.matmul(
                            pacc[:, pa, oc * 2:oc * 2 + 2], lhsT=whi[:, ws, kc * 128:(kc + 1) * 128],
                            rhs=cTb[:, hl, kc, :], start=(kc == 0 and hl == 0), stop=(kc == KC - 1 and hl == 1)),
                            reads=[whi_b[ws], cTb_b], writes=[pacc_b[pa]], sig=(kc == KC - 1 and hl == 1))
            fw.op(dve, lambda: V.tensor_tensor(
                out=modT[:], in0=pacc[:, pa, 0:48].rearrange("p (o b) -> p o b", b=BPC),
                in1=badaT[:, l, :].unsqueeze(2).to_broadcast([128, 24, BPC]), op=ALU.add),
                reads=[pacc_b[pa], cd_b, cm_b], writes=[mod_b])
            fw.op(dve, lambda: V.tensor_scalar(out=s1T[:], in0=modT[:, 8:16, :], scalar1=1.0, scalar2=None, op0=ALU.add),
                  reads=[mod_b], writes=[s1_b])
            fw.op(dve, lambda: V.tensor_scalar(out=g1T[:], in0=modT[:, 16:24, :], scalar1=1.0, scalar2=None, op0=ALU.add),
                  reads=[mod_b], writes=[g1_b])
            fw.dma(s_gain, qgb[:], qg_d[l].partition_broadcast(128), writes=[gain_b])
            fw.dma(s_gain, kgb[:], kg_d[l].partition_broadcast(128), writes=[gain_b])
            fw.dma(s_fm, fm[:], fmix_d[l].rearrange("(a g) m d -> (g m) a d", a=2), writes=[fm_b])
            fw.op(dve, lambda: V.tensor_copy(out=fmb[:, :, 0, :], in_=fm[:]), reads=[fm_b], writes=[fm_b])
            fw.op(dve, lambda: V.tensor_tensor(out=fmb[:, :, 1, :], in0=fm[:], in1=fmb[:, :, 0, :], op=ALU.subtract), reads=[fm_b], writes=[fm_b])
            for m2 in range(2):
                for ab in range(2):
                    for t3, (th, fh) in enumerate(((0, 0), (0, 1), (1, 0))):
                        fw.op(pe, lambda m2=m2, ab=ab, th=th, fh=fh, t3=t3: T.matmul(
                            pacc[:, pa, 64 + (m2 * 2 + ab) * 64: 64 + (m2 * 2 + ab + 1) * 64], lhsT=csbd[:, ab, th, :], rhs=fmb[:, m2, fh, :],
                            start=(t3 == 0), stop=(t3 == 2)), reads=[fm_b, cd_b, cm_b], writes=[pacc_b[pa]], sig=(t3 == 2))
            for m2 in range(2):
                for ab in range(2):
                    c0 = 64 + (m2 * 2 + ab) * 64
                    fw.op(dve, lambda m2=m2, ab=ab, c0=c0: V.tensor_copy(out=ABbd[0:64, m2, ab, 0:64], in_=pacc[0:64, pa, c0:c0 + 64]),
                          reads=[pacc_b[pa]], writes=[AB_b])
                    fw.op(dve, lambda m2=m2, ab=ab, c0=c0: V.tensor_copy(out=ABbd[64:128, m2, ab, 64:128], in_=pacc[64:128, pa, c0:c0 + 64]),
                          reads=[pacc_b[pa]], writes=[AB_b])

        def load_wgroup(l, slot, src0, ncol_src, casts):
            for _ in wgroup_gen(l, slot, src0, ncol_src, casts):
                pass

        def wgroup_gen(l, slot, src0, ncol_src, casts):
            for kc in range(KC):
                sl = nxt("stg", NSTG)
                fw.dma(s_stg[sl], stage[:, sl, 0:ncol_src], win_d[l, kc * 128:(kc + 1) * 128, src0:src0 + ncol_src],
                       writes=[stg_b[sl]])
                for cf in casts:
                    o_ap, i_ap = cf(wbf[:, slot, kc, :], stage[:, sl, :])
                    fw.op(pool, lambda o_ap=o_ap, i_ap=i_ap: P.tensor_copy(out=o_ap, in_=i_ap),
                          reads=[stg_b[sl]], writes=[wbf_b[slot]])
                yield

        perm_cast = lambda o, i: (o[:, 0:512].rearrange("p (j h d) -> p j h d", j=4, h=2),
                                  i[:, 0:512].rearrange("p (h j d) -> p j h d", h=2, j=4))

        def rope(src, dst, i, H, sb, wkA, wkB, src_bufs, dst_bufs):
            sv = src.rearrange("p (h a b d) -> p h a b d", h=H, a=2, b=2)
            dv = dst.rearrange("p (h a b d) -> p h a b d", h=H, a=2, b=2)
            cs = ropec[:, i, :].rearrange("p (a d) -> p a d", a=2).unsqueeze(1).to_broadcast([128, H, 2, 16])
            sn = ropes[:, i, :].rearrange("p (a d) -> p a d", a=2).unsqueeze(1).to_broadcast([128, H, 2, 16])
            n = H * 32
            t1 = wkA[:, 0:n].rearrange("p (h a d) -> p h a d", h=H, a=2)
            t2 = wkA[:, n:2 * n].rearrange("p (h a d) -> p h a d", h=H, a=2)
            t3 = wkB[:, 0:n].rearrange("p (h a d) -> p h a d", h=H, a=2)
            t4 = wkB[:, n:2 * n].rearrange("p (h a d) -> p h a d", h=H, a=2)
            x0 = sv[:, :, :, 0, :]
            x1 = sv[:, :, :, 1, :]
            wa, wb = sb
            fw.op(dve, lambda: V.tensor_tensor(out=t1, in0=x0, in1=cs, op=ALU.mult), reads=src_bufs + [rope_b], writes=[wa])
            fw.op(dve, lambda: V.tensor_tensor(out=t2, in0=x1, in1=sn, op=ALU.mult), reads=src_bufs + [rope_b], writes=[wa])
            fw.op(dve, lambda: V.tensor_tensor(out=t3, in0=x1, in1=cs, op=ALU.mult), reads=src_bufs + [rope_b], writes=[wb])
            fw.op(dve, lambda: V.tensor_tensor(out=t4, in0=x0, in1=sn, op=ALU.mult), reads=src_bufs + [rope_b], writes=[wb])
            fw.op(dve, lambda: V.tensor_tensor(out=dv[:, :, :, 0, :], in0=t1, in1=t2, op=ALU.subtract), reads=[wa], writes=dst_bufs)
            fw.op(dve, lambda: V.tensor_tensor(out=dv[:, :, :, 1, :], in0=t3, in1=t4, op=ALU.add), reads=[wb], writes=dst_bufs)

        def emit_layer(b, l, x_src, last):
            for kc in range(KC):
                ds = nxt("dg", 2)
                pa = 4 + (kc // 4)
                fw.op(dve, lambda kc=kc, ds=ds: V.tensor_scalar(out=dg[:, ds, :], in0=identf[:], scalar1=g1T[:, kc, b:b + 1],
                                                              scalar2=None, op0=ALU.mult),
                      reads=[cd_b, cm_b, g1_b], writes=[dg_b[ds]])
                fw.op(dve, lambda ds=ds: V.tensor_copy(out=dgb[:, ds, 0, :], in_=dg[:, ds, :]), reads=[dg_b[ds]], writes=[dg_b[ds]])
                fw.op(dve, lambda ds=ds: V.tensor_tensor(out=dgb[:, ds, 1, :], in0=dg[:, ds, :], in1=dgb[:, ds, 0, :], op=ALU.subtract), reads=[dg_b[ds]], writes=[dg_b[ds]])
                for hl in range(2):
                    fw.op(pe, lambda kc=kc, ds=ds, pa=pa, hl=hl: T.matmul(pacc[:, pa, (kc % 4) * 128:(kc % 4 + 1) * 128], lhsT=onesb[:], rhs=dgb[:, ds, hl, :],
                                                                        start=(hl == 0), stop=(hl == 1)),
                          reads=[dg_b[ds], cd_b, cm_b], writes=[pacc_b[pa]], sig=(hl == 1))
            for h2 in range(2):
                fw.op(act, lambda h2=h2: A.copy(out=gateb[:, h2 * 512:(h2 + 1) * 512], in_=pacc[:, 4 + h2, :]),
                      reads=[pacc_b[4 + h2]], writes=[gate_b])

            if stop <= 1:
                return
            import itertools
            wpre = itertools.chain(wgroup_gen(l, 0, 512, 256, [lambda o, i: (o[:, 0:256], i[:, 0:256])]),
                                   wgroup_gen(l, 1, 0, 512, [perm_cast]))

            if stop <= 1.5:
                return
            def ln1_a(i):
                xs_ = nxt("xin", NXIN)
                fw.dma(s_xin[xs_], xin[:, xs_, :], x_src[b, i * 128:(i + 1) * 128, :],
                       reads=[ydram_b[b][i]], writes=[xin_b[xs_]] + ([z_b] if i < NXIN else []))
                ss = nxt("st", 4)
                for h2 in range(2):
                    fw.op(dve, lambda h2=h2: V.bn_stats(out=stt[:, ss, h2, :], in_=xin[:, xs_, h2 * 512:(h2 + 1) * 512]),
                          reads=[xin_b[xs_]], writes=[stt_b[ss]])
                fw.op(dve, lambda: V.bn_aggr(out=mv[:, ss, :], in_=stt[:, ss, :, :].rearrange("p a b -> p (a b)")), reads=[stt_b[ss]], writes=[mv_b[ss]])
                fw.op(dve, lambda: V.tensor_scalar(out=sml[:, ss, 0:1], in0=mv[:, ss, 1:2], scalar1=1.0, scalar2=LN_EPS,
                                                   op0=ALU.mult, op1=ALU.add), reads=[mv_b[ss]], writes=[sml_b[ss]])
                fw.op(pool, lambda: P.tensor_tensor(out=sml[:, ss, 0:1], in0=sml[:, ss, 0:1], in1=mhalf[:, 0:1], op=ALU.pow),
                      reads=[sml_b[ss], cd_b, cm_b], writes=[sml_b[ss]])
                return xs_, ss

            def ln1_b(i, xs_, ss):
                hs = nxt("xh", 2)
                fw.op(dve, lambda: V.tensor_scalar(
                    out=xh[:, hs, :], in0=xin[:, xs_, :], scalar1=mv[:, ss, 0:1], scalar2=sml[:, ss, 0:1],
                    op0=ALU.subtract, op1=ALU.mult), reads=[xin_b[xs_], mv_b[ss], sml_b[ss]], writes=[xh_b[hs]])
                pb = nxt("ptr", 2)
                for kc in range(KC):
                    fw.op(pe, lambda kc=kc: T.transpose(ptrv[pb][:, kc * 128:(kc + 1) * 128], xh[:, hs, kc * 128:(kc + 1) * 128], identb[:]),
                          reads=[xh_b[hs], cd_b, cm_b], writes=[ptr_b[pb]], sig=(kc == KC - 1))
                for kc in range(KC):
                    if True:
                        fw.op(act, lambda kc=kc: A.activation(
                            out=hT[:, kc, i * 128:(i + 1) * 128], in_=ptrv[pb][:, kc * 128:(kc + 1) * 128], func=ACTF.Identity,
                            scale=s1T[:, kc, b:b + 1], bias=modT[:, kc, b:b + 1]),
                            reads=[ptr_b[pb], s1_b, mod_b], writes=[hT_b[i]] + (tab_b if kc == 0 else []))
                    else:
                        fw.op(dve, lambda kc=kc: V.tensor_scalar(
                            out=hT[:, kc, i * 128:(i + 1) * 128], in0=ptrv[pb][:, kc * 128:(kc + 1) * 128],
                            scalar1=s1T[:, kc, b:b + 1], scalar2=modT[:, kc, b:b + 1], op0=ALU.mult, op1=ALU.add),
                            reads=[ptr_b[pb], s1_b, mod_b], writes=[hT_b[i]])

            pend = ln1_a(0)
            for i in range(NT):
                nxt_pend = ln1_a(i + 1) if i + 1 < NT else None
                next(wpre, None)
                ln1_b(i, *pend)
                pend = nxt_pend
            for _ in wpre:
                pass

            if stop <= 2:
                return
            HQK = 10

            def qk_a1(i):
                pp = nxt("pp", 3)
                bq, bk = 2 * pp, 2 * pp + 1
                for kc in range(KC):
                    fw.op(pe, lambda kc=kc: T.matmul(pacc[:, bq, :], lhsT=hT[:, kc, i * 128:(i + 1) * 128], rhs=wbf[:, 1, kc, :],
                                                     start=(kc == 0), stop=(kc == KC - 1)),
                          reads=[hT_b[i], wbf_b[1]], writes=[pacc_b[bq]], sig=(kc == KC - 1))
                for kc in range(KC):
                    fw.op(pe, lambda kc=kc: T.matmul(pacc[:, bk, 0:256], lhsT=hT[:, kc, i * 128:(i + 1) * 128], rhs=wbf[:, 0, kc, 0:256],
                                                     start=(kc == 0), stop=(kc == KC - 1)),
                          reads=[hT_b[i], wbf_b[0]], writes=[pacc_b[bk]], sig=(kc == KC - 1))
                fw.op(act, lambda: A.copy(out=Vaug[:, i, 0, 0:64], in_=pacc[:, bk, 128:192]), reads=[pacc_b[bk]], writes=[Va_b])
                fw.op(act, lambda: A.copy(out=Vaug[:, i, 1, 64:128], in_=pacc[:, bk, 192:256]), reads=[pacc_b[bk]], writes=[Va_b])
                n = HQK * 64
                src = pacc[:, bq:bq + 2, :].rearrange("p a c -> p (a c)")[:, 0:n]
                srcb = [pacc_b[bq], pacc_b[bk]]
                wA = nxt("wk", NWK)
                fw.op(act, lambda: A.activation(out=wkf[:, wA, 0:n], in_=src, func=ACTF.Square), reads=srcb, writes=[wk_b[wA]])
                return bq, bk, wA

            def qk_a2(i, bq, bk, wA):
                n = HQK * 64
                wB = nxt("wk", NWK); wC = nxt("wk", NWK)
                ss = nxt("st", 4)
                fw.op(dve, lambda: V.tensor_reduce(out=sml[:, ss, 0:HQK], in_=wkf[:, wA, 0:n].rearrange("p (h d) -> p h d", h=HQK),
                                                   axis=AX.X, op=ALU.add), reads=[wk_b[wA]], writes=[sml_b[ss]])
                rstd_from(sml[:, ss, 0:HQK], sml[:, ss, 0:HQK], sml_b[ss], 1.0 / 64, RMS_EPS)
                fw.op(dve, lambda: V.tensor_tensor(out=wkf[:, wB, 0:512].rearrange("p (h d) -> p h d", h=8), in0=pacc[:, bq, :].rearrange("p (h d) -> p h d", h=8),
                                                   in1=qgb[:].unsqueeze(1).to_broadcast([128, 8, 64]), op=ALU.mult),
                      reads=[pacc_b[bq], gain_b], writes=[wk_b[wB]])
                fw.op(dve, lambda: V.tensor_tensor(out=wkf[:, wB, 512:640].rearrange("p (h d) -> p h d", h=2), in0=pacc[:, bk, 0:128].rearrange("p (h d) -> p h d", h=2),
                                                   in1=kgb[:].unsqueeze(1).to_broadcast([128, 2, 64]), op=ALU.mult),
                      reads=[pacc_b[bk], gain_b], writes=[wk_b[wB]])
                rope(wkf[:, wB, 0:n], wkf[:, wB, 0:n], i, HQK, (wk_b[wC], wk_b[wA]), wkf[:, wC, :], wkf[:, wA, :], [wk_b[wB]], [wk_b[wB]])
                return wB, ss

            def qk_b(i, wB, ss):
                n = HQK * 64
                qs = nxt("qrb", 2)
                fw.op(dve, lambda: V.tensor_tensor(out=qrb[:, qs, 0:n].rearrange("p (h d) -> p h d", h=HQK), in0=wkf[:, wB, 0:n].rearrange("p (h d) -> p h d", h=HQK),
                                                   in1=sml[:, ss, 0:HQK].unsqueeze(2).to_broadcast([128, HQK, 64]), op=ALU.mult),
                      reads=[wk_b[wB], sml_b[ss]], writes=[qrb_b[qs]])
                pb = nxt("ptr", 2)
                for j in range(5):
                    fw.op(pe, lambda j=j: T.transpose(ptrv[pb][:, j * 128:(j + 1) * 128], qrb[:, qs, j * 128:(j + 1) * 128], identb[:]),
                          reads=[qrb_b[qs], cd_b, cm_b], writes=[ptr_b[pb]], sig=(j == 4))
                fw.op(act, lambda: A.copy(out=QT[:, :, i * 128:(i + 1) * 128], in_=ptrv[pb][:, 0:512].rearrange("p (j t) -> p j t", j=4)),
                      reads=[ptr_b[pb]], writes=QT_b + [UAB_b])
                fw.op(act, lambda: A.copy(out=KTp[0:64, 0, i * 128:(i + 1) * 128], in_=ptrv[pb][0:64, 512:640]), reads=[ptr_b[pb]], writes=[KT_b])
                fw.op(act, lambda: A.copy(out=KTp[64:128, 1, i * 128:(i + 1) * 128], in_=ptrv[pb][64:128, 512:640]), reads=[ptr_b[pb]], writes=[KT_b])

            a1 = {0: qk_a1(0), 1: qk_a1(1)}
            a2 = {0: qk_a2(0, *a1[0])}
            for i in range(NT):
                if i + 2 < NT:
                    a1[i + 2] = qk_a1(i + 2)
                if i + 1 < NT:
                    a2[i + 1] = qk_a2(i + 1, *a1[i + 1])
                qk_b(i, *a2[i])
            load_wgroup(l, 0, 768, 512, [perm_cast])
            if stop <= 4:
                return
            for j in range(4):
                for tc in range(4):
                    pa = nxt("pa", 4)
                    for kc in range(KC):
                        fw.op(pe, lambda kc=kc, pa=pa, j=j, tc=tc: T.matmul(
                            pacc[:, pa, :], lhsT=wbf[:, 0, kc, j * 128:(j + 1) * 128], rhs=hT[:, kc, tc * 512:(tc + 1) * 512],
                            start=(kc == 0), stop=(kc == KC - 1)),
                            reads=hT_b[tc * 4:(tc + 1) * 4] + [wbf_b[0]], writes=[pacc_b[pa]], sig=(kc == KC - 1))
                    fw.op(act, lambda pa=pa, j=j, tc=tc: A.activation(out=yT[:, j, tc * 512:(tc + 1) * 512], in_=pacc[:, pa, :], func=ACTF.Silu),
                          reads=[pacc_b[pa]], writes=[yT_b[j][tc]])

            if stop <= 5:
                return
            conv_cast = lambda m: (lambda o, i: (o[:, 0:512].rearrange("p (g c) -> p g c", g=4),
                                                 i[:, 0:1024].rearrange("p (g m c) -> p g m c", g=4, m=2)[:, :, m, :]))
            load_wgroup(l, 1, 1280, 1024, [conv_cast(0)])
            load_wgroup(l, 0, 1280, 1024, [conv_cast(1)])

            if stop <= 6:
                return
            SBASE = [0, 2, 6]
            asteps = [(j, qc, kt) for j in range(4) for qc in range(4) for kt in range(NT)]

            def qk(si):
                j, qc, kt = asteps[si]
                sb_ = si % 3
                for kv in range(2):
                    fw.op(pe, lambda kv=kv: T.matmul(
                        pacc[:, SBASE[sb_] + kv, :], lhsT=KTp[:, kv, kt * 128:(kt + 1) * 128], rhs=QT[:, j, qc * 512:(qc + 1) * 512],
                        start=True, stop=True), reads=[KT_b, QT_b[j]], writes=[pacc_b[SBASE[sb_] + kv]])

            def ex(si, ps):
                sb_ = si % 3
                fw.op(act, lambda: A.activation(out=PT[:, ps, :, :], in_=pacc[:, SBASE[sb_]:SBASE[sb_] + 2, :], func=ACTF.Exp, scale=0.125),
                      reads=[pacc_b[SBASE[sb_]], pacc_b[SBASE[sb_] + 1]], writes=[PT_b[ps], fuT_b])

            def pv(si, ps):
                j, qc, kt = asteps[si]
                for kv in range(2):
                    fw.op(pe, lambda kv=kv: T.matmul(
                        pacc[:, 4 + kv, :], lhsT=Vaug[:, kt, kv, :], rhs=PT[:, ps, kv, :], start=(kt == 0), stop=(kt == NT - 1)),
                        reads=[Va_b, PT_b[ps]], writes=[pacc_b[4 + kv]], sig=(kt == NT - 1))

            def att_epilogue(j, qc):
                w0 = nxt("wk", NWK); w1 = nxt("wk", NWK); w2 = nxt("wk", NWK)
                cols = slice(qc * 512, (qc + 1) * 512)
                fw.op(dve, lambda: V.tensor_copy(out=wkf[:, w0, 0:512], in_=pacc[:, 4, :]), reads=[pacc_b[4]], writes=[wk_b[w0]])
                fw.op(dve, lambda: V.tensor_copy(out=wkf[:, w1, 0:512], in_=pacc[:, 5, :]), reads=[pacc_b[5]], writes=[wk_b[w1]])
                fw.op(dve, lambda: V.reciprocal(out=wkf[0:64, w2, 0:512], in_=wkf[64:128, w0, 0:512]), reads=[wk_b[w0]], writes=[wk_b[w2]])
                fw.op(dve, lambda: V.tensor_tensor(out=wkf[0:64, w0, 0:512], in0=wkf[0:64, w0, 0:512], in1=wkf[0:64, w2, 0:512], op=ALU.mult),
                      reads=[wk_b[w0], wk_b[w2]], writes=[wk_b[w0]])
                fw.op(pool, lambda: P.tensor_tensor(out=yT[0:64, j, cols], in0=yT[0:64, j, cols], in1=wkf[0:64, w0, 0:512], op=ALU.mult),
                      reads=[wk_b[w0], yT_b[j][qc]], writes=[yT_b[j][qc]])
                fw.op(dve, lambda: V.reciprocal(out=wkf[64:128, w2, 0:512], in_=wkf[0:64, w1, 0:512]), reads=[wk_b[w1]], writes=[wk_b[w2]])
                fw.op(dve, lambda: V.tensor_tensor(out=wkf[64:128, w1, 0:512], in0=wkf[64:128, w1, 0:512], in1=wkf[64:128, w2, 0:512], op=ALU.mult),
                      reads=[wk_b[w1], wk_b[w2]], writes=[wk_b[w1]])
                fw.op(pool, lambda: P.tensor_tensor(out=yT[64:128, j, cols], in0=yT[64:128, j, cols], in1=wkf[64:128, w1, 0:512], op=ALU.mult),
                      reads=[wk_b[w1], yT_b[j][qc]], writes=[yT_b[j][qc]])

            NS_ = len(asteps)
            qk(0)
            qk(1)
            for si in range(NS_):
                ps = nxt("pt", NPT)
                ex(si, ps)
                if si + 2 < NS_:
                    qk(si + 2)
                pv(si, ps)
                if asteps[si][2] == NT - 1:
                    att_epilogue(asteps[si][0], asteps[si][1])

            if stop <= 7:
                return
            fw.op(pool, lambda: P.memset(zt[:, 0:1], 0.0), writes=[z_b] + xin_b)
            fw.op(pool, lambda: P.memset(zt[:, S + 1:S + 2], 0.0), writes=[z_b] + xin_b)
            for m in range(2):
                slot = 1 - m
                for tc in range(4):
                    pas = [nxt("pa", 4) for _ in range(4)]
                    for g in range(4):
                        for kc in range(KC):
                            fw.op(pe, lambda kc=kc, g=g, tc=tc, slot=slot, pas=pas: T.matmul(
                                pacc[:, pas[g], :], lhsT=wbf[:, slot, kc, g * 128:(g + 1) * 128], rhs=hT[:, kc, tc * 512:(tc + 1) * 512],
                                start=(kc == 0), stop=(kc == KC - 1)),
                                reads=hT_b[tc * 4:(tc + 1) * 4] + [wbf_b[slot]], writes=[pacc_b[pas[g]]], sig=(kc == KC - 1))
                    w0 = nxt("wk", NWK); w1 = nxt("wk", NWK)
                    fw.op(act, lambda w0=w0, pas=pas: A.copy(out=wkf[:, w0, 0:512], in_=pacc[:, pas[2], :]), reads=[pacc_b[pas[2]]], writes=[wk_b[w0]])
                    fw.op(dve, lambda w0=w0, pas=pas, tc=tc: V.tensor_tensor(out=zt[:, 1 + tc * 512:1 + (tc + 1) * 512], in0=pacc[:, pas[0], :], in1=wkf[:, w0, 0:512], op=ALU.mult),
                          reads=[pacc_b[pas[0]], wk_b[w0]], writes=[z_b])
                    fw.op(act, lambda w1=w1, pas=pas: A.activation(out=wkf[:, w1, 0:512], in_=pacc[:, pas[3], :], func=ACTF.Silu), reads=[pacc_b[pas[3]]], writes=[wk_b[w1]])
                    fw.op(dve, lambda w1=w1, pas=pas, tc=tc, m=m: V.tensor_tensor(out=yT[:, 4 + m, tc * 512:(tc + 1) * 512], in0=pacc[:, pas[1], :], in1=wkf[:, w1, 0:512], op=ALU.mult),
                          reads=[pacc_b[pas[1]], wk_b[w1]], writes=[yT_b[4 + m][tc]])
                for tc in range(4):
                    w0 = nxt("wk", NWK)
                    c0 = 1 + tc * 512
                    fw.op(dve, lambda w0=w0, c0=c0, m=m: V.tensor_scalar(out=wkf[:, w0, 0:512], in0=zt[:, c0:c0 + 512], scalar1=cwT[:, l, m, 1:2], scalar2=None, op0=ALU.mult),
                          reads=[z_b, cd_b, cm_b], writes=[wk_b[w0]])
                    fw.op(dve, lambda w0=w0, c0=c0, m=m: V.scalar_tensor_tensor(out=wkf[:, w0, 0:512], in0=zt[:, c0 - 1:c0 + 511], scalar=cwT[:, l, m, 0:1], in1=wkf[:, w0, 0:512],
                                                                           op0=ALU.mult, op1=ALU.add), reads=[z_b, cd_b, cm_b, wk_b[w0]], writes=[wk_b[w0]])
                    fw.op(dve, lambda w0=w0, c0=c0, m=m: V.scalar_tensor_tensor(out=wkf[:, w0, 0:512], in0=zt[:, c0 + 1:c0 + 513], scalar=cwT[:, l, m, 2:3], in1=wkf[:, w0, 0:512],
                                                                           op0=ALU.mult, op1=ALU.add), reads=[z_b, cd_b, cm_b, wk_b[w0]], writes=[wk_b[w0]])
                    fw.op(pool, lambda w0=w0, tc=tc, m=m: P.tensor_tensor(out=yT[:, 4 + m, tc * 512:(tc + 1) * 512], in0=yT[:, 4 + m, tc * 512:(tc + 1) * 512], in1=wkf[:, w0, 0:512], op=ALU.mult),
                          reads=[wk_b[w0], yT_b[4 + m][tc]], writes=[yT_b[4 + m][tc]])
                if m == 0:
                    load_wgroup(l, 1, 2304, 512, [lambda o, i: (o[:, 0:512], i[:, 0:512])])

            if stop <= 8:
                return
            for m2 in range(2):
                for tc in range(4):
                    pa = nxt("pa", 4)
                    for kc in range(KC):
                        fw.op(pe, lambda kc=kc, pa=pa, m2=m2, tc=tc: T.matmul(
                            pacc[:, pa, :], lhsT=wbf[:, 1, kc, m2 * 128:(m2 + 1) * 128], rhs=hT[:, kc, tc * 512:(tc + 1) * 512],
                            start=(kc == 0), stop=(kc == KC - 1)),
                            reads=hT_b[tc * 4:(tc + 1) * 4] + [wbf_b[1]], writes=[pacc_b[pa]], sig=(kc == KC - 1))
                    fw.op(act, lambda pa=pa, m2=m2, tc=tc: A.copy(out=fuT[:, m2, tc * 512:(tc + 1) * 512], in_=pacc[:, pa, :]),
                          reads=[pacc_b[pa]], writes=[fuT_b] + PT_b)
                    pa = nxt("pa", 4)
                    for kc in range(KC):
                        fw.op(pe, lambda kc=kc, pa=pa, m2=m2, tc=tc: T.matmul(
                            pacc[:, pa, :], lhsT=wbf[:, 1, kc, 256 + m2 * 128:256 + (m2 + 1) * 128], rhs=hT[:, kc, tc * 512:(tc + 1) * 512],
                            start=(kc == 0), stop=(kc == KC - 1)),
                            reads=hT_b[tc * 4:(tc + 1) * 4] + [wbf_b[1]], writes=[pacc_b[pa]], sig=(kc == KC - 1))
                    fw.op(act, lambda pa=pa, m2=m2, tc=tc: A.activation(out=yT[:, 6 + m2, tc * 512:(tc + 1) * 512], in_=pacc[:, pa, :], func=ACTF.Silu),
                          reads=[pacc_b[pa]], writes=[yT_b[6 + m2][tc]])
            wall = wbf[:, :, :, :].rearrange("p s k c -> p (s k c)").rearrange("p (k c) -> p k c", k=KC)
            for kc in range(KC):
                sl = nxt("stg", NSTG)
                if kc < 4:
                    fw.dma(s_stg[sl], stage[0:64, sl, :], wout_d[l, kc * 64:(kc + 1) * 64, :], writes=[stg_b[sl]])
                    fw.dma(s_stg[sl], stage[64:128, sl, :], wout_d[l, (4 + kc) * 64:(5 + kc) * 64, :], writes=[stg_b[sl]])
                else:
                    fw.dma(s_stg[sl], stage[:, sl, :], wout_d[l, kc * 128:(kc + 1) * 128, :], writes=[stg_b[sl]])
                fw.op(pool, lambda kc=kc, sl=sl: P.tensor_tensor(out=wall[:, kc, :], in0=stage[:, sl, :], in1=gateb[:], op=ALU.mult),
                      reads=[stg_b[sl], gate_b], writes=wbf_b)
            for i in range(NT):
                pa = nxt("pa", 4)
                for m2 in range(2):
                    fw.op(pe, lambda m2=m2, pa=pa, i=i: T.matmul(
                        pacc[:, pa, m2 * 256:(m2 + 1) * 256], lhsT=fuT[:, m2, i * 128:(i + 1) * 128],
                        rhs=ABbd[:, m2, :, :].rearrange("p a n -> p (a n)"), start=True, stop=True),
                        reads=[fuT_b, AB_b], writes=[pacc_b[pa]], sig=(m2 == 1))
                fw.op(act, lambda pa=pa, i=i: A.copy(out=UAB[:, i, :, :, :].rearrange("p m c n -> p (m c n)"), in_=pacc[:, pa, :]),
                      reads=[pacc_b[pa]], writes=[UAB_b] + QT_b)
            for half in range(2):
                pas = [[nxt("pa", 4) for _ in range(2)] for _ in range(2)]
                for a in range(NT):
                    ts = nxt("tab", NTAB)
                    fw.dma(s_tab[ts], tabv[:, ts, :, :], dft_d[a * 128:(a + 1) * 128, :, half * 1024:(half + 1) * 1024], writes=[tab_b[ts]] + hT_b)
                    for m2 in range(2):
                        for kq in range(2):
                            for ab in range(2):
                                fw.op(pe, lambda m2=m2, kq=kq, ab=ab, a=a, ts=ts, pas=pas: T.matmul(
                                    pacc[:, pas[m2][kq], :], lhsT=UAB[:, a, m2, ab, :], rhs=tabv[:, ts, ab, kq * 512:(kq + 1) * 512],
                                    start=(a == 0 and ab == 0), stop=(a == NT - 1 and ab == 1)),
                                    reads=[UAB_b, tab_b[ts]], writes=[pacc_b[pas[m2][kq]]], sig=((a == NT - 1 and ab == 1) or (m2 == 1 and kq == 1 and ab == 1)))
                for m2 in range(2):
                    for kq in range(2):
                        tcc = half * 2 + kq
                        fw.op(dve, lambda m2=m2, kq=kq, tcc=tcc, pas=pas: V.tensor_tensor(
                            out=yT[:, 6 + m2, tcc * 512:(tcc + 1) * 512], in0=pacc[:, pas[m2][kq], :], in1=yT[:, 6 + m2, tcc * 512:(tcc + 1) * 512], op=ALU.mult),
                            reads=[pacc_b[pas[m2][kq]], yT_b[6 + m2][tcc]], writes=[yT_b[6 + m2][tcc]])

            if stop <= 9:
                return
            fw.dma(s_lng, lngb[:], lng_d[l].partition_broadcast(128), writes=[lng_b])
            fw.dma(s_lnb, lnbb[:], lnb_d[l].partition_broadcast(128), writes=[lnb_b])
            def fin_a(i):
                bp = [4, 0, 2][i % 3]
                xr = nxt("xres", NXR)
                fw.dma(s_xres[xr], xres[:, xr, :], x_src[b, i * 128:(i + 1) * 128, :], reads=[ydram_b[b][i]], writes=[xres_b[xr]])
                for n in range(2):
                    for kc in range(KC):
                        fw.op(pe, lambda kc=kc, n=n: T.matmul(
                            pacc[:, bp + n, :], lhsT=yT[:, kc, i * 128:(i + 1) * 128], rhs=wall[:, kc, n * 512:(n + 1) * 512],
                            start=(kc == 0), stop=(kc == KC - 1)),
                            reads=[yT_b[kc][i // 4]] + wbf_b, writes=[pacc_b[bp + n]], sig=(kc == KC - 1))
                ss = nxt("st", 4)
                for n in range(2):
                    fw.op(dve, lambda n=n: V.scalar_tensor_tensor(
                        out=xres[:, xr, n * 512:(n + 1) * 512], in0=xres[:, xr, n * 512:(n + 1) * 512], scalar=float(ALPHA),
                        in1=pacc[:, bp + n, :], op0=ALU.mult, op1=ALU.add),
                        reads=[pacc_b[bp + n], xres_b[xr]], writes=[xres_b[xr]])
                for n in range(2):
                    fw.op(dve, lambda n=n: V.bn_stats(out=stt[:, ss, n, :], in_=xres[:, xr, n * 512:(n + 1) * 512]),
                          reads=[xres_b[xr]], writes=[stt_b[ss]])
                fw.op(dve, lambda: V.bn_aggr(out=mv[:, ss, :], in_=stt[:, ss, :, :].rearrange("p a b -> p (a b)")), reads=[stt_b[ss]], writes=[mv_b[ss]])
                fw.op(dve, lambda: V.tensor_scalar(out=sml[:, ss, 0:1], in0=mv[:, ss, 1:2], scalar1=1.0, scalar2=LN_EPS,
                                                   op0=ALU.mult, op1=ALU.add), reads=[mv_b[ss]], writes=[sml_b[ss]])
                fw.op(pool, lambda: P.tensor_tensor(out=sml[:, ss, 0:1], in0=sml[:, ss, 0:1], in1=mhalf[:, 0:1], op=ALU.pow),
                      reads=[sml_b[ss], cd_b, cm_b], writes=[sml_b[ss]])
                return xr, ss

            def fin_b(i, xr, ss):
                fw.op(dve, lambda: V.scalar_tensor_tensor(out=xres[:, xr, :], in0=xres[:, xr, :], scalar=mv[:, ss, 0:1], in1=lngb[:],
                                                          op0=ALU.subtract, op1=ALU.mult),
                      reads=[xres_b[xr], mv_b[ss], lng_b], writes=[xres_b[xr]])
                fw.op(dve, lambda: V.scalar_tensor_tensor(out=xres[:, xr, :], in0=xres[:, xr, :], scalar=sml[:, ss, 0:1], in1=lnbb[:],
                                                          op0=ALU.mult, op1=ALU.add),
                      reads=[xres_b[xr], sml_b[ss], lnb_b], writes=[xres_b[xr]])
                fw.dma(s_xres[xr], y_d[b, i * 128:(i + 1) * 128, :], xres[:, xr, :], reads=[xres_b[xr]], writes=[ydram_b[b][i]], q=fw.act)

            pend = fin_a(0)
            for i in range(NT):
                nxt_pend = fin_a(i + 1) if i + 1 < NT else None
                fin_b(i, *pend)
                pend = nxt_pend

        for l in range(n_layers):
            for b in range(n_seq):
                if b == 0:
                    emit_mod(l)
                emit_layer(b, l, x_d if l == 0 else y_d, l == n_layers - 1)
        fw.wait_all(fw.sp, s_xres)
        build_program.stats = (fw.ninst, fw.nwait, {e.name: e.sem.count for e in (pe, act, dve, pool)})
    return nc


def _constants():
    bf = ml_dtypes.bfloat16
    t = np.arange(S)
    rows = (t // 64).astype(np.float32)
    cols = (t % 64).astype(np.float32)
    inv = (1.0 / (10000.0 ** (np.arange(0, 32, 2, dtype=np.float32) / np.float32(32)))).astype(np.float32)
    ra = (rows[:, None] * inv[None, :]).astype(np.float32)
    ca = (cols[:, None] * inv[None, :]).astype(np.float32)
    ang = np.concatenate([ra, ca], axis=1)
    cos = np.cos(ang.astype(np.float64)).astype(np.float32)
    sin = np.sin(ang.astype(np.float64)).astype(np.float32)
    rope_cos = np.ascontiguousarray(cos.reshape(NT, 128, 32).transpose(1, 0, 2))
    rope_sin = np.ascontiguousarray(sin.reshape(NT, 128, 32).transpose(1, 0, 2))
    ks = (np.outer(t, t) % S).astype(np.float64) * (2.0 * np.pi / S)
    dft_cs = np.ascontiguousarray(np.stack([np.cos(ks), -np.sin(ks)], axis=1).astype(np.float32).astype(bf))
    m = np.arange(64)
    a64 = (np.outer(m, m) % 64).astype(np.float64) * (2.0 * np.pi / 64)
    sc = 1.0 / np.sqrt(float(S) * 64.0)
    c64 = np.cos(a64) * sc
    s64 = np.sin(a64) * sc
    z = np.zeros((64, 64))
    c64bd = np.block([[c64, z], [z, c64]]).astype(np.float32)
    s64bd = np.block([[s64, z], [z, s64]]).astype(np.float32)
    def hilo(a):
        hi = a.astype(bf)
        lo = (a - hi.astype(np.float32)).astype(bf)
        return np.stack([hi, lo], axis=1)
    cs64bd = np.ascontiguousarray(np.stack([hilo(c64bd), hilo(s64bd)], axis=1))
    return {
        "ident_bf": np.eye(128, dtype=np.float32).astype(bf),
        "ident_f": np.eye(128, dtype=np.float32),
        "rope_cos": rope_cos, "rope_sin": rope_sin,
        "dft_cs": dft_cs,
        "cs64bd": cs64bd,
    }


def make_in_maps(x, c, w_ada, b_ada, w_in, q_gain, k_gain, conv_w, f_mix, w_out, ln_g, ln_b, ncores=NCORES):
    f = lambda a: np.ascontiguousarray(np.asarray(a, dtype=np.float32))
    consts = _constants()
    shared = {
        "w_ada": f(w_ada), "w_in": f(w_in), "q_gain": f(q_gain), "k_gain": f(k_gain),
        "f_mix": f(f_mix), "w_out": f(w_out), "ln_g": f(ln_g), "ln_b": f(ln_b),
        "b_adaT": f(np.asarray(b_ada).reshape(DEPTH, 24, 128).transpose(2, 0, 1)),
        "conv_wT": f(np.asarray(conv_w).reshape(DEPTH, 3, 2, 128).transpose(3, 0, 2, 1)),
    }
    shared.update(consts)
    x = np.asarray(x, dtype=np.float32)
    c = np.asarray(c, dtype=np.float32)
    maps = []
    for ci in range(ncores):
        m = dict(shared)
        m["x"] = np.ascontiguousarray(x[ci * BPC:(ci + 1) * BPC])
        m["cT"] = f(c[ci * BPC:(ci + 1) * BPC].reshape(BPC, KC, 128).transpose(2, 1, 0))
        maps.append(m)
    return maps


def kernel(x, c, w_ada, b_ada, w_in, q_gain, k_gain, conv_w, f_mix, w_out, ln_g, ln_b):
    maps = make_in_maps(x, c, w_ada, b_ada, w_in, q_gain, k_gain, conv_w, f_mix, w_out, ln_g, ln_b)
    nc = build_program()
    res = run_bass_kernel_spmd(nc, maps, core_ids=list(range(NCORES)))
    return np.concatenate([np.asarray(r["y"], dtype=np.float32) for r in res.results], axis=0)
```
